# Optimizing a Trainium2 kernel written in Bass

```python
import math
import jax, jax.numpy as jnp
from jax import lax
import numpy as np

D_MODEL = 1024
BATCH = 8
SEQ = 4096
DEPTH = 1

CHUNK = 64
Q_BLOCK = 128
CONV_WIDTH = D_MODEL // 2
CONV_K = 3
DA_HEADS = 4
DA_HEAD_DIM = 64
DA_V_DIM = 2 * DA_HEAD_DIM
DA_QK_WIDTH = 2 * DA_HEADS * DA_HEAD_DIM
DA_V_WIDTH = DA_HEADS * DA_V_DIM
D_FF = 2816
ROPE_THETA = 10000.0
NORM_EPS = 1e-6
LAMBDA_STD = 0.1
IN_SPLITS = [CONV_WIDTH, CONV_WIDTH, CONV_WIDTH, DA_QK_WIDTH, DA_QK_WIDTH, DA_V_WIDTH, D_MODEL, D_MODEL]
IN_WIDTH = sum(IN_SPLITS)

kernel_name = "hybrid_conv_diffattn_macaron_block"


def rms_norm(x, g):
    xf = x.astype(jnp.float32)
    y = xf * lax.rsqrt(jnp.mean(xf * xf, axis=-1, keepdims=True) + NORM_EPS)
    return (y * g.astype(jnp.float32)).astype(x.dtype)


def swiglu(x, w_gate, w_up, w_down):
    return (jax.nn.silu(x @ w_gate) * (x @ w_up)) @ w_down


def rope_tables(seq, dim):
    inv_freq = 1.0 / (ROPE_THETA ** (jnp.arange(0, dim, 2, dtype=jnp.float32) / dim))
    ang = jnp.arange(seq, dtype=jnp.float32)[:, None] * inv_freq[None, :]
    return jnp.cos(ang)[:, None, :], jnp.sin(ang)[:, None, :]


def apply_rope(x, cos, sin):
    xf = x.astype(jnp.float32)
    x1, x2 = jnp.split(xf, 2, axis=-1)
    out = jnp.concatenate([x1 * cos - x2 * sin, x2 * cos + x1 * sin], axis=-1)
    return out.astype(x.dtype)


def short_gated_conv(xc, b_gate, c_gate, conv_w):
    z = c_gate * xc
    rhs = conv_w[:, None, :].astype(z.dtype)
    zc = lax.conv_general_dilated(z, rhs, window_strides=(1,), padding=[(CONV_K - 1, 0)],
                                  dimension_numbers=('NWC', 'WIO', 'NWC'),
                                  feature_group_count=z.shape[-1])
    return b_gate * zc


def diff_attention(q, k, v, lam):
    bsz, seq = q.shape[0], q.shape[1]
    nb = seq // Q_BLOCK
    scale = 1.0 / math.sqrt(DA_HEAD_DIM)
    k_chunk = jnp.arange(seq) // CHUNK
    qb = q.reshape(bsz, nb, Q_BLOCK, 2 * DA_HEADS, DA_HEAD_DIM).transpose(1, 0, 2, 3, 4)

    def block(args):
        q_blk, blk_idx = args
        s = jnp.einsum('bqhd,bkhd->bhqk', q_blk, k).astype(jnp.float32) * scale
        q_chunk = (blk_idx * Q_BLOCK + jnp.arange(Q_BLOCK)) // CHUNK
        mask = k_chunk[None, :] <= q_chunk[:, None]
        s = jnp.where(mask[None, None], s, -jnp.inf)
        p = jax.nn.softmax(s, axis=-1).reshape(bsz, DA_HEADS, 2, Q_BLOCK, seq)
        a = p[:, :, 0] - lam * p[:, :, 1]
        return jnp.einsum('bhqk,bkhe->bqhe', a.astype(v.dtype), v)

    o = lax.map(block, (qb, jnp.arange(nb)))
    return o.transpose(1, 0, 2, 3, 4).reshape(bsz, seq, DA_HEADS, DA_V_DIM)


def setup_inputs(seed: int = 0) -> dict:
    key = jax.random.key(seed)
    ks = jax.random.split(key, 24)
    f32 = jnp.float32

    def w(k, shape, fan_in):
        return jax.random.normal(k, shape, f32) * (fan_in ** -0.5)

    def gain(k, shape):
        return 1.0 + 0.02 * jax.random.normal(k, shape, f32)

    L = DEPTH
    return {
        "x": jax.random.normal(ks[0], (BATCH, SEQ, D_MODEL), f32),
        "norm_ffn1": gain(ks[1], (L, D_MODEL)),
        "ffn1_gate": w(ks[2], (L, D_MODEL, D_FF), D_MODEL),
        "ffn1_up": w(ks[3], (L, D_MODEL, D_FF), D_MODEL),
        "ffn1_down": w(ks[4], (L, D_FF, D_MODEL), D_FF),
        "norm_mix": gain(ks[5], (L, D_MODEL)),
        "w_in": w(ks[6], (L, D_MODEL, IN_WIDTH), D_MODEL),
        "b_gate": 0.01 * jax.random.normal(ks[7], (L, 2 * D_MODEL), f32),
        "conv_w": w(ks[8], (L, CONV_K, CONV_WIDTH), CONV_K),
        "w_conv_out": w(ks[9], (L, CONV_WIDTH, D_MODEL), CONV_WIDTH),
        "lambda_q1": LAMBDA_STD * jax.random.normal(ks[10], (L, DA_HEAD_DIM), f32),
        "lambda_k1": LAMBDA_STD * jax.random.normal(ks[11], (L, DA_HEAD_DIM), f32),
        "lambda_q2": LAMBDA_STD * jax.random.normal(ks[12], (L, DA_HEAD_DIM), f32),
        "lambda_k2": LAMBDA_STD * jax.random.normal(ks[13], (L, DA_HEAD_DIM), f32),
        "subln_g": gain(ks[14], (L, DA_V_DIM)),
        "w_attn_out": w(ks[15], (L, DA_V_WIDTH, D_MODEL), DA_V_WIDTH),
        "w_o": w(ks[16], (L, D_MODEL, D_MODEL), D_MODEL),
        "norm_ffn2": gain(ks[17], (L, D_MODEL)),
        "ffn2_gate": w(ks[18], (L, D_MODEL, D_FF), D_MODEL),
        "ffn2_up": w(ks[19], (L, D_MODEL, D_FF), D_MODEL),
        "ffn2_down": w(ks[20], (L, D_FF, D_MODEL), D_FF),
        "norm_final": gain(ks[21], (D_MODEL,)),
    }


def reference(x, norm_ffn1, ffn1_gate, ffn1_up, ffn1_down, norm_mix, w_in, b_gate, conv_w,
              w_conv_out, lambda_q1, lambda_k1, lambda_q2, lambda_k2, subln_g, w_attn_out, w_o,
              norm_ffn2, ffn2_gate, ffn2_up, ffn2_down, norm_final):
    bsz, seq, _ = x.shape
    cos, sin = rope_tables(seq, DA_HEAD_DIM)
    split_pts = list(np.cumsum(IN_SPLITS)[:-1])
    h = x
    for l in range(DEPTH):
        h = h + 0.5 * swiglu(rms_norm(h, norm_ffn1[l]), ffn1_gate[l], ffn1_up[l], ffn1_down[l])

        u = rms_norm(h, norm_mix[l])
        proj = u @ w_in[l]
        xc, bg, cg, q, k, v, g_conv, g_attn = jnp.split(proj, split_pts, axis=-1)

        y_conv = short_gated_conv(xc, bg, cg, conv_w[l]) @ w_conv_out[l]

        q = apply_rope(q.reshape(bsz, seq, 2 * DA_HEADS, DA_HEAD_DIM), cos, sin)
        k = apply_rope(k.reshape(bsz, seq, 2 * DA_HEADS, DA_HEAD_DIM), cos, sin)
        v = v.reshape(bsz, seq, DA_HEADS, DA_V_DIM)
        lambda_init = 0.8 - 0.6 * math.exp(-0.3 * l)
        lam = (jnp.exp(jnp.sum(lambda_q1[l].astype(jnp.float32) * lambda_k1[l].astype(jnp.float32)))
               - jnp.exp(jnp.sum(lambda_q2[l].astype(jnp.float32) * lambda_k2[l].astype(jnp.float32)))
               + lambda_init)
        o = diff_attention(q, k, v, lam)
        o = rms_norm(o, subln_g[l]) * (1.0 - lambda_init)
        y_attn = o.reshape(bsz, seq, DA_V_WIDTH) @ w_attn_out[l]

        g_conv = jax.nn.sigmoid(g_conv + b_gate[l, :D_MODEL])
        g_attn = jax.nn.sigmoid(g_attn + b_gate[l, D_MODEL:])
        h = h + (g_conv * y_conv + g_attn * y_attn) @ w_o[l]

        h = h + 0.5 * swiglu(rms_norm(h, norm_ffn2[l]), ffn2_gate[l], ffn2_up[l], ffn2_down[l])
    return rms_norm(h, norm_final)
```

```python
from contextlib import ExitStack

import numpy as np
import concourse.bass as bass
import concourse.mybir as mybir
from concourse.bass_utils import run_bass_kernel_spmd

F32 = mybir.dt.float32
BF16 = mybir.dt.bfloat16
I32 = mybir.dt.int32
AF = mybir.ActivationFunctionType
ALU = mybir.AluOpType

ENGS = ("tensor", "vector", "scalar", "gpsimd", "sync")

D = 1024
SEQ = 4096
FF = 2816
NFF = 22
TT = 512
NT_FULL = SEQ // TT
EPS = 1e-6
NSLOT = 5
NCONV = 3
LAMBDA_INIT = 0.8 - 0.6 * 1.0

C_GMIX, C_G2, C_BG, C_CW, C_LAM, C_SUB, C_ID, C_G1B, C_GFB, C_TOT = 0, 8, 16, 32, 48, 304, 432, 560, 1584, 2608


class DSem:
    def __init__(self, sem, key):
        self.sem = sem
        self.key = key
        self.cnt = 0


class Tl:
    def __init__(self, ap, name=""):
        self.ap = ap
        self.name = name
        self.w = {}
        self.r = {}
        self.dsem = None
        self.excl = False


class Sched:
    def __init__(self, nc, stack):
        self.nc = nc
        self.stack = stack
        self.ops = {e: [] for e in ENGS}
        self.sem = {e: stack.enter_context(nc.semaphore("s_" + e)) for e in ENGS}
        self.cnt = {e: 0 for e in ENGS}
        self.waited = {e: {} for e in ENGS}
        self.nds = 0

    def new_dsem(self):
        self.nds += 1
        s = self.stack.enter_context(self.nc.semaphore(f"d{self.nds}"))
        return DSem(s, f"d{self.nds}")

    def _need(self, eng, key, sem, val, acc):
        if self.waited[eng].get(key, 0) >= val:
            return
        if key == eng and val > self.cnt[eng]:
            return
        self.waited[eng][key] = val
        acc[key] = (sem, val)

    def _wait(self, eng, key, sem, val):
        acc = {}
        self._need(eng, key, sem, val, acc)
        for (sm, v) in acc.values():
            self.ops[eng].append(lambda e, sm=sm, v=v: e.wait_ge(sm, v))

    def _deps(self, eng, reads, writes):
        acc = {}
        for t in reads:
            for k, (s, v) in t.w.items():
                self._need(eng, k, s, v, acc)
            if t.excl:
                for k, (s, v) in t.r.items():
                    if k != eng:
                        self._need(eng, k, s, v, acc)
        for t in writes:
            for k, (s, v) in t.w.items():
                self._need(eng, k, s, v, acc)
            for k, (s, v) in t.r.items():
                self._need(eng, k, s, v, acc)
        return list(acc.values())

    def _emit_waits(self, eng, waits, attach):
        if attach and waits:
            for (sm, v) in waits[1:]:
                self.ops[eng].append(lambda e, sm=sm, v=v: e.wait_ge(sm, v))
            return waits[0]
        for (sm, v) in waits:
            self.ops[eng].append(lambda e, sm=sm, v=v: e.wait_ge(sm, v))
        return None

    def _mark(self, key, sem, val, reads, writes):
        for t in writes:
            t.w = {key: (sem, val)}
            t.r = {}
        for t in reads:
            old = t.r.get(key)
            if old is None or old[1] < val:
                t.r[key] = (sem, val)

    def op(self, eng, fn, reads=(), writes=(), inc=True):
        waits = self._deps(eng, reads, writes)
        last = self._emit_waits(eng, waits, True)
        sem = self.sem[eng]
        if inc:
            self.cnt[eng] += 1
            val = self.cnt[eng]
        else:
            val = self.cnt[eng] + 1

        def run(e, fn=fn, sem=sem, last=last, inc=inc):
            ins = fn(e)
            if last is not None:
                ins = ins._wait_ge(last[0], last[1])
            if inc:
                ins.then_inc(sem, 1)
        self.ops[eng].append(run)
        self._mark(eng, sem, val, reads, writes)

    def dma(self, q, out_ap, in_ap, reads=(), writes=(), dsem=None):
        waits = self._deps(q, reads, writes)
        last = self._emit_waits(q, waits, q == "sync")
        dsem.cnt += 16
        val = dsem.cnt
        s = dsem.sem

        def run(e, o=out_ap, i=in_ap, s=s, last=last):
            ins = e.dma_start(out=o, in_=i)
            if last is not None:
                ins = ins._wait_ge(last[0], last[1])
            ins.then_inc(s, 16)
        self.ops[q].append(run)
        self._mark(dsem.key, s, val, reads, writes)

    def emit(self):
        with self.nc.Block() as block:
            @block.tensor
            def _(e):
                for f in self.ops["tensor"]:
                    f(e)

            @block.vector
            def _(e):
                for f in self.ops["vector"]:
                    f(e)

            @block.scalar
            def _(e):
                for f in self.ops["scalar"]:
                    f(e)

            @block.gpsimd
            def _(e):
                for f in self.ops["gpsimd"]:
                    f(e)

            @block.sync
            def _(e):
                for f in self.ops["sync"]:
                    f(e)


def chunk_widths():
    w = []
    for _f in range(2):
        pass
    ffn = [2048] * NFF + [1408] * 16
    mix = [2048] * 6 + [2048] * 4 + [2048] * 2
    mrg = []
    mrg += [2048, 512, 2048, 512]
    for _dt in range(8):
        mrg += [512]
        if _dt + 2 < 8:
            mrg += [2048, 512]
    wo = [2048] * 4
    w = ffn + mix + mrg + wo + ffn
    return w


WIDTHS = chunk_widths()
NCH = len(WIDTHS)


def build_program(ntiles=NT_FULL, stop_after=99):
    nc = bass.Bass("TRN2", target_bir_lowering=False)
    x_d = nc.dram_tensor("x", [SEQ, D], F32, kind="ExternalInput").ap()
    w_d = nc.dram_tensor("wsrc", [NCH, 128, 2048], F32, kind="ExternalInput").ap()
    c_d = nc.dram_tensor("consts", [128, C_TOT], F32, kind="ExternalInput").ap()
    r_d = nc.dram_tensor("rope", [2, 128, SEQ], F32, kind="ExternalInput").ap()
    o_d = nc.dram_tensor("out", [SEQ, D], F32, kind="ExternalOutput").ap()
    scr_d = nc.dram_tensor("wscratch", [NCH, 128, 2048], BF16).ap()

    with ExitStack() as st:
        S = Sched(nc, st)

        def sb(name, shape, dt):
            return st.enter_context(nc.sbuf_tensor(name, shape, dt)).ap()

        pool_t = sb("pool", [128, 32, TT], BF16)
        pool = [Tl(pool_t[:, i, :], f"pool{i}") for i in range(32)]
        uT = pool[0:8]
        h1T = pool[8:30]
        cT = pool[8:12]
        QT = pool[12:16]
        oT = pool[16:20]
        mT = pool[20:28]
        gP = pool[28:32]
        hT_t = sb("hT", [128, 8, TT], F32)
        hT = [Tl(hT_t[:, i, :], f"hT{i}") for i in range(8)]
        KT_t = sb("KT", [128, 4, SEQ], BF16)
        KT = [[Tl(KT_t[:, h, t * TT:(t + 1) * TT], f"KT{h}_{t}") for t in range(NT_FULL)] for h in range(4)]
        V_t = sb("V", [128, 32, 4, 129], BF16)
        Vt = [Tl(V_t[:, 4 * t:4 * t + 4], f"V{t}") for t in range(NT_FULL)]
        ring = [Tl(sb(f"ring{i}", [128, 2048], BF16), f"ring{i}") for i in range(NSLOT)]
        xin = [Tl(sb(f"xin{i}", [128, D], F32), f"xin{i}") for i in range(2)]
        xn = Tl(sb("xn", [128, D], BF16), "xn")
        junk = Tl(sb("junk", [128, D], BF16), "junk")
        ostage = [Tl(sb(f"ost{i}", [128, D], F32), f"ost{i}") for i in range(2)]
        zbuf = [Tl(sb(f"zbuf{i}", [128, TT + 2], F32), f"zbuf{i}") for i in range(4)]
        tmpf = [Tl(sb(f"tmpf{i}", [128, TT], F32), f"tmpf{i}") for i in range(4)]
        tmpb = [Tl(sb(f"tmpb{i}", [128, TT], BF16), f"tmpb{i}") for i in range(2)]
        PT = [Tl(sb(f"PT{i}", [128, 2, TT], BF16), f"PT{i}") for i in range(3)]
        otok = [Tl(sb(f"otok{i}", [128, TT], BF16), f"otok{i}") for i in range(4)]
        o4 = Tl(sb("o4", [128, 4, 128], F32), "o4")
        ostgA = Tl(sb("ostgA", [128, 387], F32), "ostgA")
        ostgB = Tl(sb("ostgB", [128, 1, 387], F32), "ostgB")
        ostgC = Tl(sb("ostgC", [128, 258], F32), "ostgC")
        junk2 = Tl(sb("junk2", [128, 128], BF16), "junk2")
        ot2 = [Tl(sb(f"ot2_{i}", [128, 128], F32), f"ot2_{i}") for i in range(2)]
        dg4 = Tl(sb("dg4", [128, 4, 128], F32), "dg4")
        rsfin = Tl(sb("rsfin", [128, 8], F32), "rsfin")
        rsin = Tl(sb("rsin", [128, 8], F32), "rsin")
        ropeT = Tl(sb("ropeT", [128, 2, TT], F32), "ropeT")
        cst = Tl(sb("cst", [128, C_TOT], F32), "cst")
        identb = Tl(sb("identb", [128, 128], BF16), "identb")
        permb = Tl(sb("permb", [128, 128], BF16), "permb")
        ones_bf = Tl(sb("ones_bf", [128, 2], BF16), "ones_bf")
        ones_f = Tl(sb("ones_f", [128, 128], F32), "ones_f")
        gsub = Tl(sb("gsub", [128, 128], F32), "gsub")
        hb = Tl(sb("hb", [128, 16], F32), "hb")
        lam_t = Tl(sb("lam_t", [128, 8], F32), "lam_t")
        stat_t = sb("stat", [128, 64, 8], F32)
        stats = [Tl(stat_t[:, i, :], f"stat{i}") for i in range(64)]
        stat_i = [0]
        rq_a = Tl(sb("rq_a", [128, 8], F32), "rq_a")
        rq_i = Tl(sb("rq_i", [128, 8], F32), "rq_i")
        rq_y = Tl(sb("rq_y", [128, 8], F32), "rq_y")

        def new_stat():
            s_ = stats[stat_i[0] % 64]
            stat_i[0] += 1
            return s_

        ps_all = st.enter_context(nc.psum_tensor("ps", [128, 8, 512], F32)).ap()
        bank = [Tl(ps_all[:, i, :], f"bank{i}") for i in range(8)]
        for b_ in bank:
            b_.excl = True

        def bank_bf(i):
            return ps_all[:, i, :].bitcast(BF16)

        for t_ in ring + xin + ostage + [ropeT]:
            t_.dsem = S.new_dsem()
        for t_ in ring:
            t_.ssem = S.new_dsem()
        cs = S.new_dsem()
        scr = [Tl(scr_d[c], f"scr{c}") for c in range(NCH)]

        identf_ap = cst.ap[:, C_ID:C_ID + 128]

        def mm(out_ap, lhsT, rhs, start, stop, reads, writes, inc, **kw):
            S.op("tensor", lambda e: e.matmul(out_ap, lhsT, rhs, start=start, stop=stop, **kw),
                 reads=reads, writes=writes, inc=inc)

        def mm_group(out_tl, out_ap, pairs, common, per):
            n = len(pairs)
            for i, (l, r) in enumerate(pairs):
                mm(out_ap, l, r, i == 0, i == n - 1, list(common) + [per[i]], [out_tl], i == n - 1)

        def act(out_ap, in_ap, func, reads, writes, **kw):
            S.op("scalar", lambda e: e.activation(out=out_ap, in_=in_ap, func=func, **kw), reads=reads, writes=writes)

        def tt(eng, out_ap, in0, in1, op, reads, writes):
            S.op(eng, lambda e: e.tensor_tensor(out=out_ap, in0=in0, in1=in1, op=op), reads=reads, writes=writes)

        def ts(eng, out_ap, in0, s1, s2, op0, op1, reads, writes):
            if s2 is None:
                S.op(eng, lambda e: e.tensor_scalar(out=out_ap, in0=in0, scalar1=s1, scalar2=None, op0=op0),
                     reads=reads, writes=writes)
            else:
                S.op(eng, lambda e: e.tensor_scalar(out=out_ap, in0=in0, scalar1=s1, scalar2=s2, op0=op0, op1=op1),
                     reads=reads, writes=writes)

        def stt(out_ap, in0, scalar, in1, op0, op1, reads, writes):
            S.op("vector", lambda e: e.scalar_tensor_tensor(out=out_ap, in0=in0, scalar=scalar, in1=in1, op0=op0, op1=op1),
                 reads=reads, writes=writes)

        def rstd_from_ss(ss_ap, ss_tl, n, width, out=None):
            ts("vector", rq_a.ap[:, 0:n], ss_ap, 1.0 / width, EPS, ALU.mult, ALU.add, [ss_tl], [rq_a])
            ha = new_stat()
            ts("vector", ha.ap[:, 0:n], rq_a.ap[:, 0:n], -0.5, None, ALU.mult, None, [rq_a], [ha])
            S.op("vector", lambda e: e.tensor_scalar(out=rq_i.ap[:, 0:n].bitcast(I32), in0=rq_a.ap[:, 0:n].bitcast(I32),
                                                     scalar1=1, scalar2=None, op0=ALU.logical_shift_right),
                 reads=[rq_a], writes=[rq_i])
            y = rq_y
            S.op("vector", lambda e: e.tensor_scalar(out=rq_y.ap[:, 0:n].bitcast(I32), in0=rq_i.ap[:, 0:n].bitcast(I32),
                                                     scalar1=-1.0, scalar2=1597463007.0, op0=ALU.mult, op1=ALU.add),
                 reads=[rq_i], writes=[rq_y])
            y_ap = y.ap[:, 0:n]
            y_tl = [y]
            for it in range(3):
                u = new_stat()
                if it == 2 and out is not None:
                    y2_ap, y2_tl = out
                else:
                    y2 = new_stat()
                    y2_ap, y2_tl = y2.ap[:, 0:n], [y2]
                if n == 1:
                    stt(u.ap[:, 0:1], y_ap, ha.ap[:, 0:1], y_ap, ALU.mult, ALU.mult, y_tl + [ha], [u])
                else:
                    tt("vector", u.ap[:, 0:n], y_ap, y_ap, ALU.mult, y_tl, [u])
                    tt("vector", u.ap[:, 0:n], u.ap[:, 0:n], ha.ap[:, 0:n], ALU.mult, [u, ha], [u])
                stt(y2_ap, u.ap[:, 0:n], 1.5, y_ap, ALU.add, ALU.mult, [u] + y_tl, y2_tl)
                y_ap, y_tl = y2_ap, y2_tl
            return y_tl, y_ap

        total_chunks = NCH * ntiles
        ws = {"g": 0, "loaded": 0, "released": 0}

        def ws_load():
            g = ws["loaded"]
            if g >= total_chunks or g >= ws["released"] + NSLOT:
                return
            t_, c = divmod(g, NCH)
            slot = ring[g % NSLOT]
            wd = WIDTHS[c]
            ct = c % NCONV
            if t_ <= ct:
                S.dma("gpsimd", slot.ap[:, 0:wd], w_d[c][:, 0:wd], writes=[slot], dsem=slot.dsem)
                if t_ == ct and ntiles > ct + 1:
                    S.dma("sync", scr_d[c][:, 0:wd], slot.ap[:, 0:wd], reads=[slot], writes=[scr[c]], dsem=slot.ssem)
            else:
                S.dma("sync", slot.ap[:, 0:wd], scr_d[c][:, 0:wd], reads=[scr[c]], writes=[slot], dsem=slot.ssem)
            ws["loaded"] += 1

        def ws_get():
            g = ws["g"]
            assert ws["loaded"] > g
            ws["g"] += 1
            return ring[g % NSLOT]

        def ws_done(n=1):
            for _ in range(n):
                ws["released"] += 1
                assert ws["released"] <= ws["g"]
                ws_load()

        def x_load(r, q="sync"):
            slot = xin[r % 2]
            S.dma(q, slot.ap, x_d[r * 128:(r + 1) * 128, :], writes=[slot], dsem=slot.dsem)

        def xblk(t, tb):
            if t == 0 and tb >= 2:
                return ostage[tb % 2]
            return xin[(4 * t + tb) % 2]

        S.dma("sync", cst.ap, c_d, writes=[cst], dsem=cs)
        x_load(0, "sync")
        x_load(1, "sync")
        for tb_ in (2, 3):
            S.dma("sync", ostage[tb_ % 2].ap, x_d[tb_ * 128:(tb_ + 1) * 128, :], writes=[ostage[tb_ % 2]],
                  dsem=ostage[tb_ % 2].dsem)
        S.dma("sync", ropeT.ap, r_d.rearrange("a p t -> p a t")[:, :, 0:TT], writes=[ropeT], dsem=ropeT.dsem)
        PROLOGUE_XSTATS_MARK = None
        S.op("vector", lambda e: e.memset(ones_bf.ap, 1.0), writes=[ones_bf])
        S.op("vector", lambda e: e.memset(ones_f.ap, 1.0), writes=[ones_f])
        S.op("vector", lambda e: e.tensor_copy(out=identb.ap, in_=identf_ap), reads=[cst], writes=[identb])
        for blk in range(4):
            src = (blk ^ 1) * 32
            S.op("vector", lambda e, blk=blk, src=src: e.tensor_copy(
                out=permb.ap[:, blk * 32:(blk + 1) * 32], in_=identb.ap[:, src:src + 32]), reads=[identb], writes=[permb])
        for c in range(4):
            S.op("vector", lambda e, c=c: e.memset(zbuf[c].ap[:, 0:2], 0.0), writes=[zbuf[c]])
        for t in range(ntiles):
            S.op("gpsimd", lambda e, t=t: e.memset(Vt[t].ap[:, :, :, 128:129], 1.0), writes=[Vt[t]])
        ts("vector", gsub.ap, cst.ap[:, C_SUB:C_SUB + 128], 1.0 - LAMBDA_INIT, None, ALU.mult, None, [cst], [gsub])
        ts("vector", hb.ap, cst.ap[:, C_BG:C_BG + 16], 0.5, None, ALU.mult, None, [cst], [hb])
        lv = cst.ap[:, C_LAM:C_LAM + 256]
        for i in range(2):
            tt("vector", tmpf[0].ap[:, 0:64], lv[:, (2 * i) * 64:(2 * i + 1) * 64], lv[:, (2 * i + 1) * 64:(2 * i + 2) * 64],
               ALU.mult, [cst], [tmpf[0]])
            act(tmpf[1].ap[:, 0:64], tmpf[0].ap[:, 0:64], AF.Copy, [tmpf[0]], [tmpf[1], lam_t],
                accum_out=lam_t.ap[:, i:i + 1])
        act(lam_t.ap[:, 2:4], lam_t.ap[:, 0:2], AF.Exp, [lam_t], [lam_t])
        tt("vector", lam_t.ap[:, 4:5], lam_t.ap[:, 3:4], lam_t.ap[:, 2:3], ALU.subtract, [lam_t], [lam_t])
        ts("vector", lam_t.ap[:, 5:6], lam_t.ap[:, 4:5], -LAMBDA_INIT, None, ALU.add, None, [lam_t], [lam_t])
        neglam = lam_t.ap[:, 5:6]

        rsin_cols = [Tl(rsin.ap[:, i:i + 1], f"rsin{i}") for i in range(8)]

        ssx = Tl(sb("ssx", [128, 8], F32), "ssx")

        def x_stats_dma(t, tbs):
            for tb in tbs:
                r = 4 * t + tb
                slot = ostage[r % 2]
                S.dma("sync", slot.ap, x_d[r * 128:(r + 1) * 128, :], writes=[slot], dsem=slot.dsem)

        def x_stats_sq(t, tbs):
            for tb in tbs:
                r = 4 * t + tb
                slot = ostage[r % 2]
                act(junk.ap, slot.ap, AF.Square, [slot], [junk, ssx], accum_out=ssx.ap[:, tb:tb + 1])

        def x_stats_chain(t):
            base = (t % 2) * 4
            rstd_from_ss(ssx.ap[:, 0:4], ssx, 4, D, out=(rsin.ap[:, base:base + 4], rsin_cols[base:base + 4]))

        def x_stats_tile(t):
            x_stats_dma(t, [0, 1])
            x_stats_sq(t, [0, 1])
            x_stats_dma(t, [2, 3])
            x_stats_sq(t, [2, 3])
            x_stats_chain(t)

        def input_prep(t, tb):
            r = 4 * t + tb
            xs = xblk(t, tb)
            col = rsin_cols[(t % 2) * 4 + tb]
            stt(xn.ap, xs.ap, col.ap, cst.ap[:, C_G1B:C_G1B + D], ALU.mult, ALU.mult, [xs, col, cst], [xn])

        def input_pe(t, tb):
            r = 4 * t + tb
            xs = xblk(t, tb)
            pA, pB = (0, 1) if tb % 2 == 0 else (4, 5)
            qb = 2 if tb % 2 == 0 else 6
            for dc in range(8):
                bk = bank[pA] if dc < 4 else bank[pB]
                S.op("tensor", lambda e, dc=dc, bk=bk, xs=xs: e.transpose(
                    bk.ap[:, (dc % 4) * 128:(dc % 4 + 1) * 128], xs.ap[:, dc * 128:(dc + 1) * 128], identf_ap),
                    reads=[xs, cst], writes=[bk], inc=(dc % 4 == 3))
            for dc in range(8):
                S.op("tensor", lambda e, dc=dc, qb=qb: e.transpose(
                    bank_bf(qb)[:, dc * 128:(dc + 1) * 128], xn.ap[:, dc * 128:(dc + 1) * 128], identb.ap),
                    reads=[xn, identb], writes=[bank[qb]], inc=(dc == 7))
            act(hT_t[:, :, tb * 128:(tb + 1) * 128],
                ps_all[:, pA:pA + 2, :].rearrange("p k (a b) -> p (k a) b", a=4), AF.Copy,
                [bank[pA], bank[pB]], hT)
            act(pool_t[:, 0:8, tb * 128:(tb + 1) * 128], bank_bf(qb).rearrange("p (a b) -> p a b", a=8), AF.Copy,
                [bank[qb]], uT)
            if r + 2 < 4 * ntiles and r + 2 >= 4:
                x_load(r + 2)

        sbk = bank[6]
        rbk = bank[7]
        dummy = Tl(stat_t[:, 63, :], "dummy")

        def norm_sq(dt, sq=None):
            sq = sq or uT
            act(sq[dt].ap, hT[dt].ap, AF.Square, [hT[dt]], [sq[dt]])
            if dt >= 1:
                norm_mm(dt - 1, sq)
            if dt == 7:
                norm_mm(7, sq)

        def norm_mm(dt, sq):
            for tb in range(4):
                mm(sbk.ap[:, tb:tb + 1], sq[dt].ap[:, tb * 128:(tb + 1) * 128], ones_bf.ap[:, 0:1],
                   dt == 0 and tb == 0, dt == 7, [sq[dt], ones_bf], [sbk], tb == 3, skip_group_check=True)

        Rsb = tmpf[0]

        def norm_finish_deferred():
            rs_tl, rs = rstd_from_ss(sbk.ap[:, 0:4], sbk, 4, D)
            tt("vector", dg4.ap, identf_ap.unsqueeze(1).broadcast_to([128, 4, 128]),
               rs.unsqueeze(2).broadcast_to([128, 4, 128]), ALU.mult, [cst] + rs_tl, [dg4])

        def r_matmuls():
            for tb in range(4):
                mm(rbk.ap[:, tb * 128:(tb + 1) * 128], ones_f.ap, dg4.ap[:, tb, :], True, True, [ones_f, dg4], [rbk],
                   tb == 3)
            act(Rsb.ap, rbk.ap, AF.Copy, [rbk], [Rsb])

        def norm_finish(gcol0):
            rs_tl, rs = rstd_from_ss(sbk.ap[:, 0:4], sbk, 4, D)
            tt("vector", dg4.ap, identf_ap.unsqueeze(1).broadcast_to([128, 4, 128]),
               rs.unsqueeze(2).broadcast_to([128, 4, 128]), ALU.mult, [cst] + rs_tl, [dg4])
            for tb in range(4):
                mm(rbk.ap[:, tb * 128:(tb + 1) * 128], ones_f.ap, dg4.ap[:, tb, :], True, True, [ones_f, dg4], [rbk],
                   tb == 3)
            for dc in range(8):
                stt(uT[dc].ap, hT[dc].ap, cst.ap[:, gcol0 + dc:gcol0 + dc + 1], rbk.ap, ALU.mult, ALU.mult,
                    [hT[dc], cst, rbk], [uT[dc]])

        usb = [gP[2], gP[3]]

        def ffn(norm_after, hooks=None, post_scale=False):
            hooks = hooks or {}
            pending = []
            for j in range(NFF):
                if ("up", j) in hooks:
                    hooks[("up", j)]()
                if j == 2:
                    for ev in pending:
                        ev()
                    pending = []
                slot = ws_get()
                sv = slot.ap.rearrange("p (m k c) -> p m k c", m=2, k=8)
                bg_, bu_ = (bank[0], bank[1]) if j % 2 == 0 else (bank[2], bank[3])
                mm_group(bg_, bg_.ap, [(sv[:, 0, kc, :], uT[kc].ap) for kc in range(8)], [slot], uT)
                mm_group(bu_, bu_.ap, [(sv[:, 1, kc, :], uT[kc].ap) for kc in range(8)], [slot], uT)
                ws_done()
                sl = tmpb[j % 2]
                if post_scale:
                    def evac(j=j, bg_=bg_, bu_=bu_, sl=sl):
                        gs = tmpf[1 + j % 2]
                        tt("vector", gs.ap, bg_.ap, Rsb.ap, ALU.mult, [bg_, Rsb], [gs])
                        act(sl.ap, gs.ap, AF.Silu, [gs], [sl])
                        us = usb[j % 2]
                        tt("vector", us.ap, bu_.ap, Rsb.ap, ALU.mult, [bu_, Rsb], [us])
                        tt("vector", h1T[j].ap, sl.ap, us.ap, ALU.mult, [sl, us], [h1T[j]])
                    if j < 2:
                        pending.append(evac)
                    else:
                        evac()
                else:
                    act(sl.ap, bg_.ap, AF.Silu, [bg_], [sl])
                    tt("vector", h1T[j].ap, sl.ap, bu_.ap, ALU.mult, [sl, bu_], [h1T[j]])
            for dt in range(8):
                if ("down", dt) in hooks:
                    hooks[("down", dt)]()
                bk = bank[4 + dt % 2]
                for half in range(2):
                    slot = ws_get()
                    sv = slot.ap[:, 0:1408].rearrange("p (j c) -> p j c", j=11)
                    for jj in range(11):
                        j = half * 11 + jj
                        mm(bk.ap, sv[:, jj, :], h1T[j].ap, j == 0, j == NFF - 1, [slot, h1T[j]], [bk],
                           jj == 10)
                    ws_done()
                stt(hT[dt].ap, bk.ap, 0.5, hT[dt].ap, ALU.mult, ALU.add, [bk, hT[dt]], [hT[dt]])
                if norm_after:
                    norm_sq(dt)

        fm_state = {"slot": None, "m": 0}

        def fm_next():
            if fm_state["slot"] is None:
                fm_state["slot"] = ws_get()
                fm_state["m"] = 0
            slot = fm_state["slot"]
            m = fm_state["m"]
            v = slot.ap.rearrange("p (m k c) -> p m k c", m=2, k=8)[:, m]
            return slot, v

        def fm_adv():
            fm_state["m"] += 1
            if fm_state["m"] == 2:
                fm_state["slot"] = None
                ws_done()

        def fm_proj(bk):
            slot, v = fm_next()
            mm_group(bk, bk.ap, [(v[:, kc, :], uT[kc].ap) for kc in range(8)], [slot], uT)
            fm_adv()

        def mix_proj(t):
            cw = cst.ap[:, C_CW:C_CW + 12]
            for c in range(4):
                bA, bB, bC = (bank[0], bank[1], bank[2]) if c % 2 == 0 else (bank[4], bank[5], bank[6])
                fm_proj(bA)
                fm_proj(bB)
                fm_proj(bC)
                xcs = tmpf[0]
                act(xcs.ap, bA.ap, AF.Copy, [bA], [xcs])
                z = zbuf[c]
                if t > 0:
                    S.op("vector", lambda e, z=z: e.tensor_copy(out=z.ap[:, 0:2], in_=z.ap[:, TT:TT + 2]),
                         reads=[z], writes=[z])
                tt("vector", z.ap[:, 2:TT + 2], bB.ap, xcs.ap, ALU.mult, [bB, xcs], [z])
                acc = tmpf[1]
                ts("vector", acc.ap, z.ap[:, 0:TT], cw[:, 3 * c:3 * c + 1], None, ALU.mult, None, [z, cst], [acc])
                stt(acc.ap, z.ap[:, 1:TT + 1], cw[:, 3 * c + 1:3 * c + 2], acc.ap, ALU.mult, ALU.add, [z, cst, acc], [acc])
                stt(acc.ap, z.ap[:, 2:TT + 2], cw[:, 3 * c + 2:3 * c + 3], acc.ap, ALU.mult, ALU.add, [z, cst, acc], [acc])
                tt("vector", cT[c].ap, acc.ap, bC.ap, ALU.mult, [acc, bC], [cT[c]])
            items = [(which, h) for which in range(2) for h in range(4)]

            T1 = [tmpf[2], tmpf[1]]
            T2 = [tmpf[3], tmpf[0]]

            def rope_tail(i):
                which, h = items[i]
                bB = bank[2 * (i % 4) + 1]
                qb = tmpb[i % 2]
                mm(bB.ap, permb.ap, qb.ap, True, True, [permb, qb], [bB], True)
                t1 = T1[i % 2]
                t2 = T2[i % 2]
                tt("vector", t1.ap, bB.ap, ropeT.ap[:, 1, :], ALU.mult, [bB, ropeT], [t1])
                dst = QT[h] if which == 0 else KT[h][t]
                tt("gpsimd", dst.ap, t1.ap, t2.ap, ALU.add, [t1, t2], [dst])

            for i in range(8):
                bA = bank[2 * (i % 4)]
                fm_proj(bA)
                act(tmpb[i % 2].ap, bA.ap, AF.Copy, [bA], [tmpb[i % 2]])
                tt("vector", T2[i % 2].ap, bA.ap, ropeT.ap[:, 0, :], ALU.mult, [bA, ropeT], [T2[i % 2]])
                if i >= 1:
                    rope_tail(i - 1)
            rope_tail(7)
            if t + 1 < ntiles:
                S.dma("sync", ropeT.ap, r_d.rearrange("a p t -> p a t")[:, :, (t + 1) * TT:(t + 2) * TT],
                      writes=[ropeT], dsem=ropeT.dsem)
            s0 = ws_get()
            s1 = ws_get()
            v0 = s0.ap.rearrange("p (k c) -> p k c", k=4)
            v1 = s1.ap.rearrange("p (k c) -> p k c", k=4)
            for tb in range(4):
                bk = bank[4 + tb]
                mm_group(bk, bk.ap, [(uT[kc].ap[:, tb * 128:(tb + 1) * 128], (v0 if kc < 4 else v1)[:, kc % 4, :])
                                     for kc in range(8)], [s0, s1], uT)
                act(V_t[:, 4 * t + tb, :, 0:128], bk.ap.rearrange("p (h e) -> p h e", h=4), AF.Copy, [bk], [Vt[t]])
            ws_done(2)

        att_state = {"t": 0}

        def attention(t):
            att_state["t"] = t
            if t + 1 < ntiles:
                x_stats_dma(t + 1, [0, 1])
            steps = [(h, kb) for h in range(4) for kb in range(4 * t + 4)]
            nkb = 4 * t + 4

            def acc_ap(s, j):
                a = s * 4 + j
                b = bank[4 + a // 3]
                off = (a % 3) * 129
                return b, b.ap[:, off:off + 129]

            def emit_S(idx):
                h, kb = steps[idx]
                jmin = max(0, kb - 4 * t)
                ncol = (4 - jmin) * 128
                tk, kl = divmod(kb, 4)
                pi = idx % 2
                kt = KT[h][tk]
                for s in range(2):
                    bk = bank[2 * pi + s]
                    mm(bk.ap[:, 0:ncol], kt.ap[64 * s:64 * s + 64, kl * 128:(kl + 1) * 128],
                       QT[h].ap[64 * s:64 * s + 64, jmin * 128:TT], True, True, [kt, QT[h]], [bk], s == 1)

            st_ = {"started": set()}

            def emit_exp(idx):
                h, kb = steps[idx]
                jmin = max(0, kb - 4 * t)
                ncol = (4 - jmin) * 128
                pi = idx % 2
                pt = PT[idx % 3]
                act(pt.ap[:, :, 0:ncol], ps_all[:, 2 * pi:2 * pi + 2, 0:ncol], AF.Exp,
                    [bank[2 * pi], bank[2 * pi + 1]], [pt], scale=0.125)
                if kb >= 4 * t:
                    S.op("gpsimd", lambda e, pt=pt: e.memset(pt.ap[64:128, :, 0:64], 0.0), writes=[pt])

            def emit_AV(idx):
                h, kb = steps[idx]
                if kb == 0:
                    st_["started"] = set()
                started = st_["started"]
                jmin = max(0, kb - 4 * t)
                tk, kl = divmod(kb, 4)
                pt = PT[idx % 3]
                vt = Vt[tk]
                for s in range(2):
                    for j in range(jmin, 4):
                        b, aap = acc_ap(s, j)
                        first = b.name not in started
                        started.add(b.name)
                        last_for_j = (kb == 4 * t + j)
                        mm(aap, pt.ap[:, s, (j - jmin) * 128:(j - jmin + 1) * 128], V_t[:, kb, h, :],
                           first, last_for_j, [pt, vt], [b], (s == 1 and j == 3), skip_group_check=True)
                if kb == nkb - 1:
                    finalize_head(h)

            emit_S(0)
            for idx in range(len(steps)):
                if idx + 1 < len(steps):
                    emit_S(idx + 1)
                emit_exp(idx)
                if idx >= 1:
                    emit_AV(idx - 1)
            emit_AV(len(steps) - 1)

        def otok_transposes():
            for h in range(4):
                for j in range(4):
                    S.op("tensor", lambda e, h=h, j=j: e.transpose(
                        bank_bf(1)[:, j * 128:(j + 1) * 128], otok[j].ap[:, h * 128:(h + 1) * 128], identb.ap),
                        reads=[otok[j], identb], writes=[bank[1]], inc=(j == 3))
                S.op("vector", lambda e, h=h: e.tensor_copy(out=oT[h].ap, in_=bank_bf(1)[:, 0:TT]),
                     reads=[bank[1]], writes=[oT[h]])

        def finalize_head(h):
            if att_state["t"] <= 1:
                act(ostgA.ap, bank[4].ap[:, 0:387], AF.Copy, [bank[4]], [ostgA])
                act(ostgB.ap[:, 0, :], bank[5].ap[:, 0:387], AF.Copy, [bank[5]], [ostgB])
                act(ostgC.ap[:, 0:258], bank[6].ap[:, 0:258], AF.Copy, [bank[6]], [ostgC])
            else:
                S.op("vector", lambda e: e.tensor_copy(out=ostgA.ap, in_=bank[4].ap[:, 0:387]),
                     reads=[bank[4]], writes=[ostgA])
                S.op("vector", lambda e: e.tensor_copy(out=ostgB.ap[:, 0, :], in_=bank[5].ap[:, 0:387]),
                     reads=[bank[5]], writes=[ostgB])
                S.op("vector", lambda e: e.tensor_copy(out=ostgC.ap[:, 0:258], in_=bank[6].ap[:, 0:258]),
                     reads=[bank[6]], writes=[ostgC])
            stg = [ostgA, ostgB, ostgC]
            sap = [ostgA.ap, ostgB.ap[:, 0, :], ostgC.ap]

            def accs(a):
                return stg[a // 3], sap[a // 3][:, (a % 3) * 129:(a % 3) * 129 + 128]

            ls = new_stat()
            for bi in range(3):
                n = 3 if bi < 2 else 2
                S.op("vector", lambda e, bi=bi, n=n, ls=ls: e.tensor_copy(
                    out=ls.ap[:, 3 * bi:3 * bi + n],
                    in_=sap[bi][:, 0:129 * n].rearrange("p (a c) -> p a c", a=n)[:, :, 128]),
                    reads=[stg[bi]], writes=[ls])
            rl = new_stat()
            S.op("vector", lambda e: e.reciprocal(out=rl.ap, in_=ls.ap), reads=[ls], writes=[rl])
            rn = new_stat()
            ts("vector", rn.ap[:, 0:4], rl.ap[:, 4:8], neglam, None, ALU.mult, None, [rl, lam_t], [rn])
            sso = new_stat()
            for j in range(4):
                t0_, a0 = accs(j)
                t1_, a1 = accs(4 + j)
                tq = ot2[j % 2]
                ts("vector", tq.ap, a1, rn.ap[:, j:j + 1], None, ALU.mult, None, [t1_, rn], [tq])
                stt(o4.ap[:, j, :], a0, rl.ap[:, j:j + 1], tq.ap, ALU.mult, ALU.add, [t0_, rl, tq], [o4])
            for j in range(4):
                S.op("vector", lambda e, j=j, sso=sso: e.scalar_tensor_tensor(
                    out=junk2.ap, in0=o4.ap[:, j, :], scalar=1.0, in1=o4.ap[:, j, :], op0=ALU.mult, op1=ALU.mult,
                    accum_out=sso.ap[:, j:j + 1]), reads=[o4], writes=[junk2, sso])
            rs_tl, rs = rstd_from_ss(sso.ap[:, 0:4], sso, 4, 128)
            for j in range(4):
                stt(otok[j].ap[:, h * 128:(h + 1) * 128], o4.ap[:, j, :], rs[:, j:j + 1], gsub.ap, ALU.mult, ALU.mult,
                    [o4, gsub] + rs_tl, [otok[j]])

        def gates_yc(dt):
            bgc, bga = (bank[4], bank[5]) if dt % 2 == 0 else (bank[6], bank[7])
            byc = bank[0] if dt % 2 == 0 else bank[2]
            fm_proj(bgc)
            fm_proj(bga)
            tc_ = gP[2 * (dt % 2)]
            ta_ = gP[2 * (dt % 2) + 1]
            act(tc_.ap, bgc.ap, AF.Tanh, [bgc, hb], [tc_], scale=0.5, bias=hb.ap[:, dt:dt + 1])
            act(ta_.ap, bga.ap, AF.Tanh, [bga, hb], [ta_], scale=0.5, bias=hb.ap[:, 8 + dt:9 + dt])
            c_ = ws_get()
            cv = c_.ap[:, 0:512].rearrange("p (k c) -> p k c", k=4)
            mm_group(byc, byc.ap, [(cv[:, kc, :], cT[kc].ap) for kc in range(4)], [c_], cT)
            ws_done()

        def merge_wo():
            gates_yc(0)
            gates_yc(1)
            otok_transposes()
            for dt in range(8):
                byc, bya = (bank[0], bank[1]) if dt % 2 == 0 else (bank[2], bank[3])
                a_sl = ws_get()
                av = a_sl.ap[:, 0:512].rearrange("p (k c) -> p k c", k=4)
                mm_group(bya, bya.ap, [(av[:, kc, :], oT[kc].ap) for kc in range(4)], [a_sl], oT)
                ws_done()
                tc_ = gP[2 * (dt % 2)]
                ta_ = gP[2 * (dt % 2) + 1]
                a_ = tmpf[0]
                b_ = tmpf[1]
                stt(a_.ap, tc_.ap, 1.0, byc.ap, ALU.add, ALU.mult, [tc_, byc], [a_])
                stt(b_.ap, ta_.ap, 1.0, bya.ap, ALU.add, ALU.mult, [ta_, bya], [b_])
                tt("gpsimd", mT[dt].ap, a_.ap, b_.ap, ALU.add, [a_, b_], [mT[dt]])
                if dt + 2 < 8:
                    gates_yc(dt + 2)
            for i in range(4):
                slot = ws_get()
                sv = slot.ap.rearrange("p (m k c) -> p m k c", m=2, k=8)
                for m in range(2):
                    dt2 = 2 * i + m
                    bk = bank[dt2 % 4]
                    mm_group(bk, bk.ap, [(sv[:, m, dt, :], mT[dt].ap) for dt in range(8)], [slot], mT)
                    stt(hT[dt2].ap, bk.ap, 0.5, hT[dt2].ap, ALU.mult, ALU.add, [bk, hT[dt2]], [hT[dt2]])
                    norm_sq(dt2, pool[8:16])
                    act(uT[dt2].ap, hT[dt2].ap, AF.Copy, [hT[dt2], cst], [uT[dt2]],
                        scale=cst.ap[:, C_G2 + dt2:C_G2 + dt2 + 1])
                ws_done()

        def final_stats():
            rstd_from_ss(sbk.ap[:, 0:4], sbk, 4, D, out=(rsfin.ap[:, 0:4], [rsfin]))

        fin_ps = ps_all[:, 3:8:4, :]

        def final_tb(t, tb):
            gfb = cst.ap[:, C_GFB:C_GFB + D].rearrange("p (k n) -> p k n", k=2)
            r = 4 * t + tb
            if t == ntiles - 1:
                bA, bB = bank[2 * tb], bank[2 * tb + 1]
                fps = ps_all[:, 2 * tb:2 * tb + 2, :]
            else:
                bA, bB = bank[3], bank[7]
                fps = fin_ps
            for dc in range(8):
                bk = bA if dc < 4 else bB
                S.op("tensor", lambda e, dc=dc, bk=bk, tb=tb: e.transpose(
                    bk.ap[:, (dc % 4) * 128:(dc % 4 + 1) * 128], hT[dc].ap[:, tb * 128:(tb + 1) * 128], identf_ap),
                    reads=[hT[dc], cst], writes=[bk], inc=(dc % 4 == 3))
            os_ = ostage[r % 2]
            stt(os_.ap.rearrange("p (k n) -> p k n", k=2), fps, rsfin.ap[:, tb:tb + 1], gfb, ALU.mult, ALU.mult,
                [bA, bB, rsfin, cst], [os_])
            S.dma("sync", o_d[r * 128:(r + 1) * 128, :], os_.ap, reads=[os_], dsem=os_.dsem)

        def ffn2_with_prefetch(t):
            n_ = t + 1
            if n_ >= ntiles:
                ffn(True, {("up", 2): r_matmuls}, True)
                return
            ffn(True, {
                ("up", 2): r_matmuls,
                ("up", 0): lambda: (x_stats_sq(n_, [0, 1]), x_stats_dma(n_, [2, 3])),
                ("up", 11): lambda: x_stats_sq(n_, [2, 3]),
                ("down", 0): lambda: x_stats_chain(n_),
                ("down", 2): lambda: input_prep(n_, 0),
            }, True)

        stages = [lambda t: ffn(True), lambda t: norm_finish(C_GMIX), lambda t: mix_proj(t),
                  lambda t: attention(t), lambda t: merge_wo(), lambda t: norm_finish_deferred(),
                  lambda t: ffn2_with_prefetch(t), lambda t: final_stats()]
        for xs_ in xin + ostage:
            S._wait("gpsimd", xs_.dsem.key, xs_.dsem.sem, 16)
        for _ in range(NSLOT):
            ws_load()

        def input_stats_direct(tb):
            xs = xblk(0, tb)
            ss = new_stat()
            act(junk.ap, xs.ap, AF.Square, [xs], [junk, ss], accum_out=ss.ap[:, 0:1])
            rstd_from_ss(ss.ap[:, 0:1], ss, 1, D, out=(rsin.ap[:, tb:tb + 1], [rsin_cols[tb]]))

        for tb in range(4):
            input_stats_direct(tb)
            input_prep(0, tb)
            input_pe(0, tb)
        for t in range(ntiles):
            for si, stg_ in enumerate(stages):
                if si < stop_after:
                    stg_(t)
            for tb in range(4):
                if t + 1 < ntiles and tb >= 1:
                    input_prep(t + 1, tb)
                final_tb(t, tb)
                if t + 1 < ntiles:
                    input_pe(t + 1, tb)

        for os_ in ostage:
            S._wait("sync", os_.dsem.key, os_.dsem.sem, os_.dsem.cnt)
        assert stop_after < 99 or ws["g"] == total_chunks, (ws["g"], total_chunks)
        S.emit()
    return nc


def _pad(a):
    out = np.zeros((128, 2048), np.float32)
    out[:, :a.shape[1]] = a
    return out


def _fm_chunk(A, B):
    k = A.shape[0] // 128
    s = np.stack([A.reshape(k, 128, 128), B.reshape(k, 128, 128)])
    return np.ascontiguousarray(s.transpose(2, 0, 1, 3)).reshape(128, 2 * k * 128)


def _ffn_chunks(wg, wu, wd):
    ch = []
    for j in range(NFF):
        ch.append(_fm_chunk(wg[:, j * 128:(j + 1) * 128], wu[:, j * 128:(j + 1) * 128]))
    for dt in range(8):
        for half in range(2):
            blk = wd[half * 11 * 128:(half + 1) * 11 * 128, dt * 128:(dt + 1) * 128]
            a = np.ascontiguousarray(blk.reshape(11, 128, 128).transpose(1, 0, 2)).reshape(128, 1408)
            ch.append(_pad(a))
    return ch


def _swap_perm():
    p = []
    for s in range(2):
        p += list(range(s * 64 + 32, s * 64 + 64)) + list(range(s * 64, s * 64 + 32))
    return np.array(p)


def build_wstream(inp):
    w_in = inp["w_in"][0]
    ch = _ffn_chunks(inp["ffn1_gate"][0], inp["ffn1_up"][0], inp["ffn1_down"][0])
    tiles = []
    for c in range(4):
        tiles.append(w_in[:, 0 + c * 128:0 + (c + 1) * 128])
        tiles.append(w_in[:, 1024 + c * 128:1024 + (c + 1) * 128])
        tiles.append(w_in[:, 512 + c * 128:512 + (c + 1) * 128])
    for i in range(0, 12, 2):
        ch.append(_fm_chunk(tiles[i], tiles[i + 1]))
    for base in (1536, 2048):
        for h in range(0, 4, 2):
            ch.append(_fm_chunk(w_in[:, base + h * 128:base + (h + 1) * 128],
                                w_in[:, base + (h + 1) * 128:base + (h + 2) * 128]))
    wv = w_in[:, 2560:3072]
    for kq in range(2):
        a = wv[kq * 512:(kq + 1) * 512, :].reshape(4, 128, 512).transpose(1, 0, 2)
        ch.append(np.ascontiguousarray(a).reshape(128, 2048))
    wco = inp["w_conv_out"][0].reshape(4, 128, 8, 128)
    wao = inp["w_attn_out"][0].reshape(4, 128, 8, 128)
    def _g(dt):
        return _fm_chunk(w_in[:, 3072 + dt * 128:3072 + (dt + 1) * 128], w_in[:, 4096 + dt * 128:4096 + (dt + 1) * 128])

    def _ca(wmat, dt):
        return _pad(np.ascontiguousarray(wmat[:, :, dt, :].transpose(1, 0, 2)).reshape(128, 512))

    ch += [_g(0), _ca(wco, 0), _g(1), _ca(wco, 1)]
    for dt in range(8):
        ch.append(_ca(wao, dt))
        if dt + 2 < 8:
            ch += [_g(dt + 2), _ca(wco, dt + 2)]
    wo = inp["w_o"][0]
    for i in range(4):
        ch.append(_fm_chunk(wo[:, (2 * i) * 128:(2 * i + 1) * 128], wo[:, (2 * i + 1) * 128:(2 * i + 2) * 128]))
    ch += _ffn_chunks(inp["ffn2_gate"][0], inp["ffn2_up"][0], inp["ffn2_down"][0])
    assert len(ch) == NCH, (len(ch), NCH)
    return np.stack(ch).astype(np.float32)


def build_consts(inp):
    c = np.zeros((128, C_TOT), np.float32)
    c[:, C_GMIX:C_GMIX + 8] = inp["norm_mix"][0].reshape(8, 128).T
    c[:, C_G2:C_G2 + 8] = inp["norm_ffn2"][0].reshape(8, 128).T
    c[:, C_BG:C_BG + 16] = inp["b_gate"][0].reshape(16, 128).T
    cw = inp["conv_w"][0]
    c[:, C_CW:C_CW + 12] = cw.reshape(3, 4, 128).transpose(2, 1, 0).reshape(128, 12)
    lam = np.concatenate([inp["lambda_q1"][0], inp["lambda_k1"][0], inp["lambda_q2"][0], inp["lambda_k2"][0]])
    c[:, C_LAM:C_LAM + 256] = np.broadcast_to(lam[None, :], (128, 256))
    c[:, C_SUB:C_SUB + 128] = np.broadcast_to(inp["subln_g"][0][None, :], (128, 128))
    c[:, C_ID:C_ID + 128] = np.eye(128, dtype=np.float32)
    c[:, C_G1B:C_G1B + D] = np.broadcast_to(inp["norm_ffn1"][0][None, :], (128, D))
    c[:, C_GFB:C_GFB + D] = np.broadcast_to(inp["norm_final"][None, :], (128, D))
    return c


def build_rope():
    inv_freq = (1.0 / (np.float32(10000.0) ** (np.arange(0, 64, 2, dtype=np.float32) / np.float32(64)))).astype(np.float32)
    ang = (np.arange(SEQ, dtype=np.float32)[:, None] * inv_freq[None, :]).astype(np.float32)
    cos = np.cos(ang).astype(np.float32).T
    sin = np.sin(ang).astype(np.float32).T
    tab = np.zeros((2, 128, SEQ), np.float32)
    for p in range(128):
        d = p % 64
        tab[0, p] = cos[d % 32]
        tab[1, p] = -sin[d % 32] if d < 32 else sin[d % 32]
    return tab


_CACHE = {}


def kernel(**inputs):
    inp = {k: np.asarray(v) for k, v in inputs.items()}
    x = inp["x"]
    n = x.shape[0]
    wsrc = build_wstream(inp)
    consts = build_consts(inp)
    rope = build_rope()
    if "nc" not in _CACHE:
        _CACHE["nc"] = build_program(NT_FULL)
    nc = _CACHE["nc"]
    in_maps = [{"x": np.ascontiguousarray(x[i]), "wsrc": wsrc, "consts": consts, "rope": rope} for i in range(n)]
    res = run_bass_kernel_spmd(nc, in_maps, core_ids=list(range(n)))
    return np.stack([np.asarray(r["out"]) for r in res.results]).astype(np.float32)
```

```python
from contextlib import ExitStack

import numpy as np
import concourse.bass as bass
import concourse.mybir as mybir
from concourse.bass_utils import run_bass_kernel_spmd

F32 = mybir.dt.float32
BF16 = mybir.dt.bfloat16
I32 = mybir.dt.int32
AF = mybir.ActivationFunctionType
ALU = mybir.AluOpType

ENGS = ("tensor", "vector", "scalar", "gpsimd", "sync")

D = 1024
SEQ = 4096
FF = 2816
NFF = 22
TT = 512
NT_FULL = SEQ // TT
EPS = 1e-6
NSLOT = 5
NCONV = 3
LAMBDA_INIT = 0.8 - 0.6 * 1.0

C_GMIX, C_G2, C_BG, C_CW, C_LAM, C_SUB, C_ID, C_G1B, C_GFB, C_TOT = 0, 8, 16, 32, 48, 304, 432, 560, 1584, 2608


class DSem:
    def __init__(self, sem, key):
        self.sem = sem
        self.key = key
        self.cnt = 0


class Tl:
    def __init__(self, ap, name=""):
        self.ap = ap
        self.name = name
        self.w = {}
        self.r = {}
        self.dsem = None
        self.excl = False


class Sched:
    def __init__(self, nc, stack):
        self.nc = nc
        self.stack = stack
        self.ops = {e: [] for e in ENGS}
        self.sem = {e: stack.enter_context(nc.semaphore("s_" + e)) for e in ENGS}
        self.cnt = {e: 0 for e in ENGS}
        self.waited = {e: {} for e in ENGS}
        self.nds = 0

    def new_dsem(self):
        self.nds += 1
        s = self.stack.enter_context(self.nc.semaphore(f"d{self.nds}"))
        return DSem(s, f"d{self.nds}")

    def _need(self, eng, key, sem, val, acc):
        if self.waited[eng].get(key, 0) >= val:
            return
        if key == eng and val > self.cnt[eng]:
            return
        self.waited[eng][key] = val
        acc[key] = (sem, val)

    def _wait(self, eng, key, sem, val):
        acc = {}
        self._need(eng, key, sem, val, acc)
        for (sm, v) in acc.values():
            self.ops[eng].append(lambda e, sm=sm, v=v: e.wait_ge(sm, v))

    def _deps(self, eng, reads, writes):
        acc = {}
        for t in reads:
            for k, (s, v) in t.w.items():
                self._need(eng, k, s, v, acc)
            if t.excl:
                for k, (s, v) in t.r.items():
                    if k != eng:
                        self._need(eng, k, s, v, acc)
        for t in writes:
            for k, (s, v) in t.w.items():
                self._need(eng, k, s, v, acc)
            for k, (s, v) in t.r.items():
                self._need(eng, k, s, v, acc)
        return list(acc.values())

    def _emit_waits(self, eng, waits, attach):
        if attach and waits:
            for (sm, v) in waits[:-1]:
                self.ops[eng].append(lambda e, sm=sm, v=v: e.wait_ge(sm, v))
            return waits[-1]
        for (sm, v) in waits:
            self.ops[eng].append(lambda e, sm=sm, v=v: e.wait_ge(sm, v))
        return None

    def _mark(self, key, sem, val, reads, writes):
        for t in writes:
            t.w = {key: (sem, val)}
            t.r = {}
        for t in reads:
            old = t.r.get(key)
            if old is None or old[1] < val:
                t.r[key] = (sem, val)

    def op(self, eng, fn, reads=(), writes=(), inc=True):
        waits = self._deps(eng, reads, writes)
        last = self._emit_waits(eng, waits, True)
        sem = self.sem[eng]
        if inc:
            self.cnt[eng] += 1
            val = self.cnt[eng]
        else:
            val = self.cnt[eng] + 1

        def run(e, fn=fn, sem=sem, last=last, inc=inc):
            ins = fn(e)
            if last is not None:
                ins = ins._wait_ge(last[0], last[1])
            if inc:
                ins.then_inc(sem, 1)
        self.ops[eng].append(run)
        self._mark(eng, sem, val, reads, writes)

    def dma(self, q, out_ap, in_ap, reads=(), writes=(), dsem=None):
        waits = self._deps(q, reads, writes)
        last = self._emit_waits(q, waits, q == "sync")
        dsem.cnt += 16
        val = dsem.cnt
        s = dsem.sem

        def run(e, o=out_ap, i=in_ap, s=s, last=last):
            ins = e.dma_start(out=o, in_=i)
            if last is not None:
                ins = ins._wait_ge(last[0], last[1])
            ins.then_inc(s, 16)
        self.ops[q].append(run)
        self._mark(dsem.key, s, val, reads, writes)

    def emit(self):
        with self.nc.Block() as block:
            @block.tensor
            def _(e):
                for f in self.ops["tensor"]:
                    f(e)

            @block.vector
            def _(e):
                for f in self.ops["vector"]:
                    f(e)

            @block.scalar
            def _(e):
                for f in self.ops["scalar"]:
                    f(e)

            @block.gpsimd
            def _(e):
                for f in self.ops["gpsimd"]:
                    f(e)

            @block.sync
            def _(e):
                for f in self.ops["sync"]:
                    f(e)


def chunk_widths():
    w = []
    for _f in range(2):
        pass
    ffn = [2048] * NFF + [1408] * 16
    mix = [2048] * 6 + [2048] * 4 + [2048] * 2
    mrg = []
    mrg += [2048, 512, 2048, 512]
    for _dt in range(8):
        mrg += [512]
        if _dt + 2 < 8:
            mrg += [2048, 512]
    wo = [2048] * 4
    w = ffn + mix + mrg + wo + ffn
    return w


WIDTHS = chunk_widths()
NCH = len(WIDTHS)


def build_program(ntiles=NT_FULL, stop_after=99):
    nc = bass.Bass("TRN2", target_bir_lowering=False)
    x_d = nc.dram_tensor("x", [SEQ, D], F32, kind="ExternalInput").ap()
    w_d = nc.dram_tensor("wsrc", [NCH, 128, 2048], F32, kind="ExternalInput").ap()
    c_d = nc.dram_tensor("consts", [128, C_TOT], F32, kind="ExternalInput").ap()
    r_d = nc.dram_tensor("rope", [2, 128, SEQ], F32, kind="ExternalInput").ap()
    o_d = nc.dram_tensor("out", [SEQ, D], F32, kind="ExternalOutput").ap()
    scr_d = nc.dram_tensor("wscratch", [NCH, 128, 2048], BF16).ap()

    with ExitStack() as st:
        S = Sched(nc, st)

        def sb(name, shape, dt):
            return st.enter_context(nc.sbuf_tensor(name, shape, dt)).ap()

        pool_t = sb("pool", [128, 32, TT], BF16)
        pool = [Tl(pool_t[:, i, :], f"pool{i}") for i in range(32)]
        uT = pool[0:8]
        h1T = pool[8:30]
        cT = pool[8:12]
        QT = pool[12:16]
        oT = pool[16:20]
        mT = pool[20:28]
        gP = pool[28:32]
        hT_t = sb("hT", [128, 8, TT], F32)
        hT = [Tl(hT_t[:, i, :], f"hT{i}") for i in range(8)]
        KT_t = sb("KT", [128, 4, SEQ], BF16)
        KT = [[Tl(KT_t[:, h, t * TT:(t + 1) * TT], f"KT{h}_{t}") for t in range(NT_FULL)] for h in range(4)]
        V_t = sb("V", [128, 32, 4, 129], BF16)
        Vt = [Tl(V_t[:, 4 * t:4 * t + 4], f"V{t}") for t in range(NT_FULL)]
        ring = [Tl(sb(f"ring{i}", [128, 2048], BF16), f"ring{i}") for i in range(NSLOT)]
        xin = [Tl(sb(f"xin{i}", [128, D], F32), f"xin{i}") for i in range(2)]
        xn = Tl(sb("xn", [128, D], BF16), "xn")
        junk = Tl(sb("junk", [128, D], BF16), "junk")
        ostage = [Tl(sb(f"ost{i}", [128, D], F32), f"ost{i}") for i in range(2)]
        zbuf = [Tl(sb(f"zbuf{i}", [128, TT + 2], F32), f"zbuf{i}") for i in range(4)]
        tmpf = [Tl(sb(f"tmpf{i}", [128, TT], F32), f"tmpf{i}") for i in range(4)]
        tmpb = [Tl(sb(f"tmpb{i}", [128, TT], BF16), f"tmpb{i}") for i in range(2)]
        PT = [Tl(sb(f"PT{i}", [128, 2, TT], BF16), f"PT{i}") for i in range(3)]
        otok = [Tl(sb(f"otok{i}", [128, TT], BF16), f"otok{i}") for i in range(4)]
        o4 = Tl(sb("o4", [128, 4, 128], F32), "o4")
        ostgA = Tl(sb("ostgA", [128, 387], F32), "ostgA")
        ostgB = Tl(sb("ostgB", [128, 1, 387], F32), "ostgB")
        ostgC = Tl(sb("ostgC", [128, 258], F32), "ostgC")
        junk2 = Tl(sb("junk2", [128, 128], BF16), "junk2")
        ot2 = [Tl(sb(f"ot2_{i}", [128, 128], F32), f"ot2_{i}") for i in range(2)]
        dg4 = Tl(sb("dg4", [128, 4, 128], F32), "dg4")
        rsfin = Tl(sb("rsfin", [128, 8], F32), "rsfin")
        rsin = Tl(sb("rsin", [128, 8], F32), "rsin")
        ropeT = Tl(sb("ropeT", [128, 2, TT], F32), "ropeT")
        cst = Tl(sb("cst", [128, C_TOT], F32), "cst")
        identb = Tl(sb("identb", [128, 128], BF16), "identb")
        permb = Tl(sb("permb", [128, 128], BF16), "permb")
        ones_bf = Tl(sb("ones_bf", [128, 2], BF16), "ones_bf")
        ones_f = Tl(sb("ones_f", [128, 128], F32), "ones_f")
        gsub = Tl(sb("gsub", [128, 128], F32), "gsub")
        hb = Tl(sb("hb", [128, 16], F32), "hb")
        lam_t = Tl(sb("lam_t", [128, 8], F32), "lam_t")
        stat_t = sb("stat", [128, 64, 8], F32)
        stats = [Tl(stat_t[:, i, :], f"stat{i}") for i in range(64)]
        stat_i = [0]
        rq_a = Tl(sb("rq_a", [128, 8], F32), "rq_a")
        rq_i = Tl(sb("rq_i", [128, 8], F32), "rq_i")
        rq_y = Tl(sb("rq_y", [128, 8], F32), "rq_y")

        def new_stat():
            s_ = stats[stat_i[0] % 64]
            stat_i[0] += 1
            return s_

        ps_all = st.enter_context(nc.psum_tensor("ps", [128, 8, 512], F32)).ap()
        bank = [Tl(ps_all[:, i, :], f"bank{i}") for i in range(8)]
        for b_ in bank:
            b_.excl = True

        def bank_bf(i):
            return ps_all[:, i, :].bitcast(BF16)

        for t_ in ring + xin + ostage + [ropeT]:
            t_.dsem = S.new_dsem()
        for t_ in ring:
            t_.ssem = S.new_dsem()
        cs = S.new_dsem()
        scr = [Tl(scr_d[c], f"scr{c}") for c in range(NCH)]

        identf_ap = cst.ap[:, C_ID:C_ID + 128]

        def mm(out_ap, lhsT, rhs, start, stop, reads, writes, inc, **kw):
            S.op("tensor", lambda e: e.matmul(out_ap, lhsT, rhs, start=start, stop=stop, **kw),
                 reads=reads, writes=writes, inc=inc)

        def mm_group(out_tl, out_ap, pairs, common, per):
            n = len(pairs)
            for i, (l, r) in enumerate(pairs):
                mm(out_ap, l, r, i == 0, i == n - 1, list(common) + [per[i]], [out_tl], i == n - 1)

        def act(out_ap, in_ap, func, reads, writes, **kw):
            S.op("scalar", lambda e: e.activation(out=out_ap, in_=in_ap, func=func, **kw), reads=reads, writes=writes)

        def tt(eng, out_ap, in0, in1, op, reads, writes):
            S.op(eng, lambda e: e.tensor_tensor(out=out_ap, in0=in0, in1=in1, op=op), reads=reads, writes=writes)

        def ts(eng, out_ap, in0, s1, s2, op0, op1, reads, writes):
            if s2 is None:
                S.op(eng, lambda e: e.tensor_scalar(out=out_ap, in0=in0, scalar1=s1, scalar2=None, op0=op0),
                     reads=reads, writes=writes)
            else:
                S.op(eng, lambda e: e.tensor_scalar(out=out_ap, in0=in0, scalar1=s1, scalar2=s2, op0=op0, op1=op1),
                     reads=reads, writes=writes)

        def stt(out_ap, in0, scalar, in1, op0, op1, reads, writes):
            S.op("vector", lambda e: e.scalar_tensor_tensor(out=out_ap, in0=in0, scalar=scalar, in1=in1, op0=op0, op1=op1),
                 reads=reads, writes=writes)

        def rstd_from_ss(ss_ap, ss_tl, n, width, out=None):
            ts("vector", rq_a.ap[:, 0:n], ss_ap, 1.0 / width, EPS, ALU.mult, ALU.add, [ss_tl], [rq_a])
            ha = new_stat()
            ts("vector", ha.ap[:, 0:n], rq_a.ap[:, 0:n], -0.5, None, ALU.mult, None, [rq_a], [ha])
            S.op("vector", lambda e: e.tensor_scalar(out=rq_i.ap[:, 0:n].bitcast(I32), in0=rq_a.ap[:, 0:n].bitcast(I32),
                                                     scalar1=1, scalar2=None, op0=ALU.logical_shift_right),
                 reads=[rq_a], writes=[rq_i])
            y = rq_y
            S.op("vector", lambda e: e.tensor_scalar(out=rq_y.ap[:, 0:n].bitcast(I32), in0=rq_i.ap[:, 0:n].bitcast(I32),
                                                     scalar1=-1.0, scalar2=1597463007.0, op0=ALU.mult, op1=ALU.add),
                 reads=[rq_i], writes=[rq_y])
            y_ap = y.ap[:, 0:n]
            y_tl = [y]
            for it in range(3):
                u = new_stat()
                if it == 2 and out is not None:
                    y2_ap, y2_tl = out
                else:
                    y2 = new_stat()
                    y2_ap, y2_tl = y2.ap[:, 0:n], [y2]
                if n == 1:
                    stt(u.ap[:, 0:1], y_ap, ha.ap[:, 0:1], y_ap, ALU.mult, ALU.mult, y_tl + [ha], [u])
                else:
                    tt("vector", u.ap[:, 0:n], y_ap, y_ap, ALU.mult, y_tl, [u])
                    tt("vector", u.ap[:, 0:n], u.ap[:, 0:n], ha.ap[:, 0:n], ALU.mult, [u, ha], [u])
                stt(y2_ap, u.ap[:, 0:n], 1.5, y_ap, ALU.add, ALU.mult, [u] + y_tl, y2_tl)
                y_ap, y_tl = y2_ap, y2_tl
            return y_tl, y_ap

        total_chunks = NCH * ntiles
        ws = {"g": 0, "loaded": 0, "released": 0}

        def ws_load():
            g = ws["loaded"]
            if g >= total_chunks or g >= ws["released"] + NSLOT:
                return
            t_, c = divmod(g, NCH)
            slot = ring[g % NSLOT]
            wd = WIDTHS[c]
            ct = c % NCONV
            if t_ <= ct:
                S.dma("gpsimd", slot.ap[:, 0:wd], w_d[c][:, 0:wd], writes=[slot], dsem=slot.dsem)
                if t_ == ct and ntiles > ct + 1:
                    S.dma("sync", scr_d[c][:, 0:wd], slot.ap[:, 0:wd], reads=[slot], writes=[scr[c]], dsem=slot.ssem)
            else:
                S.dma("sync", slot.ap[:, 0:wd], scr_d[c][:, 0:wd], reads=[scr[c]], writes=[slot], dsem=slot.ssem)
            ws["loaded"] += 1

        def ws_get():
            g = ws["g"]
            assert ws["loaded"] > g
            ws["g"] += 1
            return ring[g % NSLOT]

        def ws_done(n=1):
            for _ in range(n):
                ws["released"] += 1
                assert ws["released"] <= ws["g"]
                ws_load()

        def x_load(r, q="sync"):
            slot = xin[r % 2]
            S.dma(q, slot.ap, x_d[r * 128:(r + 1) * 128, :], writes=[slot], dsem=slot.dsem)

        def xblk(t, tb):
            if t == 0 and tb >= 2:
                return ostage[tb % 2]
            return xin[(4 * t + tb) % 2]

        S.dma("sync", cst.ap, c_d, writes=[cst], dsem=cs)
        x_load(0, "sync")
        x_load(1, "sync")
        for tb_ in (2, 3):
            S.dma("sync", ostage[tb_ % 2].ap, x_d[tb_ * 128:(tb_ + 1) * 128, :], writes=[ostage[tb_ % 2]],
                  dsem=ostage[tb_ % 2].dsem)
        S.dma("sync", ropeT.ap, r_d.rearrange("a p t -> p a t")[:, :, 0:TT], writes=[ropeT], dsem=ropeT.dsem)
        PROLOGUE_XSTATS_MARK = None
        S.op("vector", lambda e: e.memset(ones_bf.ap, 1.0), writes=[ones_bf])
        S.op("vector", lambda e: e.memset(ones_f.ap, 1.0), writes=[ones_f])
        S.op("vector", lambda e: e.tensor_copy(out=identb.ap, in_=identf_ap), reads=[cst], writes=[identb])
        for blk in range(4):
            src = (blk ^ 1) * 32
            S.op("vector", lambda e, blk=blk, src=src: e.tensor_copy(
                out=permb.ap[:, blk * 32:(blk + 1) * 32], in_=identb.ap[:, src:src + 32]), reads=[identb], writes=[permb])
        for c in range(4):
            S.op("vector", lambda e, c=c: e.memset(zbuf[c].ap[:, 0:2], 0.0), writes=[zbuf[c]])
        for t in range(ntiles):
            S.op("gpsimd", lambda e, t=t: e.memset(Vt[t].ap[:, :, :, 128:129], 1.0), writes=[Vt[t]])
        ts("vector", gsub.ap, cst.ap[:, C_SUB:C_SUB + 128], 1.0 - LAMBDA_INIT, None, ALU.mult, None, [cst], [gsub])
        ts("vector", hb.ap, cst.ap[:, C_BG:C_BG + 16], 0.5, None, ALU.mult, None, [cst], [hb])
        lv = cst.ap[:, C_LAM:C_LAM + 256]
        for i in range(2):
            tt("vector", tmpf[0].ap[:, 0:64], lv[:, (2 * i) * 64:(2 * i + 1) * 64], lv[:, (2 * i + 1) * 64:(2 * i + 2) * 64],
               ALU.mult, [cst], [tmpf[0]])
            act(tmpf[1].ap[:, 0:64], tmpf[0].ap[:, 0:64], AF.Copy, [tmpf[0]], [tmpf[1], lam_t],
                accum_out=lam_t.ap[:, i:i + 1])
        act(lam_t.ap[:, 2:4], lam_t.ap[:, 0:2], AF.Exp, [lam_t], [lam_t])
        tt("vector", lam_t.ap[:, 4:5], lam_t.ap[:, 3:4], lam_t.ap[:, 2:3], ALU.subtract, [lam_t], [lam_t])
        ts("vector", lam_t.ap[:, 5:6], lam_t.ap[:, 4:5], -LAMBDA_INIT, None, ALU.add, None, [lam_t], [lam_t])
        neglam = lam_t.ap[:, 5:6]

        rsin_cols = [Tl(rsin.ap[:, i:i + 1], f"rsin{i}") for i in range(8)]

        ssx = Tl(sb("ssx", [128, 8], F32), "ssx")

        def x_stats_dma(t, tbs):
            for tb in tbs:
                r = 4 * t + tb
                slot = ostage[r % 2]
                S.dma("sync", slot.ap, x_d[r * 128:(r + 1) * 128, :], writes=[slot], dsem=slot.dsem)

        def x_stats_sq(t, tbs):
            for tb in tbs:
                r = 4 * t + tb
                slot = ostage[r % 2]
                act(junk.ap, slot.ap, AF.Square, [slot], [junk, ssx], accum_out=ssx.ap[:, tb:tb + 1])

        def x_stats_chain(t):
            base = (t % 2) * 4
            rstd_from_ss(ssx.ap[:, 0:4], ssx, 4, D, out=(rsin.ap[:, base:base + 4], rsin_cols[base:base + 4]))

        def x_stats_tile(t):
            x_stats_dma(t, [0, 1])
            x_stats_sq(t, [0, 1])
            x_stats_dma(t, [2, 3])
            x_stats_sq(t, [2, 3])
            x_stats_chain(t)

        def input_prep(t, tb):
            r = 4 * t + tb
            xs = xblk(t, tb)
            col = rsin_cols[(t % 2) * 4 + tb]
            stt(xn.ap, xs.ap, col.ap, cst.ap[:, C_G1B:C_G1B + D], ALU.mult, ALU.mult, [xs, col, cst], [xn])

        def input_pe(t, tb):
            r = 4 * t + tb
            xs = xblk(t, tb)
            pA, pB = (0, 1) if tb % 2 == 0 else (4, 5)
            qb = 2 if tb % 2 == 0 else 6
            for dc in range(8):
                bk = bank[pA] if dc < 4 else bank[pB]
                S.op("tensor", lambda e, dc=dc, bk=bk, xs=xs: e.transpose(
                    bk.ap[:, (dc % 4) * 128:(dc % 4 + 1) * 128], xs.ap[:, dc * 128:(dc + 1) * 128], identf_ap),
                    reads=[xs, cst], writes=[bk], inc=(dc % 4 == 3))
            for dc in range(8):
                S.op("tensor", lambda e, dc=dc, qb=qb: e.transpose(
                    bank_bf(qb)[:, dc * 128:(dc + 1) * 128], xn.ap[:, dc * 128:(dc + 1) * 128], identb.ap),
                    reads=[xn, identb], writes=[bank[qb]], inc=(dc == 7))
            act(hT_t[:, :, tb * 128:(tb + 1) * 128],
                ps_all[:, pA:pA + 2, :].rearrange("p k (a b) -> p (k a) b", a=4), AF.Copy,
                [bank[pA], bank[pB]], hT)
            act(pool_t[:, 0:8, tb * 128:(tb + 1) * 128], bank_bf(qb).rearrange("p (a b) -> p a b", a=8), AF.Copy,
                [bank[qb]], uT)
            if r + 2 < 4 * ntiles and r + 2 >= 4:
                x_load(r + 2)

        sbk = bank[6]
        rbk = bank[7]
        dummy = Tl(stat_t[:, 63, :], "dummy")

        def norm_sq(dt, sq=None, defer_last=False):
            sq = sq or uT
            act(sq[dt].ap, hT[dt].ap, AF.Square, [hT[dt]], [sq[dt]])
            if dt >= 1:
                norm_mm(dt - 1, sq)
            if dt == 7 and not defer_last:
                norm_mm(7, sq)

        sq_alt = [pool[30 + (d_ % 2)] for d_ in range(8)]

        def mix_norm_and_v(t):
            s0 = ws_get()
            s1 = ws_get()
            v0 = s0.ap.rearrange("p (k c) -> p k c", k=4)
            v1 = s1.ap.rearrange("p (k c) -> p k c", k=4)

            def vgroup(tb):
                bk = bank[tb]
                mm_group(bk, bk.ap, [(uT[kc].ap[:, tb * 128:(tb + 1) * 128], (v0 if kc < 4 else v1)[:, kc % 4, :])
                                     for kc in range(8)], [s0, s1], uT)

            vgroup(0)
            norm_mm(7, sq_alt)
            for tb in range(1, 4):
                vgroup(tb)
            ws_done(2)
            rs_tl, rs = rstd_from_ss(sbk.ap[:, 0:4], sbk, 4, D)
            for tb in range(4):
                bk = bank[tb]
                act(V_t[:, 4 * t + tb, :, 0:128], bk.ap.rearrange("p (h e) -> p h e", h=4), AF.Copy,
                    [bk] + rs_tl, [Vt[t]], scale=rs[:, tb:tb + 1])
            tt("vector", dg4.ap, identf_ap.unsqueeze(1).broadcast_to([128, 4, 128]),
               rs.unsqueeze(2).broadcast_to([128, 4, 128]), ALU.mult, [cst] + rs_tl, [dg4])
            for tb in range(4):
                mm(rbk.ap[:, tb * 128:(tb + 1) * 128], ones_f.ap, dg4.ap[:, tb, :], True, True, [ones_f, dg4], [rbk],
                   tb == 3)
            for dc in range(8):
                stt(uT[dc].ap, hT[dc].ap, cst.ap[:, C_GMIX + dc:C_GMIX + dc + 1], rbk.ap, ALU.mult, ALU.mult,
                    [hT[dc], cst, rbk], [uT[dc]])

        def norm_mm(dt, sq):
            for tb in range(4):
                mm(sbk.ap[:, tb:tb + 1], sq[dt].ap[:, tb * 128:(tb + 1) * 128], ones_bf.ap[:, 0:1],
                   dt == 0 and tb == 0, dt == 7, [sq[dt], ones_bf], [sbk], tb == 3, skip_group_check=True)

        Rsb = tmpf[0]

        def norm_finish_deferred():
            rs_tl, rs = rstd_from_ss(sbk.ap[:, 0:4], sbk, 4, D)
            tt("vector", dg4.ap, identf_ap.unsqueeze(1).broadcast_to([128, 4, 128]),
               rs.unsqueeze(2).broadcast_to([128, 4, 128]), ALU.mult, [cst] + rs_tl, [dg4])

        def r_matmuls():
            for tb in range(4):
                mm(rbk.ap[:, tb * 128:(tb + 1) * 128], ones_f.ap, dg4.ap[:, tb, :], True, True, [ones_f, dg4], [rbk],
                   tb == 3)
            act(Rsb.ap, rbk.ap, AF.Copy, [rbk], [Rsb])

        def norm_finish(gcol0):
            rs_tl, rs = rstd_from_ss(sbk.ap[:, 0:4], sbk, 4, D)
            tt("vector", dg4.ap, identf_ap.unsqueeze(1).broadcast_to([128, 4, 128]),
               rs.unsqueeze(2).broadcast_to([128, 4, 128]), ALU.mult, [cst] + rs_tl, [dg4])
            for tb in range(4):
                mm(rbk.ap[:, tb * 128:(tb + 1) * 128], ones_f.ap, dg4.ap[:, tb, :], True, True, [ones_f, dg4], [rbk],
                   tb == 3)
            for dc in range(8):
                stt(uT[dc].ap, hT[dc].ap, cst.ap[:, gcol0 + dc:gcol0 + dc + 1], rbk.ap, ALU.mult, ALU.mult,
                    [hT[dc], cst, rbk], [uT[dc]])

        usb = [gP[2], gP[3]]

        def ffn(norm_after, hooks=None, post_scale=False, mix_defer=False):
            hooks = hooks or {}
            pending = []
            for j in range(NFF):
                if ("up", j) in hooks:
                    hooks[("up", j)]()
                if j == 2:
                    for ev in pending:
                        ev()
                    pending = []
                slot = ws_get()
                sv = slot.ap.rearrange("p (m k c) -> p m k c", m=2, k=8)
                bg_, bu_ = (bank[0], bank[1]) if j % 2 == 0 else (bank[2], bank[3])
                mm_group(bg_, bg_.ap, [(sv[:, 0, kc, :], uT[kc].ap) for kc in range(8)], [slot], uT)
                mm_group(bu_, bu_.ap, [(sv[:, 1, kc, :], uT[kc].ap) for kc in range(8)], [slot], uT)
                ws_done()
                sl = tmpb[j % 2]
                if post_scale:
                    def evac(j=j, bg_=bg_, bu_=bu_, sl=sl):
                        gs = tmpf[1 + j % 2]
                        tt("vector", gs.ap, bg_.ap, Rsb.ap, ALU.mult, [bg_, Rsb], [gs])
                        act(sl.ap, gs.ap, AF.Silu, [gs], [sl])
                        us = usb[j % 2]
                        tt("vector", us.ap, bu_.ap, Rsb.ap, ALU.mult, [bu_, Rsb], [us])
                        tt("vector", h1T[j].ap, sl.ap, us.ap, ALU.mult, [sl, us], [h1T[j]])
                    if j < 2:
                        pending.append(evac)
                    else:
                        evac()
                else:
                    act(sl.ap, bg_.ap, AF.Silu, [bg_], [sl])
                    tt("vector", h1T[j].ap, sl.ap, bu_.ap, ALU.mult, [sl, bu_], [h1T[j]])
            for dt in range(8):
                if ("down", dt) in hooks:
                    hooks[("down", dt)]()
                bk = bank[4 + dt % 2]
                for half in range(2):
                    slot = ws_get()
                    sv = slot.ap[:, 0:1408].rearrange("p (j c) -> p j c", j=11)
                    for jj in range(11):
                        j = half * 11 + jj
                        mm(bk.ap, sv[:, jj, :], h1T[j].ap, j == 0, j == NFF - 1, [slot, h1T[j]], [bk],
                           jj == 10)
                    ws_done()
                stt(hT[dt].ap, bk.ap, 0.5, hT[dt].ap, ALU.mult, ALU.add, [bk, hT[dt]], [hT[dt]])
                if norm_after and mix_defer:
                    act(uT[dt].ap, hT[dt].ap, AF.Copy, [hT[dt], cst], [uT[dt]],
                        scale=cst.ap[:, C_GMIX + dt:C_GMIX + dt + 1])
                    norm_sq(dt, sq_alt, defer_last=True)
                elif norm_after:
                    norm_sq(dt)

        fm_state = {"slot": None, "m": 0}

        def fm_next():
            if fm_state["slot"] is None:
                fm_state["slot"] = ws_get()
                fm_state["m"] = 0
            slot = fm_state["slot"]
            m = fm_state["m"]
            v = slot.ap.rearrange("p (m k c) -> p m k c", m=2, k=8)[:, m]
            return slot, v

        def fm_adv():
            fm_state["m"] += 1
            if fm_state["m"] == 2:
                fm_state["slot"] = None
                ws_done()

        def fm_proj(bk):
            slot, v = fm_next()
            mm_group(bk, bk.ap, [(v[:, kc, :], uT[kc].ap) for kc in range(8)], [slot], uT)
            fm_adv()

        def mix_proj(t):
            cw = cst.ap[:, C_CW:C_CW + 12]
            for c in range(4):
                bA, bB, bC = (bank[0], bank[1], bank[2]) if c % 2 == 0 else (bank[4], bank[5], bank[6])
                fm_proj(bA)
                fm_proj(bB)
                fm_proj(bC)
                xcs = tmpf[0]
                act(xcs.ap, bA.ap, AF.Copy, [bA], [xcs])
                z = zbuf[c]
                if t > 0:
                    S.op("vector", lambda e, z=z: e.tensor_copy(out=z.ap[:, 0:2], in_=z.ap[:, TT:TT + 2]),
                         reads=[z], writes=[z])
                tt("vector", z.ap[:, 2:TT + 2], bB.ap, xcs.ap, ALU.mult, [bB, xcs], [z])
                acc = tmpf[1]
                ts("vector", acc.ap, z.ap[:, 0:TT], cw[:, 3 * c:3 * c + 1], None, ALU.mult, None, [z, cst], [acc])
                stt(acc.ap, z.ap[:, 1:TT + 1], cw[:, 3 * c + 1:3 * c + 2], acc.ap, ALU.mult, ALU.add, [z, cst, acc], [acc])
                stt(acc.ap, z.ap[:, 2:TT + 2], cw[:, 3 * c + 2:3 * c + 3], acc.ap, ALU.mult, ALU.add, [z, cst, acc], [acc])
                tt("vector", cT[c].ap, acc.ap, bC.ap, ALU.mult, [acc, bC], [cT[c]])
            items = [(which, h) for which in range(2) for h in range(4)]

            T1 = [tmpf[2], tmpf[1]]
            T2 = [tmpf[3], tmpf[0]]

            def rope_tail(i):
                which, h = items[i]
                bB = bank[2 * (i % 4) + 1]
                qb = tmpb[i % 2]
                mm(bB.ap, permb.ap, qb.ap, True, True, [permb, qb], [bB], True)
                t1 = T1[i % 2]
                t2 = T2[i % 2]
                tt("vector", t1.ap, bB.ap, ropeT.ap[:, 1, :], ALU.mult, [bB, ropeT], [t1])
                dst = QT[h] if which == 0 else KT[h][t]
                tt("gpsimd", dst.ap, t1.ap, t2.ap, ALU.add, [t1, t2], [dst])

            for i in range(8):
                bA = bank[2 * (i % 4)]
                fm_proj(bA)
                act(tmpb[i % 2].ap, bA.ap, AF.Copy, [bA], [tmpb[i % 2]])
                tt("vector", T2[i % 2].ap, bA.ap, ropeT.ap[:, 0, :], ALU.mult, [bA, ropeT], [T2[i % 2]])
                if i >= 1:
                    rope_tail(i - 1)
            rope_tail(7)
            if t + 1 < ntiles:
                S.dma("sync", ropeT.ap, r_d.rearrange("a p t -> p a t")[:, :, (t + 1) * TT:(t + 2) * TT],
                      writes=[ropeT], dsem=ropeT.dsem)

        att_state = {"t": 0}

        def attention(t):
            att_state["t"] = t
            if t + 1 < ntiles:
                x_stats_dma(t + 1, [0, 1])
            steps = [(h, kb) for h in range(4) for kb in range(4 * t + 4)]
            nkb = 4 * t + 4

            def acc_ap(s, j):
                a = s * 4 + j
                b = bank[4 + a // 3]
                off = (a % 3) * 129
                return b, b.ap[:, off:off + 129]

            def emit_S(idx):
                h, kb = steps[idx]
                jmin = max(0, kb - 4 * t)
                ncol = (4 - jmin) * 128
                tk, kl = divmod(kb, 4)
                pi = idx % 2
                kt = KT[h][tk]
                for s in range(2):
                    bk = bank[2 * pi + s]
                    mm(bk.ap[:, 0:ncol], kt.ap[64 * s:64 * s + 64, kl * 128:(kl + 1) * 128],
                       QT[h].ap[64 * s:64 * s + 64, jmin * 128:TT], True, True, [kt, QT[h]], [bk], s == 1)

            st_ = {"started": set()}

            def emit_exp(idx):
                h, kb = steps[idx]
                jmin = max(0, kb - 4 * t)
                ncol = (4 - jmin) * 128
                pi = idx % 2
                pt = PT[idx % 3]
                act(pt.ap[:, :, 0:ncol], ps_all[:, 2 * pi:2 * pi + 2, 0:ncol], AF.Exp,
                    [bank[2 * pi], bank[2 * pi + 1]], [pt], scale=0.125)
                if kb >= 4 * t:
                    S.op("gpsimd", lambda e, pt=pt: e.memset(pt.ap[64:128, :, 0:64], 0.0), writes=[pt])

            def emit_AV(idx):
                h, kb = steps[idx]
                if kb == 0:
                    st_["started"] = set()
                started = st_["started"]
                jmin = max(0, kb - 4 * t)
                tk, kl = divmod(kb, 4)
                pt = PT[idx % 3]
                vt = Vt[tk]
                for s in range(2):
                    for j in range(jmin, 4):
                        b, aap = acc_ap(s, j)
                        first = b.name not in started
                        started.add(b.name)
                        last_for_j = (kb == 4 * t + j)
                        mm(aap, pt.ap[:, s, (j - jmin) * 128:(j - jmin + 1) * 128], V_t[:, kb, h, :],
                           first, last_for_j, [pt, vt], [b], (s == 1 and j == 3), skip_group_check=True)
                if kb == nkb - 1:
                    finalize_head(h)

            emit_S(0)
            for idx in range(len(steps)):
                if idx + 1 < len(steps):
                    emit_S(idx + 1)
                emit_exp(idx)
                if idx >= 1:
                    emit_AV(idx - 1)
            emit_AV(len(steps) - 1)

        def otok_transposes():
            for h in range(4):
                for j in range(4):
                    S.op("tensor", lambda e, h=h, j=j: e.transpose(
                        bank_bf(1)[:, j * 128:(j + 1) * 128], otok[j].ap[:, h * 128:(h + 1) * 128], identb.ap),
                        reads=[otok[j], identb], writes=[bank[1]], inc=(j == 3))
                S.op("vector", lambda e, h=h: e.tensor_copy(out=oT[h].ap, in_=bank_bf(1)[:, 0:TT]),
                     reads=[bank[1]], writes=[oT[h]])

        def finalize_head(h):
            if att_state["t"] <= 1:
                act(ostgA.ap, bank[4].ap[:, 0:387], AF.Copy, [bank[4]], [ostgA])
                act(ostgB.ap[:, 0, :], bank[5].ap[:, 0:387], AF.Copy, [bank[5]], [ostgB])
                act(ostgC.ap[:, 0:258], bank[6].ap[:, 0:258], AF.Copy, [bank[6]], [ostgC])
            else:
                S.op("vector", lambda e: e.tensor_copy(out=ostgA.ap, in_=bank[4].ap[:, 0:387]),
                     reads=[bank[4]], writes=[ostgA])
                S.op("vector", lambda e: e.tensor_copy(out=ostgB.ap[:, 0, :], in_=bank[5].ap[:, 0:387]),
                     reads=[bank[5]], writes=[ostgB])
                S.op("vector", lambda e: e.tensor_copy(out=ostgC.ap[:, 0:258], in_=bank[6].ap[:, 0:258]),
                     reads=[bank[6]], writes=[ostgC])
            stg = [ostgA, ostgB, ostgC]
            sap = [ostgA.ap, ostgB.ap[:, 0, :], ostgC.ap]

            def accs(a):
                return stg[a // 3], sap[a // 3][:, (a % 3) * 129:(a % 3) * 129 + 128]

            ls = new_stat()
            for bi in range(3):
                n = 3 if bi < 2 else 2
                S.op("vector", lambda e, bi=bi, n=n, ls=ls: e.tensor_copy(
                    out=ls.ap[:, 3 * bi:3 * bi + n],
                    in_=sap[bi][:, 0:129 * n].rearrange("p (a c) -> p a c", a=n)[:, :, 128]),
                    reads=[stg[bi]], writes=[ls])
            rl = new_stat()
            S.op("vector", lambda e: e.reciprocal(out=rl.ap, in_=ls.ap), reads=[ls], writes=[rl])
            rn = new_stat()
            ts("vector", rn.ap[:, 0:4], rl.ap[:, 4:8], neglam, None, ALU.mult, None, [rl, lam_t], [rn])
            sso = new_stat()
            for j in range(4):
                t0_, a0 = accs(j)
                t1_, a1 = accs(4 + j)
                tq = ot2[j % 2]
                ts("vector", tq.ap, a1, rn.ap[:, j:j + 1], None, ALU.mult, None, [t1_, rn], [tq])
                stt(o4.ap[:, j, :], a0, rl.ap[:, j:j + 1], tq.ap, ALU.mult, ALU.add, [t0_, rl, tq], [o4])
            for j in range(4):
                S.op("vector", lambda e, j=j, sso=sso: e.scalar_tensor_tensor(
                    out=junk2.ap, in0=o4.ap[:, j, :], scalar=1.0, in1=o4.ap[:, j, :], op0=ALU.mult, op1=ALU.mult,
                    accum_out=sso.ap[:, j:j + 1]), reads=[o4], writes=[junk2, sso])
            rs_tl, rs = rstd_from_ss(sso.ap[:, 0:4], sso, 4, 128)
            for j in range(4):
                stt(otok[j].ap[:, h * 128:(h + 1) * 128], o4.ap[:, j, :], rs[:, j:j + 1], gsub.ap, ALU.mult, ALU.mult,
                    [o4, gsub] + rs_tl, [otok[j]])

        def gates_yc(dt):
            bgc, bga = (bank[4], bank[5]) if dt % 2 == 0 else (bank[6], bank[7])
            byc = bank[0] if dt % 2 == 0 else bank[2]
            fm_proj(bgc)
            fm_proj(bga)
            tc_ = gP[2 * (dt % 2)]
            ta_ = gP[2 * (dt % 2) + 1]
            act(tc_.ap, bgc.ap, AF.Tanh, [bgc, hb], [tc_], scale=0.5, bias=hb.ap[:, dt:dt + 1])
            act(ta_.ap, bga.ap, AF.Tanh, [bga, hb], [ta_], scale=0.5, bias=hb.ap[:, 8 + dt:9 + dt])
            c_ = ws_get()
            cv = c_.ap[:, 0:512].rearrange("p (k c) -> p k c", k=4)
            mm_group(byc, byc.ap, [(cv[:, kc, :], cT[kc].ap) for kc in range(4)], [c_], cT)
            ws_done()

        def merge_wo():
            gates_yc(0)
            gates_yc(1)
            otok_transposes()
            for dt in range(8):
                byc, bya = (bank[0], bank[1]) if dt % 2 == 0 else (bank[2], bank[3])
                a_sl = ws_get()
                av = a_sl.ap[:, 0:512].rearrange("p (k c) -> p k c", k=4)
                mm_group(bya, bya.ap, [(av[:, kc, :], oT[kc].ap) for kc in range(4)], [a_sl], oT)
                ws_done()
                tc_ = gP[2 * (dt % 2)]
                ta_ = gP[2 * (dt % 2) + 1]
                a_ = tmpf[0]
                b_ = tmpf[1]
                stt(a_.ap, tc_.ap, 1.0, byc.ap, ALU.add, ALU.mult, [tc_, byc], [a_])
                stt(b_.ap, ta_.ap, 1.0, bya.ap, ALU.add, ALU.mult, [ta_, bya], [b_])
                tt("gpsimd", mT[dt].ap, a_.ap, b_.ap, ALU.add, [a_, b_], [mT[dt]])
                if dt + 2 < 8:
                    gates_yc(dt + 2)
            for i in range(4):
                slot = ws_get()
                sv = slot.ap.rearrange("p (m k c) -> p m k c", m=2, k=8)
                for m in range(2):
                    dt2 = 2 * i + m
                    bk = bank[dt2 % 4]
                    mm_group(bk, bk.ap, [(sv[:, m, dt, :], mT[dt].ap) for dt in range(8)], [slot], mT)
                    stt(hT[dt2].ap, bk.ap, 0.5, hT[dt2].ap, ALU.mult, ALU.add, [bk, hT[dt2]], [hT[dt2]])
                    norm_sq(dt2, pool[8:16])
                    act(uT[dt2].ap, hT[dt2].ap, AF.Copy, [hT[dt2], cst], [uT[dt2]],
                        scale=cst.ap[:, C_G2 + dt2:C_G2 + dt2 + 1])
                ws_done()

        def final_stats():
            rstd_from_ss(sbk.ap[:, 0:4], sbk, 4, D, out=(rsfin.ap[:, 0:4], [rsfin]))

        fin_ps = ps_all[:, 3:8:4, :]

        def final_tb(t, tb):
            gfb = cst.ap[:, C_GFB:C_GFB + D].rearrange("p (k n) -> p k n", k=2)
            r = 4 * t + tb
            if t == ntiles - 1:
                bA, bB = bank[2 * tb], bank[2 * tb + 1]
                fps = ps_all[:, 2 * tb:2 * tb + 2, :]
            else:
                bA, bB = bank[3], bank[7]
                fps = fin_ps
            for dc in range(8):
                bk = bA if dc < 4 else bB
                S.op("tensor", lambda e, dc=dc, bk=bk, tb=tb: e.transpose(
                    bk.ap[:, (dc % 4) * 128:(dc % 4 + 1) * 128], hT[dc].ap[:, tb * 128:(tb + 1) * 128], identf_ap),
                    reads=[hT[dc], cst], writes=[bk], inc=(dc % 4 == 3))
            os_ = ostage[r % 2]
            stt(os_.ap.rearrange("p (k n) -> p k n", k=2), fps, rsfin.ap[:, tb:tb + 1], gfb, ALU.mult, ALU.mult,
                [bA, bB, rsfin, cst], [os_])
            S.dma("sync", o_d[r * 128:(r + 1) * 128, :], os_.ap, reads=[os_], dsem=os_.dsem)

        def ffn2_with_prefetch(t):
            n_ = t + 1
            if n_ >= ntiles:
                ffn(True, {("up", 2): r_matmuls}, True)
                return
            ffn(True, {
                ("up", 2): r_matmuls,
                ("up", 0): lambda: (x_stats_sq(n_, [0, 1]), x_stats_dma(n_, [2, 3])),
                ("up", 11): lambda: x_stats_sq(n_, [2, 3]),
                ("down", 0): lambda: x_stats_chain(n_),
                ("down", 2): lambda: input_prep(n_, 0),
            }, True)

        stages = [lambda t: ffn(True, None, False, True), lambda t: mix_norm_and_v(t), lambda t: mix_proj(t),
                  lambda t: attention(t), lambda t: merge_wo(), lambda t: norm_finish_deferred(),
                  lambda t: ffn2_with_prefetch(t), lambda t: final_stats()]
        for xs_ in xin + ostage:
            S._wait("gpsimd", xs_.dsem.key, xs_.dsem.sem, 16)
        for _ in range(NSLOT):
            ws_load()

        def input_stats_direct(tb):
            xs = xblk(0, tb)
            ss = new_stat()
            act(junk.ap, xs.ap, AF.Square, [xs], [junk, ss], accum_out=ss.ap[:, 0:1])
            rstd_from_ss(ss.ap[:, 0:1], ss, 1, D, out=(rsin.ap[:, tb:tb + 1], [rsin_cols[tb]]))

        for tb in range(4):
            input_stats_direct(tb)
            input_prep(0, tb)
            input_pe(0, tb)
        for t in range(ntiles):
            for si, stg_ in enumerate(stages):
                if si < stop_after:
                    stg_(t)
            for tb in range(4):
                if t + 1 < ntiles and tb >= 1:
                    input_prep(t + 1, tb)
                final_tb(t, tb)
                if t + 1 < ntiles:
                    input_pe(t + 1, tb)

        for os_ in ostage:
            S._wait("sync", os_.dsem.key, os_.dsem.sem, os_.dsem.cnt)
        assert stop_after < 99 or ws["g"] == total_chunks, (ws["g"], total_chunks)
        S.emit()
    return nc


def _pad(a):
    out = np.zeros((128, 2048), np.float32)
    out[:, :a.shape[1]] = a
    return out


def _fm_chunk(A, B):
    k = A.shape[0] // 128
    s = np.stack([A.reshape(k, 128, 128), B.reshape(k, 128, 128)])
    return np.ascontiguousarray(s.transpose(2, 0, 1, 3)).reshape(128, 2 * k * 128)


def _ffn_chunks(wg, wu, wd):
    ch = []
    for j in range(NFF):
        ch.append(_fm_chunk(wg[:, j * 128:(j + 1) * 128], wu[:, j * 128:(j + 1) * 128]))
    for dt in range(8):
        for half in range(2):
            blk = wd[half * 11 * 128:(half + 1) * 11 * 128, dt * 128:(dt + 1) * 128]
            a = np.ascontiguousarray(blk.reshape(11, 128, 128).transpose(1, 0, 2)).reshape(128, 1408)
            ch.append(_pad(a))
    return ch


def _swap_perm():
    p = []
    for s in range(2):
        p += list(range(s * 64 + 32, s * 64 + 64)) + list(range(s * 64, s * 64 + 32))
    return np.array(p)


def build_wstream(inp):
    w_in = inp["w_in"][0]
    ch = _ffn_chunks(inp["ffn1_gate"][0], inp["ffn1_up"][0], inp["ffn1_down"][0])
    wv = w_in[:, 2560:3072]
    for kq in range(2):
        a = wv[kq * 512:(kq + 1) * 512, :].reshape(4, 128, 512).transpose(1, 0, 2)
        ch.append(np.ascontiguousarray(a).reshape(128, 2048))
    tiles = []
    for c in range(4):
        tiles.append(w_in[:, 0 + c * 128:0 + (c + 1) * 128])
        tiles.append(w_in[:, 1024 + c * 128:1024 + (c + 1) * 128])
        tiles.append(w_in[:, 512 + c * 128:512 + (c + 1) * 128])
    for i in range(0, 12, 2):
        ch.append(_fm_chunk(tiles[i], tiles[i + 1]))
    for base in (1536, 2048):
        for h in range(0, 4, 2):
            ch.append(_fm_chunk(w_in[:, base + h * 128:base + (h + 1) * 128],
                                w_in[:, base + (h + 1) * 128:base + (h + 2) * 128]))
    wco = inp["w_conv_out"][0].reshape(4, 128, 8, 128)
    wao = inp["w_attn_out"][0].reshape(4, 128, 8, 128)
    def _g(dt):
        return _fm_chunk(w_in[:, 3072 + dt * 128:3072 + (dt + 1) * 128], w_in[:, 4096 + dt * 128:4096 + (dt + 1) * 128])

    def _ca(wmat, dt):
        return _pad(np.ascontiguousarray(wmat[:, :, dt, :].transpose(1, 0, 2)).reshape(128, 512))

    ch += [_g(0), _ca(wco, 0), _g(1), _ca(wco, 1)]
    for dt in range(8):
        ch.append(_ca(wao, dt))
        if dt + 2 < 8:
            ch += [_g(dt + 2), _ca(wco, dt + 2)]
    wo = inp["w_o"][0]
    for i in range(4):
        ch.append(_fm_chunk(wo[:, (2 * i) * 128:(2 * i + 1) * 128], wo[:, (2 * i + 1) * 128:(2 * i + 2) * 128]))
    ch += _ffn_chunks(inp["ffn2_gate"][0], inp["ffn2_up"][0], inp["ffn2_down"][0])
    assert len(ch) == NCH, (len(ch), NCH)
    return np.stack(ch).astype(np.float32)


def build_consts(inp):
    c = np.zeros((128, C_TOT), np.float32)
    c[:, C_GMIX:C_GMIX + 8] = inp["norm_mix"][0].reshape(8, 128).T
    c[:, C_G2:C_G2 + 8] = inp["norm_ffn2"][0].reshape(8, 128).T
    c[:, C_BG:C_BG + 16] = inp["b_gate"][0].reshape(16, 128).T
    cw = inp["conv_w"][0]
    c[:, C_CW:C_CW + 12] = cw.reshape(3, 4, 128).transpose(2, 1, 0).reshape(128, 12)
    lam = np.concatenate([inp["lambda_q1"][0], inp["lambda_k1"][0], inp["lambda_q2"][0], inp["lambda_k2"][0]])
    c[:, C_LAM:C_LAM + 256] = np.broadcast_to(lam[None, :], (128, 256))
    c[:, C_SUB:C_SUB + 128] = np.broadcast_to(inp["subln_g"][0][None, :], (128, 128))
    c[:, C_ID:C_ID + 128] = np.eye(128, dtype=np.float32)
    c[:, C_G1B:C_G1B + D] = np.broadcast_to(inp["norm_ffn1"][0][None, :], (128, D))
    c[:, C_GFB:C_GFB + D] = np.broadcast_to(inp["norm_final"][None, :], (128, D))
    return c


def build_rope():
    inv_freq = (1.0 / (np.float32(10000.0) ** (np.arange(0, 64, 2, dtype=np.float32) / np.float32(64)))).astype(np.float32)
    ang = (np.arange(SEQ, dtype=np.float32)[:, None] * inv_freq[None, :]).astype(np.float32)
    cos = np.cos(ang).astype(np.float32).T
    sin = np.sin(ang).astype(np.float32).T
    tab = np.zeros((2, 128, SEQ), np.float32)
    for p in range(128):
        d = p % 64
        tab[0, p] = cos[d % 32]
        tab[1, p] = -sin[d % 32] if d < 32 else sin[d % 32]
    return tab


_CACHE = {}


def kernel(**inputs):
    inp = {k: np.asarray(v) for k, v in inputs.items()}
    x = inp["x"]
    n = x.shape[0]
    wsrc = build_wstream(inp)
    consts = build_consts(inp)
    rope = build_rope()
    if "nc" not in _CACHE:
        _CACHE["nc"] = build_program(NT_FULL)
    nc = _CACHE["nc"]
    in_maps = [{"x": np.ascontiguousarray(x[i]), "wsrc": wsrc, "consts": consts, "rope": rope} for i in range(n)]
    res = run_bass_kernel_spmd(nc, in_maps, core_ids=list(range(n)))
    return np.stack([np.asarray(r["out"]) for r in res.results]).astype(np.float32)
```

```python
from contextlib import ExitStack

import numpy as np
import concourse.bass as bass
import concourse.mybir as mybir
from concourse.bass_utils import run_bass_kernel_spmd

F32 = mybir.dt.float32
BF16 = mybir.dt.bfloat16
I32 = mybir.dt.int32
AF = mybir.ActivationFunctionType
ALU = mybir.AluOpType

ENGS = ("tensor", "vector", "scalar", "gpsimd", "sync")

D = 1024
SEQ = 4096
FF = 2816
NFF = 22
TT = 512
NT_FULL = SEQ // TT
EPS = 1e-6
NSLOT = 5
NCONV = 3
LAMBDA_INIT = 0.8 - 0.6 * 1.0

C_GMIX, C_G2, C_BG, C_CW, C_LAM, C_SUB, C_ID, C_G1B, C_GFB, C_TOT = 0, 8, 16, 32, 48, 304, 432, 560, 1584, 2608


class DSem:
    def __init__(self, sem, key):
        self.sem = sem
        self.key = key
        self.cnt = 0


class Tl:
    def __init__(self, ap, name=""):
        self.ap = ap
        self.name = name
        self.w = {}
        self.r = {}
        self.dsem = None
        self.excl = False


class Sched:
    def __init__(self, nc, stack):
        self.nc = nc
        self.stack = stack
        self.ops = {e: [] for e in ENGS}
        self.sem = {e: stack.enter_context(nc.semaphore("s_" + e)) for e in ENGS}
        self.cnt = {e: 0 for e in ENGS}
        self.waited = {e: {} for e in ENGS}
        self.nds = 0

    def new_dsem(self):
        self.nds += 1
        s = self.stack.enter_context(self.nc.semaphore(f"d{self.nds}"))
        return DSem(s, f"d{self.nds}")

    def _need(self, eng, key, sem, val, acc):
        if self.waited[eng].get(key, 0) >= val:
            return
        if key == eng and val > self.cnt[eng]:
            return
        self.waited[eng][key] = val
        acc[key] = (sem, val)

    def _wait(self, eng, key, sem, val):
        acc = {}
        self._need(eng, key, sem, val, acc)
        for (sm, v) in acc.values():
            self.ops[eng].append(lambda e, sm=sm, v=v: e.wait_ge(sm, v))

    def _deps(self, eng, reads, writes):
        acc = {}
        for t in reads:
            for k, (s, v) in t.w.items():
                self._need(eng, k, s, v, acc)
            if t.excl:
                for k, (s, v) in t.r.items():
                    if k != eng:
                        self._need(eng, k, s, v, acc)
        for t in writes:
            for k, (s, v) in t.w.items():
                self._need(eng, k, s, v, acc)
            for k, (s, v) in t.r.items():
                self._need(eng, k, s, v, acc)
        return list(acc.values())

    def _emit_waits(self, eng, waits, attach):
        if attach and waits:
            for (sm, v) in waits[:-1]:
                self.ops[eng].append(lambda e, sm=sm, v=v: e.wait_ge(sm, v))
            return waits[-1]
        for (sm, v) in waits:
            self.ops[eng].append(lambda e, sm=sm, v=v: e.wait_ge(sm, v))
        return None

    def _mark(self, key, sem, val, reads, writes):
        for t in writes:
            t.w = {key: (sem, val)}
            t.r = {}
        for t in reads:
            old = t.r.get(key)
            if old is None or old[1] < val:
                t.r[key] = (sem, val)

    def op(self, eng, fn, reads=(), writes=(), inc=True):
        waits = self._deps(eng, reads, writes)
        last = self._emit_waits(eng, waits, True)
        sem = self.sem[eng]
        if inc:
            self.cnt[eng] += 1
            val = self.cnt[eng]
        else:
            val = self.cnt[eng] + 1

        def run(e, fn=fn, sem=sem, last=last, inc=inc):
            ins = fn(e)
            if last is not None:
                ins = ins._wait_ge(last[0], last[1])
            if inc:
                ins.then_inc(sem, 1)
        self.ops[eng].append(run)
        self._mark(eng, sem, val, reads, writes)

    def dma(self, q, out_ap, in_ap, reads=(), writes=(), dsem=None):
        waits = self._deps(q, reads, writes)
        last = self._emit_waits(q, waits, q == "sync")
        dsem.cnt += 16
        val = dsem.cnt
        s = dsem.sem

        def run(e, o=out_ap, i=in_ap, s=s, last=last):
            ins = e.dma_start(out=o, in_=i)
            if last is not None:
                ins = ins._wait_ge(last[0], last[1])
            ins.then_inc(s, 16)
        self.ops[q].append(run)
        self._mark(dsem.key, s, val, reads, writes)

    def emit(self):
        with self.nc.Block() as block:
            @block.tensor
            def _(e):
                for f in self.ops["tensor"]:
                    f(e)

            @block.vector
            def _(e):
                for f in self.ops["vector"]:
                    f(e)

            @block.scalar
            def _(e):
                for f in self.ops["scalar"]:
                    f(e)

            @block.gpsimd
            def _(e):
                for f in self.ops["gpsimd"]:
                    f(e)

            @block.sync
            def _(e):
                for f in self.ops["sync"]:
                    f(e)


def chunk_widths():
    w = []
    for _f in range(2):
        pass
    ffn = [2048] * NFF + [1408] * 16
    mix = [2048] * 6 + [2048] * 4 + [2048] * 2
    mrg = []
    mrg += [2048, 512, 2048, 512]
    for _dt in range(8):
        mrg += [512]
        if _dt + 2 < 8:
            mrg += [2048, 512]
    wo = [2048] * 4
    w = ffn + mix + mrg + wo + ffn
    return w


WIDTHS = chunk_widths()
NCH = len(WIDTHS)


def build_program(ntiles=NT_FULL, stop_after=99):
    nc = bass.Bass("TRN2", target_bir_lowering=False)
    x_d = nc.dram_tensor("x", [SEQ, D], F32, kind="ExternalInput").ap()
    w_d = nc.dram_tensor("wsrc", [NCH, 128, 2048], F32, kind="ExternalInput").ap()
    c_d = nc.dram_tensor("consts", [128, C_TOT], F32, kind="ExternalInput").ap()
    r_d = nc.dram_tensor("rope", [2, 128, SEQ], F32, kind="ExternalInput").ap()
    o_d = nc.dram_tensor("out", [SEQ, D], F32, kind="ExternalOutput").ap()
    scr_d = nc.dram_tensor("wscratch", [NCH, 128, 2048], BF16).ap()

    with ExitStack() as st:
        S = Sched(nc, st)

        def sb(name, shape, dt):
            return st.enter_context(nc.sbuf_tensor(name, shape, dt)).ap()

        pool_t = sb("pool", [128, 32, TT], BF16)
        pool = [Tl(pool_t[:, i, :], f"pool{i}") for i in range(32)]
        uT = pool[0:8]
        h1T = pool[8:30]
        cT = pool[8:12]
        QT = pool[12:16]
        oT = pool[16:20]
        mT = pool[20:28]
        gP = pool[28:32]
        hT_t = sb("hT", [128, 8, TT], F32)
        hT = [Tl(hT_t[:, i, :], f"hT{i}") for i in range(8)]
        KT_t = sb("KT", [128, 4, SEQ], BF16)
        KT = [[Tl(KT_t[:, h, t * TT:(t + 1) * TT], f"KT{h}_{t}") for t in range(NT_FULL)] for h in range(4)]
        V_t = sb("V", [128, 32, 4, 129], BF16)
        Vt = [Tl(V_t[:, 4 * t:4 * t + 4], f"V{t}") for t in range(NT_FULL)]
        ring = [Tl(sb(f"ring{i}", [128, 2048], BF16), f"ring{i}") for i in range(NSLOT)]
        xin = [Tl(sb(f"xin{i}", [128, D], F32), f"xin{i}") for i in range(2)]
        xn = Tl(sb("xn", [128, D], BF16), "xn")
        junk = Tl(sb("junk", [128, D], BF16), "junk")
        ostage = [Tl(sb(f"ost{i}", [128, D], F32), f"ost{i}") for i in range(2)]
        zbuf = [Tl(sb(f"zbuf{i}", [128, TT + 2], F32), f"zbuf{i}") for i in range(4)]
        tmpf = [Tl(sb(f"tmpf{i}", [128, TT], F32), f"tmpf{i}") for i in range(4)]
        tmpb = [Tl(sb(f"tmpb{i}", [128, TT], BF16), f"tmpb{i}") for i in range(2)]
        PT = [Tl(sb(f"PT{i}", [128, 2, TT], BF16), f"PT{i}") for i in range(3)]
        otok = [Tl(sb(f"otok{i}", [128, TT], BF16), f"otok{i}") for i in range(4)]
        o4 = Tl(sb("o4", [128, 4, 128], F32), "o4")
        ostgA = Tl(sb("ostgA", [128, 387], F32), "ostgA")
        ostgB = Tl(sb("ostgB", [128, 1, 387], F32), "ostgB")
        ostgC = Tl(sb("ostgC", [128, 258], F32), "ostgC")
        junk2 = Tl(sb("junk2", [128, 128], BF16), "junk2")
        ot2 = [Tl(sb(f"ot2_{i}", [128, 128], F32), f"ot2_{i}") for i in range(2)]
        dg4 = Tl(sb("dg4", [128, 4, 128], F32), "dg4")
        rsfin = Tl(sb("rsfin", [128, 8], F32), "rsfin")
        rsin = Tl(sb("rsin", [128, 8], F32), "rsin")
        ropeT = Tl(sb("ropeT", [128, 2, TT], F32), "ropeT")
        cst = Tl(sb("cst", [128, C_TOT], F32), "cst")
        identb = Tl(sb("identb", [128, 128], BF16), "identb")
        permb = Tl(sb("permb", [128, 128], BF16), "permb")
        ones_bf = Tl(sb("ones_bf", [128, 2], BF16), "ones_bf")
        ones_f = Tl(sb("ones_f", [128, 128], F32), "ones_f")
        gsub = Tl(sb("gsub", [128, 128], F32), "gsub")
        hb = Tl(sb("hb", [128, 16], F32), "hb")
        lam_t = Tl(sb("lam_t", [128, 8], F32), "lam_t")
        stat_t = sb("stat", [128, 64, 8], F32)
        stats = [Tl(stat_t[:, i, :], f"stat{i}") for i in range(64)]
        stat_i = [0]
        rq_a = Tl(sb("rq_a", [128, 8], F32), "rq_a")
        rq_i = Tl(sb("rq_i", [128, 8], F32), "rq_i")
        rq_y = Tl(sb("rq_y", [128, 8], F32), "rq_y")

        def new_stat():
            s_ = stats[stat_i[0] % 64]
            stat_i[0] += 1
            return s_

        ps_all = st.enter_context(nc.psum_tensor("ps", [128, 8, 512], F32)).ap()
        bank = [Tl(ps_all[:, i, :], f"bank{i}") for i in range(8)]
        for b_ in bank:
            b_.excl = True

        def bank_bf(i):
            return ps_all[:, i, :].bitcast(BF16)

        for t_ in ring + xin + ostage + [ropeT]:
            t_.dsem = S.new_dsem()
        for t_ in ring:
            t_.ssem = S.new_dsem()
        cs = S.new_dsem()
        scr = [Tl(scr_d[c], f"scr{c}") for c in range(NCH)]

        identf_ap = cst.ap[:, C_ID:C_ID + 128]

        def mm(out_ap, lhsT, rhs, start, stop, reads, writes, inc, **kw):
            S.op("tensor", lambda e: e.matmul(out_ap, lhsT, rhs, start=start, stop=stop, **kw),
                 reads=reads, writes=writes, inc=inc)

        def mm_group(out_tl, out_ap, pairs, common, per):
            n = len(pairs)
            for i, (l, r) in enumerate(pairs):
                mm(out_ap, l, r, i == 0, i == n - 1, list(common) + [per[i]], [out_tl], i == n - 1)

        def act(out_ap, in_ap, func, reads, writes, **kw):
            S.op("scalar", lambda e: e.activation(out=out_ap, in_=in_ap, func=func, **kw), reads=reads, writes=writes)

        def tt(eng, out_ap, in0, in1, op, reads, writes):
            S.op(eng, lambda e: e.tensor_tensor(out=out_ap, in0=in0, in1=in1, op=op), reads=reads, writes=writes)

        def ts(eng, out_ap, in0, s1, s2, op0, op1, reads, writes):
            if s2 is None:
                S.op(eng, lambda e: e.tensor_scalar(out=out_ap, in0=in0, scalar1=s1, scalar2=None, op0=op0),
                     reads=reads, writes=writes)
            else:
                S.op(eng, lambda e: e.tensor_scalar(out=out_ap, in0=in0, scalar1=s1, scalar2=s2, op0=op0, op1=op1),
                     reads=reads, writes=writes)

        def stt(out_ap, in0, scalar, in1, op0, op1, reads, writes):
            S.op("vector", lambda e: e.scalar_tensor_tensor(out=out_ap, in0=in0, scalar=scalar, in1=in1, op0=op0, op1=op1),
                 reads=reads, writes=writes)

        def rstd_from_ss(ss_ap, ss_tl, n, width, out=None):
            ts("vector", rq_a.ap[:, 0:n], ss_ap, 1.0 / width, EPS, ALU.mult, ALU.add, [ss_tl], [rq_a])
            ha = new_stat()
            ts("vector", ha.ap[:, 0:n], rq_a.ap[:, 0:n], -0.5, None, ALU.mult, None, [rq_a], [ha])
            S.op("vector", lambda e: e.tensor_scalar(out=rq_i.ap[:, 0:n].bitcast(I32), in0=rq_a.ap[:, 0:n].bitcast(I32),
                                                     scalar1=1, scalar2=None, op0=ALU.logical_shift_right),
                 reads=[rq_a], writes=[rq_i])
            y = rq_y
            S.op("vector", lambda e: e.tensor_scalar(out=rq_y.ap[:, 0:n].bitcast(I32), in0=rq_i.ap[:, 0:n].bitcast(I32),
                                                     scalar1=-1.0, scalar2=1597463007.0, op0=ALU.mult, op1=ALU.add),
                 reads=[rq_i], writes=[rq_y])
            y_ap = y.ap[:, 0:n]
            y_tl = [y]
            for it in range(3):
                u = new_stat()
                if it == 2 and out is not None:
                    y2_ap, y2_tl = out
                else:
                    y2 = new_stat()
                    y2_ap, y2_tl = y2.ap[:, 0:n], [y2]
                if n == 1:
                    stt(u.ap[:, 0:1], y_ap, ha.ap[:, 0:1], y_ap, ALU.mult, ALU.mult, y_tl + [ha], [u])
                else:
                    tt("vector", u.ap[:, 0:n], y_ap, y_ap, ALU.mult, y_tl, [u])
                    tt("vector", u.ap[:, 0:n], u.ap[:, 0:n], ha.ap[:, 0:n], ALU.mult, [u, ha], [u])
                stt(y2_ap, u.ap[:, 0:n], 1.5, y_ap, ALU.add, ALU.mult, [u] + y_tl, y2_tl)
                y_ap, y_tl = y2_ap, y2_tl
            return y_tl, y_ap

        total_chunks = NCH * ntiles
        ws = {"g": 0, "loaded": 0, "released": 0}

        def ws_load():
            g = ws["loaded"]
            if g >= total_chunks or g >= ws["released"] + NSLOT:
                return
            t_, c = divmod(g, NCH)
            slot = ring[g % NSLOT]
            wd = WIDTHS[c]
            ct = c % NCONV
            if t_ <= ct:
                S.dma("gpsimd", slot.ap[:, 0:wd], w_d[c][:, 0:wd], writes=[slot], dsem=slot.dsem)
                if t_ == ct and ntiles > ct + 1:
                    S.dma("sync", scr_d[c][:, 0:wd], slot.ap[:, 0:wd], reads=[slot], writes=[scr[c]], dsem=slot.ssem)
            else:
                S.dma("sync", slot.ap[:, 0:wd], scr_d[c][:, 0:wd], reads=[scr[c]], writes=[slot], dsem=slot.ssem)
            ws["loaded"] += 1

        def ws_get():
            g = ws["g"]
            assert ws["loaded"] > g
            ws["g"] += 1
            return ring[g % NSLOT]

        def ws_done(n=1):
            for _ in range(n):
                ws["released"] += 1
                assert ws["released"] <= ws["g"]
                ws_load()

        def x_load(r, q="sync"):
            slot = xin[r % 2]
            S.dma(q, slot.ap, x_d[r * 128:(r + 1) * 128, :], writes=[slot], dsem=slot.dsem)

        def xblk(t, tb):
            if t == 0 and tb >= 2:
                return ostage[tb % 2]
            return xin[(4 * t + tb) % 2]

        S.dma("sync", cst.ap, c_d, writes=[cst], dsem=cs)
        x_load(0, "sync")
        x_load(1, "sync")
        for tb_ in (2, 3):
            S.dma("sync", ostage[tb_ % 2].ap, x_d[tb_ * 128:(tb_ + 1) * 128, :], writes=[ostage[tb_ % 2]],
                  dsem=ostage[tb_ % 2].dsem)
        S.dma("sync", ropeT.ap, r_d.rearrange("a p t -> p a t")[:, :, 0:TT], writes=[ropeT], dsem=ropeT.dsem)
        PROLOGUE_XSTATS_MARK = None
        S.op("vector", lambda e: e.memset(ones_bf.ap, 1.0), writes=[ones_bf])
        S.op("vector", lambda e: e.memset(ones_f.ap, 1.0), writes=[ones_f])
        S.op("vector", lambda e: e.tensor_copy(out=identb.ap, in_=identf_ap), reads=[cst], writes=[identb])
        for blk in range(4):
            src = (blk ^ 1) * 32
            S.op("vector", lambda e, blk=blk, src=src: e.tensor_copy(
                out=permb.ap[:, blk * 32:(blk + 1) * 32], in_=identb.ap[:, src:src + 32]), reads=[identb], writes=[permb])
        for c in range(4):
            S.op("vector", lambda e, c=c: e.memset(zbuf[c].ap[:, 0:2], 0.0), writes=[zbuf[c]])
        for t in range(ntiles):
            S.op("gpsimd", lambda e, t=t: e.memset(Vt[t].ap[:, :, :, 128:129], 1.0), writes=[Vt[t]])
        ts("vector", gsub.ap, cst.ap[:, C_SUB:C_SUB + 128], 1.0 - LAMBDA_INIT, None, ALU.mult, None, [cst], [gsub])
        ts("vector", hb.ap, cst.ap[:, C_BG:C_BG + 16], 0.5, None, ALU.mult, None, [cst], [hb])
        lv = cst.ap[:, C_LAM:C_LAM + 256]
        for i in range(2):
            tt("vector", tmpf[0].ap[:, 0:64], lv[:, (2 * i) * 64:(2 * i + 1) * 64], lv[:, (2 * i + 1) * 64:(2 * i + 2) * 64],
               ALU.mult, [cst], [tmpf[0]])
            act(tmpf[1].ap[:, 0:64], tmpf[0].ap[:, 0:64], AF.Copy, [tmpf[0]], [tmpf[1], lam_t],
                accum_out=lam_t.ap[:, i:i + 1])
        act(lam_t.ap[:, 2:4], lam_t.ap[:, 0:2], AF.Exp, [lam_t], [lam_t])
        tt("vector", lam_t.ap[:, 4:5], lam_t.ap[:, 3:4], lam_t.ap[:, 2:3], ALU.subtract, [lam_t], [lam_t])
        ts("vector", lam_t.ap[:, 5:6], lam_t.ap[:, 4:5], -LAMBDA_INIT, None, ALU.add, None, [lam_t], [lam_t])
        neglam = lam_t.ap[:, 5:6]

        rsin_cols = [Tl(rsin.ap[:, i:i + 1], f"rsin{i}") for i in range(8)]

        ssx = Tl(sb("ssx", [128, 8], F32), "ssx")

        def x_stats_dma(t, tbs):
            for tb in tbs:
                r = 4 * t + tb
                slot = ostage[r % 2]
                S.dma("sync", slot.ap, x_d[r * 128:(r + 1) * 128, :], writes=[slot], dsem=slot.dsem)

        def x_stats_sq(t, tbs):
            for tb in tbs:
                r = 4 * t + tb
                slot = ostage[r % 2]
                act(junk.ap, slot.ap, AF.Square, [slot], [junk, ssx], accum_out=ssx.ap[:, tb:tb + 1])

        def x_stats_chain(t):
            base = (t % 2) * 4
            rstd_from_ss(ssx.ap[:, 0:4], ssx, 4, D, out=(rsin.ap[:, base:base + 4], rsin_cols[base:base + 4]))

        def x_stats_tile(t):
            x_stats_dma(t, [0, 1])
            x_stats_sq(t, [0, 1])
            x_stats_dma(t, [2, 3])
            x_stats_sq(t, [2, 3])
            x_stats_chain(t)

        def input_prep(t, tb):
            r = 4 * t + tb
            xs = xblk(t, tb)
            col = rsin_cols[(t % 2) * 4 + tb]
            stt(xn.ap, xs.ap, col.ap, cst.ap[:, C_G1B:C_G1B + D], ALU.mult, ALU.mult, [xs, col, cst], [xn])

        def input_pe(t, tb):
            r = 4 * t + tb
            xs = xblk(t, tb)
            pA, pB = (0, 1) if tb % 2 == 0 else (4, 5)
            qb = 2 if tb % 2 == 0 else 6
            for dc in range(8):
                bk = bank[pA] if dc < 4 else bank[pB]
                S.op("tensor", lambda e, dc=dc, bk=bk, xs=xs: e.transpose(
                    bk.ap[:, (dc % 4) * 128:(dc % 4 + 1) * 128], xs.ap[:, dc * 128:(dc + 1) * 128], identf_ap),
                    reads=[xs, cst], writes=[bk], inc=(dc % 4 == 3))
            for dc in range(8):
                S.op("tensor", lambda e, dc=dc, qb=qb: e.transpose(
                    bank_bf(qb)[:, dc * 128:(dc + 1) * 128], xn.ap[:, dc * 128:(dc + 1) * 128], identb.ap),
                    reads=[xn, identb], writes=[bank[qb]], inc=(dc == 7))
            act(hT_t[:, :, tb * 128:(tb + 1) * 128],
                ps_all[:, pA:pA + 2, :].rearrange("p k (a b) -> p (k a) b", a=4), AF.Copy,
                [bank[pA], bank[pB]], hT)
            act(pool_t[:, 0:8, tb * 128:(tb + 1) * 128], bank_bf(qb).rearrange("p (a b) -> p a b", a=8), AF.Copy,
                [bank[qb]], uT)
            if r + 2 < 4 * ntiles and r + 2 >= 4:
                x_load(r + 2, "scalar" if (r + 2) // 4 <= NCONV else "sync")

        sbk = bank[6]
        rbk = bank[7]
        dummy = Tl(stat_t[:, 63, :], "dummy")

        def norm_sq(dt, sq=None, defer_last=False):
            sq = sq or uT
            act(sq[dt].ap, hT[dt].ap, AF.Square, [hT[dt]], [sq[dt]])
            if dt >= 1:
                norm_mm(dt - 1, sq)
            if dt == 7 and not defer_last:
                norm_mm(7, sq)

        sq_alt = [pool[30 + (d_ % 2)] for d_ in range(8)]

        def mix_norm_and_v(t):
            s0 = ws_get()
            s1 = ws_get()
            v0 = s0.ap.rearrange("p (k c) -> p k c", k=4)
            v1 = s1.ap.rearrange("p (k c) -> p k c", k=4)

            def vgroup(tb):
                bk = bank[tb]
                mm_group(bk, bk.ap, [(uT[kc].ap[:, tb * 128:(tb + 1) * 128], (v0 if kc < 4 else v1)[:, kc % 4, :])
                                     for kc in range(8)], [s0, s1], uT)

            vgroup(0)
            norm_mm(7, sq_alt)
            vgroup(1)
            vgroup(2)
            rs_tl, rs = rstd_from_ss(sbk.ap[:, 0:4], sbk, 4, D)
            tt("vector", dg4.ap, identf_ap.unsqueeze(1).broadcast_to([128, 4, 128]),
               rs.unsqueeze(2).broadcast_to([128, 4, 128]), ALU.mult, [cst] + rs_tl, [dg4])
            vgroup(3)
            ws_done(2)
            for tb in range(4):
                mm(rbk.ap[:, tb * 128:(tb + 1) * 128], ones_f.ap, dg4.ap[:, tb, :], True, True, [ones_f, dg4], [rbk],
                   tb == 3)
            for tb in range(4):
                bk = bank[tb]
                act(V_t[:, 4 * t + tb, :, 0:128], bk.ap.rearrange("p (h e) -> p h e", h=4), AF.Copy,
                    [bk] + rs_tl, [Vt[t]], scale=rs[:, tb:tb + 1])
            for dc in range(8):
                stt(uT[dc].ap, hT[dc].ap, cst.ap[:, C_GMIX + dc:C_GMIX + dc + 1], rbk.ap, ALU.mult, ALU.mult,
                    [hT[dc], cst, rbk], [uT[dc]])

        def norm_mm(dt, sq):
            for tb in range(4):
                mm(sbk.ap[:, tb:tb + 1], sq[dt].ap[:, tb * 128:(tb + 1) * 128], ones_bf.ap[:, 0:1],
                   dt == 0 and tb == 0, dt == 7, [sq[dt], ones_bf], [sbk], tb == 3, skip_group_check=True)

        Rsb = tmpf[0]

        def norm_finish_deferred():
            rs_tl, rs = rstd_from_ss(sbk.ap[:, 0:4], sbk, 4, D)
            tt("vector", dg4.ap, identf_ap.unsqueeze(1).broadcast_to([128, 4, 128]),
               rs.unsqueeze(2).broadcast_to([128, 4, 128]), ALU.mult, [cst] + rs_tl, [dg4])

        def r_matmuls():
            for tb in range(4):
                mm(rbk.ap[:, tb * 128:(tb + 1) * 128], ones_f.ap, dg4.ap[:, tb, :], True, True, [ones_f, dg4], [rbk],
                   tb == 3)
            act(Rsb.ap, rbk.ap, AF.Copy, [rbk], [Rsb])

        def norm_finish(gcol0):
            rs_tl, rs = rstd_from_ss(sbk.ap[:, 0:4], sbk, 4, D)
            tt("vector", dg4.ap, identf_ap.unsqueeze(1).broadcast_to([128, 4, 128]),
               rs.unsqueeze(2).broadcast_to([128, 4, 128]), ALU.mult, [cst] + rs_tl, [dg4])
            for tb in range(4):
                mm(rbk.ap[:, tb * 128:(tb + 1) * 128], ones_f.ap, dg4.ap[:, tb, :], True, True, [ones_f, dg4], [rbk],
                   tb == 3)
            for dc in range(8):
                stt(uT[dc].ap, hT[dc].ap, cst.ap[:, gcol0 + dc:gcol0 + dc + 1], rbk.ap, ALU.mult, ALU.mult,
                    [hT[dc], cst, rbk], [uT[dc]])

        usb = [gP[2], gP[3]]

        def ffn(norm_after, hooks=None, post_scale=False, mix_defer=False):
            hooks = hooks or {}
            pending = []
            for j in range(NFF):
                if ("up", j) in hooks:
                    hooks[("up", j)]()
                if j == 2:
                    for ev in pending:
                        ev()
                    pending = []
                slot = ws_get()
                sv = slot.ap.rearrange("p (m k c) -> p m k c", m=2, k=8)
                bg_, bu_ = (bank[0], bank[1]) if j % 2 == 0 else (bank[2], bank[3])
                mm_group(bg_, bg_.ap, [(sv[:, 0, kc, :], uT[kc].ap) for kc in range(8)], [slot], uT)
                mm_group(bu_, bu_.ap, [(sv[:, 1, kc, :], uT[kc].ap) for kc in range(8)], [slot], uT)
                ws_done()
                sl = tmpb[j % 2]
                if post_scale:
                    def evac(j=j, bg_=bg_, bu_=bu_, sl=sl):
                        gs = tmpf[1 + j % 2]
                        tt("vector", gs.ap, bg_.ap, Rsb.ap, ALU.mult, [bg_, Rsb], [gs])
                        act(sl.ap, gs.ap, AF.Silu, [gs], [sl])
                        us = usb[j % 2]
                        tt("vector", us.ap, bu_.ap, Rsb.ap, ALU.mult, [bu_, Rsb], [us])
                        tt("vector", h1T[j].ap, sl.ap, us.ap, ALU.mult, [sl, us], [h1T[j]])
                    if j < 2:
                        pending.append(evac)
                    else:
                        evac()
                else:
                    act(sl.ap, bg_.ap, AF.Silu, [bg_], [sl])
                    tt("vector", h1T[j].ap, sl.ap, bu_.ap, ALU.mult, [sl, bu_], [h1T[j]])
            for dt in range(8):
                if ("down", dt) in hooks:
                    hooks[("down", dt)]()
                bk = bank[4 + dt % 2]
                for half in range(2):
                    slot = ws_get()
                    sv = slot.ap[:, 0:1408].rearrange("p (j c) -> p j c", j=11)
                    for jj in range(11):
                        j = half * 11 + jj
                        mm(bk.ap, sv[:, jj, :], h1T[j].ap, j == 0, j == NFF - 1, [slot, h1T[j]], [bk],
                           jj == 10)
                    ws_done()
                stt(hT[dt].ap, bk.ap, 0.5, hT[dt].ap, ALU.mult, ALU.add, [bk, hT[dt]], [hT[dt]])
                if norm_after and mix_defer:
                    act(uT[dt].ap, hT[dt].ap, AF.Copy, [hT[dt], cst], [uT[dt]],
                        scale=cst.ap[:, C_GMIX + dt:C_GMIX + dt + 1])
                    norm_sq(dt, sq_alt, defer_last=True)
                elif norm_after:
                    norm_sq(dt)

        fm_state = {"slot": None, "m": 0}

        def fm_next():
            if fm_state["slot"] is None:
                fm_state["slot"] = ws_get()
                fm_state["m"] = 0
            slot = fm_state["slot"]
            m = fm_state["m"]
            v = slot.ap.rearrange("p (m k c) -> p m k c", m=2, k=8)[:, m]
            return slot, v

        def fm_adv():
            fm_state["m"] += 1
            if fm_state["m"] == 2:
                fm_state["slot"] = None
                ws_done()

        def fm_proj(bk):
            slot, v = fm_next()
            mm_group(bk, bk.ap, [(v[:, kc, :], uT[kc].ap) for kc in range(8)], [slot], uT)
            fm_adv()

        def mix_proj(t):
            cw = cst.ap[:, C_CW:C_CW + 12]
            for c in range(4):
                bA, bB, bC = (bank[0], bank[1], bank[2]) if c % 2 == 0 else (bank[4], bank[5], bank[6])
                fm_proj(bA)
                fm_proj(bB)
                fm_proj(bC)
                xcs = tmpf[0]
                act(xcs.ap, bA.ap, AF.Copy, [bA], [xcs])
                z = zbuf[c]
                if t > 0:
                    S.op("vector", lambda e, z=z: e.tensor_copy(out=z.ap[:, 0:2], in_=z.ap[:, TT:TT + 2]),
                         reads=[z], writes=[z])
                tt("vector", z.ap[:, 2:TT + 2], bB.ap, xcs.ap, ALU.mult, [bB, xcs], [z])
                acc = tmpf[1]
                ts("vector", acc.ap, z.ap[:, 0:TT], cw[:, 3 * c:3 * c + 1], None, ALU.mult, None, [z, cst], [acc])
                stt(acc.ap, z.ap[:, 1:TT + 1], cw[:, 3 * c + 1:3 * c + 2], acc.ap, ALU.mult, ALU.add, [z, cst, acc], [acc])
                stt(acc.ap, z.ap[:, 2:TT + 2], cw[:, 3 * c + 2:3 * c + 3], acc.ap, ALU.mult, ALU.add, [z, cst, acc], [acc])
                tt("vector", cT[c].ap, acc.ap, bC.ap, ALU.mult, [acc, bC], [cT[c]])
            items = [(which, h) for which in range(2) for h in range(4)]

            T1 = [tmpf[2], tmpf[1]]
            T2 = [tmpf[3], tmpf[0]]

            def rope_tail(i):
                which, h = items[i]
                bB = bank[2 * (i % 4) + 1]
                qb = tmpb[i % 2]
                mm(bB.ap, permb.ap, qb.ap, True, True, [permb, qb], [bB], True)
                t1 = T1[i % 2]
                t2 = T2[i % 2]
                tt("vector", t1.ap, bB.ap, ropeT.ap[:, 1, :], ALU.mult, [bB, ropeT], [t1])
                dst = QT[h] if which == 0 else KT[h][t]
                tt("gpsimd", dst.ap, t1.ap, t2.ap, ALU.add, [t1, t2], [dst])

            for i in range(8):
                bA = bank[2 * (i % 4)]
                fm_proj(bA)
                act(tmpb[i % 2].ap, bA.ap, AF.Copy, [bA], [tmpb[i % 2]])
                tt("vector", T2[i % 2].ap, bA.ap, ropeT.ap[:, 0, :], ALU.mult, [bA, ropeT], [T2[i % 2]])
                if i >= 1:
                    rope_tail(i - 1)
            rope_tail(7)
            if t + 1 < ntiles:
                S.dma("sync", ropeT.ap, r_d.rearrange("a p t -> p a t")[:, :, (t + 1) * TT:(t + 2) * TT],
                      writes=[ropeT], dsem=ropeT.dsem)

        att_state = {"t": 0}

        def attention(t):
            att_state["t"] = t
            if t + 1 < ntiles:
                x_stats_dma(t + 1, [0, 1])
            steps = [(h, kb) for h in range(4) for kb in range(4 * t + 4)]
            nkb = 4 * t + 4

            def acc_ap(s, j):
                a = s * 4 + j
                b = bank[4 + a // 3]
                off = (a % 3) * 129
                return b, b.ap[:, off:off + 129]

            def emit_S(idx):
                h, kb = steps[idx]
                jmin = max(0, kb - 4 * t)
                ncol = (4 - jmin) * 128
                tk, kl = divmod(kb, 4)
                pi = idx % 2
                kt = KT[h][tk]
                for s in range(2):
                    bk = bank[2 * pi + s]
                    mm(bk.ap[:, 0:ncol], kt.ap[64 * s:64 * s + 64, kl * 128:(kl + 1) * 128],
                       QT[h].ap[64 * s:64 * s + 64, jmin * 128:TT], True, True, [kt, QT[h]], [bk], s == 1)

            st_ = {"started": set()}

            def emit_exp(idx):
                h, kb = steps[idx]
                jmin = max(0, kb - 4 * t)
                ncol = (4 - jmin) * 128
                pi = idx % 2
                pt = PT[idx % 3]
                act(pt.ap[:, :, 0:ncol], ps_all[:, 2 * pi:2 * pi + 2, 0:ncol], AF.Exp,
                    [bank[2 * pi], bank[2 * pi + 1]], [pt], scale=0.125)
                if kb >= 4 * t:
                    S.op("gpsimd", lambda e, pt=pt: e.memset(pt.ap[64:128, :, 0:64], 0.0), writes=[pt])

            def emit_AV(idx):
                h, kb = steps[idx]
                if kb == 0:
                    st_["started"] = set()
                started = st_["started"]
                jmin = max(0, kb - 4 * t)
                tk, kl = divmod(kb, 4)
                pt = PT[idx % 3]
                vt = Vt[tk]
                for s in range(2):
                    for j in range(jmin, 4):
                        b, aap = acc_ap(s, j)
                        first = b.name not in started
                        started.add(b.name)
                        last_for_j = (kb == 4 * t + j)
                        mm(aap, pt.ap[:, s, (j - jmin) * 128:(j - jmin + 1) * 128], V_t[:, kb, h, :],
                           first, last_for_j, [pt, vt], [b], (s == 1 and j == 3), skip_group_check=True)
                if kb == nkb - 1:
                    finalize_head(h)

            emit_S(0)
            for idx in range(len(steps)):
                if idx + 1 < len(steps):
                    emit_S(idx + 1)
                emit_exp(idx)
                if idx >= 1:
                    emit_AV(idx - 1)
            emit_AV(len(steps) - 1)

        def otok_transposes():
            for h in range(4):
                for j in range(4):
                    S.op("tensor", lambda e, h=h, j=j: e.transpose(
                        bank_bf(1)[:, j * 128:(j + 1) * 128], otok[j].ap[:, h * 128:(h + 1) * 128], identb.ap),
                        reads=[otok[j], identb], writes=[bank[1]], inc=(j == 3))
                S.op("vector", lambda e, h=h: e.tensor_copy(out=oT[h].ap, in_=bank_bf(1)[:, 0:TT]),
                     reads=[bank[1]], writes=[oT[h]])

        def finalize_head(h):
            if att_state["t"] <= 1:
                act(ostgA.ap, bank[4].ap[:, 0:387], AF.Copy, [bank[4]], [ostgA])
                act(ostgB.ap[:, 0, :], bank[5].ap[:, 0:387], AF.Copy, [bank[5]], [ostgB])
                act(ostgC.ap[:, 0:258], bank[6].ap[:, 0:258], AF.Copy, [bank[6]], [ostgC])
            else:
                S.op("vector", lambda e: e.tensor_copy(out=ostgA.ap, in_=bank[4].ap[:, 0:387]),
                     reads=[bank[4]], writes=[ostgA])
                S.op("vector", lambda e: e.tensor_copy(out=ostgB.ap[:, 0, :], in_=bank[5].ap[:, 0:387]),
                     reads=[bank[5]], writes=[ostgB])
                S.op("vector", lambda e: e.tensor_copy(out=ostgC.ap[:, 0:258], in_=bank[6].ap[:, 0:258]),
                     reads=[bank[6]], writes=[ostgC])
            stg = [ostgA, ostgB, ostgC]
            sap = [ostgA.ap, ostgB.ap[:, 0, :], ostgC.ap]

            def accs(a):
                return stg[a // 3], sap[a // 3][:, (a % 3) * 129:(a % 3) * 129 + 128]

            ls = new_stat()
            for bi in range(3):
                n = 3 if bi < 2 else 2
                S.op("vector", lambda e, bi=bi, n=n, ls=ls: e.tensor_copy(
                    out=ls.ap[:, 3 * bi:3 * bi + n],
                    in_=sap[bi][:, 0:129 * n].rearrange("p (a c) -> p a c", a=n)[:, :, 128]),
                    reads=[stg[bi]], writes=[ls])
            rl = new_stat()
            S.op("vector", lambda e: e.reciprocal(out=rl.ap, in_=ls.ap), reads=[ls], writes=[rl])
            rn = new_stat()
            ts("vector", rn.ap[:, 0:4], rl.ap[:, 4:8], neglam, None, ALU.mult, None, [rl, lam_t], [rn])
            sso = new_stat()
            for j in range(4):
                t0_, a0 = accs(j)
                t1_, a1 = accs(4 + j)
                tq = ot2[j % 2]
                ts("vector", tq.ap, a1, rn.ap[:, j:j + 1], None, ALU.mult, None, [t1_, rn], [tq])
                stt(o4.ap[:, j, :], a0, rl.ap[:, j:j + 1], tq.ap, ALU.mult, ALU.add, [t0_, rl, tq], [o4])
            for j in range(4):
                S.op("vector", lambda e, j=j, sso=sso: e.scalar_tensor_tensor(
                    out=junk2.ap, in0=o4.ap[:, j, :], scalar=1.0, in1=o4.ap[:, j, :], op0=ALU.mult, op1=ALU.mult,
                    accum_out=sso.ap[:, j:j + 1]), reads=[o4], writes=[junk2, sso])
            rs_tl, rs = rstd_from_ss(sso.ap[:, 0:4], sso, 4, 128)
            for j in range(4):
                stt(otok[j].ap[:, h * 128:(h + 1) * 128], o4.ap[:, j, :], rs[:, j:j + 1], gsub.ap, ALU.mult, ALU.mult,
                    [o4, gsub] + rs_tl, [otok[j]])

        def gates_yc(dt):
            bgc, bga = (bank[4], bank[5]) if dt % 2 == 0 else (bank[6], bank[7])
            byc = bank[0] if dt % 2 == 0 else bank[2]
            fm_proj(bgc)
            fm_proj(bga)
            tc_ = gP[2 * (dt % 2)]
            ta_ = gP[2 * (dt % 2) + 1]
            act(tc_.ap, bgc.ap, AF.Tanh, [bgc, hb], [tc_], scale=0.5, bias=hb.ap[:, dt:dt + 1])
            act(ta_.ap, bga.ap, AF.Tanh, [bga, hb], [ta_], scale=0.5, bias=hb.ap[:, 8 + dt:9 + dt])
            c_ = ws_get()
            cv = c_.ap[:, 0:512].rearrange("p (k c) -> p k c", k=4)
            mm_group(byc, byc.ap, [(cv[:, kc, :], cT[kc].ap) for kc in range(4)], [c_], cT)
            ws_done()

        def merge_wo():
            gates_yc(0)
            gates_yc(1)
            otok_transposes()
            for dt in range(8):
                byc, bya = (bank[0], bank[1]) if dt % 2 == 0 else (bank[2], bank[3])
                a_sl = ws_get()
                av = a_sl.ap[:, 0:512].rearrange("p (k c) -> p k c", k=4)
                mm_group(bya, bya.ap, [(av[:, kc, :], oT[kc].ap) for kc in range(4)], [a_sl], oT)
                ws_done()
                tc_ = gP[2 * (dt % 2)]
                ta_ = gP[2 * (dt % 2) + 1]
                a_ = tmpf[0]
                b_ = tmpf[1]
                stt(a_.ap, tc_.ap, 1.0, byc.ap, ALU.add, ALU.mult, [tc_, byc], [a_])
                stt(b_.ap, ta_.ap, 1.0, bya.ap, ALU.add, ALU.mult, [ta_, bya], [b_])
                tt("gpsimd", mT[dt].ap, a_.ap, b_.ap, ALU.add, [a_, b_], [mT[dt]])
                if dt + 2 < 8:
                    gates_yc(dt + 2)
            for i in range(4):
                slot = ws_get()
                sv = slot.ap.rearrange("p (m k c) -> p m k c", m=2, k=8)
                for m in range(2):
                    dt2 = 2 * i + m
                    bk = bank[dt2 % 4]
                    mm_group(bk, bk.ap, [(sv[:, m, dt, :], mT[dt].ap) for dt in range(8)], [slot], mT)
                    stt(hT[dt2].ap, bk.ap, 0.5, hT[dt2].ap, ALU.mult, ALU.add, [bk, hT[dt2]], [hT[dt2]])
                    norm_sq(dt2, pool[8:16])
                    act(uT[dt2].ap, hT[dt2].ap, AF.Copy, [hT[dt2], cst], [uT[dt2]],
                        scale=cst.ap[:, C_G2 + dt2:C_G2 + dt2 + 1])
                ws_done()

        def final_stats():
            rstd_from_ss(sbk.ap[:, 0:4], sbk, 4, D, out=(rsfin.ap[:, 0:4], [rsfin]))

        fin_ps = ps_all[:, 3:8:4, :]

        def final_tb(t, tb):
            gfb = cst.ap[:, C_GFB:C_GFB + D].rearrange("p (k n) -> p k n", k=2)
            r = 4 * t + tb
            if t == ntiles - 1:
                bA, bB = bank[2 * tb], bank[2 * tb + 1]
                fps = ps_all[:, 2 * tb:2 * tb + 2, :]
            else:
                bA, bB = bank[3], bank[7]
                fps = fin_ps
            for dc in range(8):
                bk = bA if dc < 4 else bB
                S.op("tensor", lambda e, dc=dc, bk=bk, tb=tb: e.transpose(
                    bk.ap[:, (dc % 4) * 128:(dc % 4 + 1) * 128], hT[dc].ap[:, tb * 128:(tb + 1) * 128], identf_ap),
                    reads=[hT[dc], cst], writes=[bk], inc=(dc % 4 == 3))
            os_ = ostage[r % 2]
            stt(os_.ap.rearrange("p (k n) -> p k n", k=2), fps, rsfin.ap[:, tb:tb + 1], gfb, ALU.mult, ALU.mult,
                [bA, bB, rsfin, cst], [os_])
            S.dma("sync", o_d[r * 128:(r + 1) * 128, :], os_.ap, reads=[os_], dsem=os_.dsem)

        def ffn2_with_prefetch(t):
            n_ = t + 1
            if n_ >= ntiles:
                ffn(True, {("up", 2): r_matmuls}, True)
                return
            ffn(True, {
                ("up", 2): r_matmuls,
                ("up", 0): lambda: (x_stats_sq(n_, [0, 1]), x_stats_dma(n_, [2, 3])),
                ("up", 11): lambda: x_stats_sq(n_, [2, 3]),
                ("down", 0): lambda: x_stats_chain(n_),
                ("down", 2): lambda: input_prep(n_, 0),
            }, True)

        stages = [lambda t: ffn(True, None, False, True), lambda t: mix_norm_and_v(t), lambda t: mix_proj(t),
                  lambda t: attention(t), lambda t: merge_wo(), lambda t: norm_finish_deferred(),
                  lambda t: ffn2_with_prefetch(t), lambda t: final_stats()]
        for xs_ in xin + ostage:
            S._wait("gpsimd", xs_.dsem.key, xs_.dsem.sem, 16)
        for _ in range(NSLOT):
            ws_load()

        def input_stats_direct(tb):
            xs = xblk(0, tb)
            ss = new_stat()
            act(junk.ap, xs.ap, AF.Square, [xs], [junk, ss], accum_out=ss.ap[:, 0:1])
            rstd_from_ss(ss.ap[:, 0:1], ss, 1, D, out=(rsin.ap[:, tb:tb + 1], [rsin_cols[tb]]))

        for tb in range(4):
            input_stats_direct(tb)
            input_prep(0, tb)
            input_pe(0, tb)
        for t in range(ntiles):
            for si, stg_ in enumerate(stages):
                if si < stop_after:
                    stg_(t)
            for tb in range(4):
                if t + 1 < ntiles and tb >= 1:
                    input_prep(t + 1, tb)
                final_tb(t, tb)
                if t + 1 < ntiles:
                    input_pe(t + 1, tb)

        for os_ in ostage:
            S._wait("sync", os_.dsem.key, os_.dsem.sem, os_.dsem.cnt)
        assert stop_after < 99 or ws["g"] == total_chunks, (ws["g"], total_chunks)
        S.emit()
    return nc


def _pad(a):
    out = np.zeros((128, 2048), np.float32)
    out[:, :a.shape[1]] = a
    return out


def _fm_chunk(A, B):
    k = A.shape[0] // 128
    s = np.stack([A.reshape(k, 128, 128), B.reshape(k, 128, 128)])
    return np.ascontiguousarray(s.transpose(2, 0, 1, 3)).reshape(128, 2 * k * 128)


def _ffn_chunks(wg, wu, wd):
    ch = []
    for j in range(NFF):
        ch.append(_fm_chunk(wg[:, j * 128:(j + 1) * 128], wu[:, j * 128:(j + 1) * 128]))
    for dt in range(8):
        for half in range(2):
            blk = wd[half * 11 * 128:(half + 1) * 11 * 128, dt * 128:(dt + 1) * 128]
            a = np.ascontiguousarray(blk.reshape(11, 128, 128).transpose(1, 0, 2)).reshape(128, 1408)
            ch.append(_pad(a))
    return ch


def _swap_perm():
    p = []
    for s in range(2):
        p += list(range(s * 64 + 32, s * 64 + 64)) + list(range(s * 64, s * 64 + 32))
    return np.array(p)


def build_wstream(inp):
    w_in = inp["w_in"][0]
    ch = _ffn_chunks(inp["ffn1_gate"][0], inp["ffn1_up"][0], inp["ffn1_down"][0])
    wv = w_in[:, 2560:3072]
    for kq in range(2):
        a = wv[kq * 512:(kq + 1) * 512, :].reshape(4, 128, 512).transpose(1, 0, 2)
        ch.append(np.ascontiguousarray(a).reshape(128, 2048))
    tiles = []
    for c in range(4):
        tiles.append(w_in[:, 0 + c * 128:0 + (c + 1) * 128])
        tiles.append(w_in[:, 1024 + c * 128:1024 + (c + 1) * 128])
        tiles.append(w_in[:, 512 + c * 128:512 + (c + 1) * 128])
    for i in range(0, 12, 2):
        ch.append(_fm_chunk(tiles[i], tiles[i + 1]))
    for base in (1536, 2048):
        for h in range(0, 4, 2):
            ch.append(_fm_chunk(w_in[:, base + h * 128:base + (h + 1) * 128],
                                w_in[:, base + (h + 1) * 128:base + (h + 2) * 128]))
    wco = inp["w_conv_out"][0].reshape(4, 128, 8, 128)
    wao = inp["w_attn_out"][0].reshape(4, 128, 8, 128)
    def _g(dt):
        return _fm_chunk(w_in[:, 3072 + dt * 128:3072 + (dt + 1) * 128], w_in[:, 4096 + dt * 128:4096 + (dt + 1) * 128])

    def _ca(wmat, dt):
        return _pad(np.ascontiguousarray(wmat[:, :, dt, :].transpose(1, 0, 2)).reshape(128, 512))

    ch += [_g(0), _ca(wco, 0), _g(1), _ca(wco, 1)]
    for dt in range(8):
        ch.append(_ca(wao, dt))
        if dt + 2 < 8:
            ch += [_g(dt + 2), _ca(wco, dt + 2)]
    wo = inp["w_o"][0]
    for i in range(4):
        ch.append(_fm_chunk(wo[:, (2 * i) * 128:(2 * i + 1) * 128], wo[:, (2 * i + 1) * 128:(2 * i + 2) * 128]))
    ch += _ffn_chunks(inp["ffn2_gate"][0], inp["ffn2_up"][0], inp["ffn2_down"][0])
    assert len(ch) == NCH, (len(ch), NCH)
    return np.stack(ch).astype(np.float32)


def build_consts(inp):
    c = np.zeros((128, C_TOT), np.float32)
    c[:, C_GMIX:C_GMIX + 8] = inp["norm_mix"][0].reshape(8, 128).T
    c[:, C_G2:C_G2 + 8] = inp["norm_ffn2"][0].reshape(8, 128).T
    c[:, C_BG:C_BG + 16] = inp["b_gate"][0].reshape(16, 128).T
    cw = inp["conv_w"][0]
    c[:, C_CW:C_CW + 12] = cw.reshape(3, 4, 128).transpose(2, 1, 0).reshape(128, 12)
    lam = np.concatenate([inp["lambda_q1"][0], inp["lambda_k1"][0], inp["lambda_q2"][0], inp["lambda_k2"][0]])
    c[:, C_LAM:C_LAM + 256] = np.broadcast_to(lam[None, :], (128, 256))
    c[:, C_SUB:C_SUB + 128] = np.broadcast_to(inp["subln_g"][0][None, :], (128, 128))
    c[:, C_ID:C_ID + 128] = np.eye(128, dtype=np.float32)
    c[:, C_G1B:C_G1B + D] = np.broadcast_to(inp["norm_ffn1"][0][None, :], (128, D))
    c[:, C_GFB:C_GFB + D] = np.broadcast_to(inp["norm_final"][None, :], (128, D))
    return c


def build_rope():
    inv_freq = (1.0 / (np.float32(10000.0) ** (np.arange(0, 64, 2, dtype=np.float32) / np.float32(64)))).astype(np.float32)
    ang = (np.arange(SEQ, dtype=np.float32)[:, None] * inv_freq[None, :]).astype(np.float32)
    cos = np.cos(ang).astype(np.float32).T
    sin = np.sin(ang).astype(np.float32).T
    tab = np.zeros((2, 128, SEQ), np.float32)
    for p in range(128):
        d = p % 64
        tab[0, p] = cos[d % 32]
        tab[1, p] = -sin[d % 32] if d < 32 else sin[d % 32]
    return tab


_CACHE = {}


def kernel(**inputs):
    inp = {k: np.asarray(v) for k, v in inputs.items()}
    x = inp["x"]
    n = x.shape[0]
    wsrc = build_wstream(inp)
    consts = build_consts(inp)
    rope = build_rope()
    if "nc" not in _CACHE:
        _CACHE["nc"] = build_program(NT_FULL)
    nc = _CACHE["nc"]
    in_maps = [{"x": np.ascontiguousarray(x[i]), "wsrc": wsrc, "consts": consts, "rope": rope} for i in range(n)]
    res = run_bass_kernel_spmd(nc, in_maps, core_ids=list(range(n)))
    return np.stack([np.asarray(r["out"]) for r in res.results]).astype(np.float32)
```

```python
from contextlib import ExitStack

import numpy as np
import concourse.bass as bass
import concourse.mybir as mybir
from concourse.bass_utils import run_bass_kernel_spmd

F32 = mybir.dt.float32
BF16 = mybir.dt.bfloat16
I32 = mybir.dt.int32
AF = mybir.ActivationFunctionType
ALU = mybir.AluOpType

ENGS = ("tensor", "vector", "scalar", "gpsimd", "sync")

D = 1024
SEQ = 4096
FF = 2816
NFF = 22
TT = 512
NT_FULL = SEQ // TT
EPS = 1e-6
NSLOT = 5
NCONV = 3
LAMBDA_INIT = 0.8 - 0.6 * 1.0

C_GMIX, C_G2, C_BG, C_CW, C_LAM, C_SUB, C_ID, C_G1B, C_GFB, C_TOT = 0, 8, 16, 32, 48, 304, 432, 560, 1584, 2608


class DSem:
    def __init__(self, sem, key):
        self.sem = sem
        self.key = key
        self.cnt = 0


class Tl:
    def __init__(self, ap, name=""):
        self.ap = ap
        self.name = name
        self.w = {}
        self.r = {}
        self.dsem = None
        self.excl = False


class Sched:
    def __init__(self, nc, stack):
        self.nc = nc
        self.stack = stack
        self.ops = {e: [] for e in ENGS}
        self.sem = {e: stack.enter_context(nc.semaphore("s_" + e)) for e in ENGS}
        self.cnt = {e: 0 for e in ENGS}
        self.waited = {e: {} for e in ENGS}
        self.nds = 0

    def new_dsem(self):
        self.nds += 1
        s = self.stack.enter_context(self.nc.semaphore(f"d{self.nds}"))
        return DSem(s, f"d{self.nds}")

    def _need(self, eng, key, sem, val, acc):
        if self.waited[eng].get(key, 0) >= val:
            return
        if key == eng and val > self.cnt[eng]:
            return
        self.waited[eng][key] = val
        acc[key] = (sem, val)

    def _wait(self, eng, key, sem, val):
        acc = {}
        self._need(eng, key, sem, val, acc)
        for (sm, v) in acc.values():
            self.ops[eng].append(lambda e, sm=sm, v=v: e.wait_ge(sm, v))

    def _deps(self, eng, reads, writes):
        acc = {}
        for t in reads:
            for k, (s, v) in t.w.items():
                self._need(eng, k, s, v, acc)
            if t.excl:
                for k, (s, v) in t.r.items():
                    if k != eng:
                        self._need(eng, k, s, v, acc)
        for t in writes:
            for k, (s, v) in t.w.items():
                self._need(eng, k, s, v, acc)
            for k, (s, v) in t.r.items():
                self._need(eng, k, s, v, acc)
        return list(acc.values())

    def _emit_waits(self, eng, waits, attach):
        if attach and waits:
            for (sm, v) in waits[:-1]:
                self.ops[eng].append(lambda e, sm=sm, v=v: e.wait_ge(sm, v))
            return waits[-1]
        for (sm, v) in waits:
            self.ops[eng].append(lambda e, sm=sm, v=v: e.wait_ge(sm, v))
        return None

    def _mark(self, key, sem, val, reads, writes):
        for t in writes:
            t.w = {key: (sem, val)}
            t.r = {}
        for t in reads:
            old = t.r.get(key)
            if old is None or old[1] < val:
                t.r[key] = (sem, val)

    def op(self, eng, fn, reads=(), writes=(), inc=True):
        waits = self._deps(eng, reads, writes)
        last = self._emit_waits(eng, waits, True)
        sem = self.sem[eng]
        if inc:
            self.cnt[eng] += 1
            val = self.cnt[eng]
        else:
            val = self.cnt[eng] + 1

        def run(e, fn=fn, sem=sem, last=last, inc=inc):
            ins = fn(e)
            if last is not None:
                ins = ins._wait_ge(last[0], last[1])
            if inc:
                ins.then_inc(sem, 1)
        self.ops[eng].append(run)
        self._mark(eng, sem, val, reads, writes)

    def dma(self, q, out_ap, in_ap, reads=(), writes=(), dsem=None):
        waits = self._deps(q, reads, writes)
        last = self._emit_waits(q, waits, q == "sync")
        dsem.cnt += 16
        val = dsem.cnt
        s = dsem.sem

        def run(e, o=out_ap, i=in_ap, s=s, last=last):
            ins = e.dma_start(out=o, in_=i)
            if last is not None:
                ins = ins._wait_ge(last[0], last[1])
            ins.then_inc(s, 16)
        self.ops[q].append(run)
        self._mark(dsem.key, s, val, reads, writes)

    def emit(self):
        with self.nc.Block() as block:
            @block.tensor
            def _(e):
                for f in self.ops["tensor"]:
                    f(e)

            @block.vector
            def _(e):
                for f in self.ops["vector"]:
                    f(e)

            @block.scalar
            def _(e):
                for f in self.ops["scalar"]:
                    f(e)

            @block.gpsimd
            def _(e):
                for f in self.ops["gpsimd"]:
                    f(e)

            @block.sync
            def _(e):
                for f in self.ops["sync"]:
                    f(e)


def chunk_widths():
    w = []
    for _f in range(2):
        pass
    ffn = [2048] * NFF + [1408] * 16
    mix = [2048] * 6 + [2048] * 4 + [2048] * 2
    mrg = []
    mrg += [2048, 512, 2048, 512]
    for _dt in range(8):
        mrg += [512]
        if _dt + 2 < 8:
            mrg += [2048, 512]
    wo = [2048] * 4
    w = ffn + mix + mrg + wo + ffn
    return w


WIDTHS = chunk_widths()
NCH = len(WIDTHS)


def build_program(ntiles=NT_FULL, stop_after=99):
    nc = bass.Bass("TRN2", target_bir_lowering=False)
    x_d = nc.dram_tensor("x", [SEQ, D], F32, kind="ExternalInput").ap()
    w_d = nc.dram_tensor("wsrc", [NCH, 128, 2048], F32, kind="ExternalInput").ap()
    c_d = nc.dram_tensor("consts", [128, C_TOT], F32, kind="ExternalInput").ap()
    r_d = nc.dram_tensor("rope", [2, 128, SEQ], F32, kind="ExternalInput").ap()
    o_d = nc.dram_tensor("out", [SEQ, D], F32, kind="ExternalOutput").ap()
    scr_d = nc.dram_tensor("wscratch", [NCH, 128, 2048], BF16).ap()

    with ExitStack() as st:
        S = Sched(nc, st)

        def sb(name, shape, dt):
            return st.enter_context(nc.sbuf_tensor(name, shape, dt)).ap()

        pool_t = sb("pool", [128, 32, TT], BF16)
        pool = [Tl(pool_t[:, i, :], f"pool{i}") for i in range(32)]
        uT = pool[0:8]
        h1T = pool[8:30]
        cT = pool[8:12]
        QT = pool[12:16]
        oT = pool[16:20]
        mT = pool[20:28]
        gP = pool[28:32]
        hT_t = sb("hT", [128, 8, TT], F32)
        hT = [Tl(hT_t[:, i, :], f"hT{i}") for i in range(8)]
        KT_t = sb("KT", [128, 4, SEQ], BF16)
        KT = [[Tl(KT_t[:, h, t * TT:(t + 1) * TT], f"KT{h}_{t}") for t in range(NT_FULL)] for h in range(4)]
        V_t = sb("V", [128, 32, 4, 129], BF16)
        Vt = [Tl(V_t[:, 4 * t:4 * t + 4], f"V{t}") for t in range(NT_FULL)]
        ring = [Tl(sb(f"ring{i}", [128, 2048], BF16), f"ring{i}") for i in range(NSLOT)]
        xin = [Tl(sb(f"xin{i}", [128, D], F32), f"xin{i}") for i in range(2)]
        xn = Tl(sb("xn", [128, D], BF16), "xn")
        junk = Tl(sb("junk", [128, D], BF16), "junk")
        ostage = [Tl(sb(f"ost{i}", [128, D], F32), f"ost{i}") for i in range(2)]
        zbuf = [Tl(sb(f"zbuf{i}", [128, TT + 2], F32), f"zbuf{i}") for i in range(4)]
        tmpf = [Tl(sb(f"tmpf{i}", [128, TT], F32), f"tmpf{i}") for i in range(4)]
        tmpb = [Tl(sb(f"tmpb{i}", [128, TT], BF16), f"tmpb{i}") for i in range(2)]
        PT = [Tl(sb(f"PT{i}", [128, 2, TT], BF16), f"PT{i}") for i in range(3)]
        otok = [Tl(sb(f"otok{i}", [128, TT], BF16), f"otok{i}") for i in range(4)]
        o4 = Tl(sb("o4", [128, 4, 128], F32), "o4")
        ostgA = Tl(sb("ostgA", [128, 387], F32), "ostgA")
        ostgB = Tl(sb("ostgB", [128, 1, 387], F32), "ostgB")
        ostgC = Tl(sb("ostgC", [128, 258], F32), "ostgC")
        junk2 = Tl(sb("junk2", [128, 128], BF16), "junk2")
        ot2 = [Tl(sb(f"ot2_{i}", [128, 128], F32), f"ot2_{i}") for i in range(2)]
        dg4 = Tl(sb("dg4", [128, 4, 128], F32), "dg4")
        rsfin = Tl(sb("rsfin", [128, 8], F32), "rsfin")
        rsin = Tl(sb("rsin", [128, 8], F32), "rsin")
        ropeT = Tl(sb("ropeT", [128, 2, TT], F32), "ropeT")
        cst = Tl(sb("cst", [128, C_TOT], F32), "cst")
        identb = Tl(sb("identb", [128, 128], BF16), "identb")
        permb = Tl(sb("permb", [128, 128], BF16), "permb")
        ones_bf = Tl(sb("ones_bf", [128, 2], BF16), "ones_bf")
        ones_f = Tl(sb("ones_f", [128, 128], F32), "ones_f")
        gsub = Tl(sb("gsub", [128, 128], F32), "gsub")
        hb = Tl(sb("hb", [128, 16], F32), "hb")
        lam_t = Tl(sb("lam_t", [128, 8], F32), "lam_t")
        stat_t = sb("stat", [128, 64, 8], F32)
        stats = [Tl(stat_t[:, i, :], f"stat{i}") for i in range(64)]
        stat_i = [0]
        rq_a = Tl(sb("rq_a", [128, 8], F32), "rq_a")
        rq_i = Tl(sb("rq_i", [128, 8], F32), "rq_i")
        rq_y = Tl(sb("rq_y", [128, 8], F32), "rq_y")

        def new_stat():
            s_ = stats[stat_i[0] % 64]
            stat_i[0] += 1
            return s_

        ps_all = st.enter_context(nc.psum_tensor("ps", [128, 8, 512], F32)).ap()
        bank = [Tl(ps_all[:, i, :], f"bank{i}") for i in range(8)]
        for b_ in bank:
            b_.excl = True

        def bank_bf(i):
            return ps_all[:, i, :].bitcast(BF16)

        for t_ in ring + xin + ostage + [ropeT]:
            t_.dsem = S.new_dsem()
        for t_ in ring:
            t_.ssem = S.new_dsem()
        cs = S.new_dsem()
        scr = [Tl(scr_d[c], f"scr{c}") for c in range(NCH)]

        identf_ap = cst.ap[:, C_ID:C_ID + 128]

        def mm(out_ap, lhsT, rhs, start, stop, reads, writes, inc, **kw):
            S.op("tensor", lambda e: e.matmul(out_ap, lhsT, rhs, start=start, stop=stop, **kw),
                 reads=reads, writes=writes, inc=inc)

        def mm_group(out_tl, out_ap, pairs, common, per):
            n = len(pairs)
            for i, (l, r) in enumerate(pairs):
                mm(out_ap, l, r, i == 0, i == n - 1, list(common) + [per[i]], [out_tl], i == n - 1)

        def act(out_ap, in_ap, func, reads, writes, **kw):
            S.op("scalar", lambda e: e.activation(out=out_ap, in_=in_ap, func=func, **kw), reads=reads, writes=writes)

        def tt(eng, out_ap, in0, in1, op, reads, writes):
            S.op(eng, lambda e: e.tensor_tensor(out=out_ap, in0=in0, in1=in1, op=op), reads=reads, writes=writes)

        def ts(eng, out_ap, in0, s1, s2, op0, op1, reads, writes):
            if s2 is None:
                S.op(eng, lambda e: e.tensor_scalar(out=out_ap, in0=in0, scalar1=s1, scalar2=None, op0=op0),
                     reads=reads, writes=writes)
            else:
                S.op(eng, lambda e: e.tensor_scalar(out=out_ap, in0=in0, scalar1=s1, scalar2=s2, op0=op0, op1=op1),
                     reads=reads, writes=writes)

        def stt(out_ap, in0, scalar, in1, op0, op1, reads, writes):
            S.op("vector", lambda e: e.scalar_tensor_tensor(out=out_ap, in0=in0, scalar=scalar, in1=in1, op0=op0, op1=op1),
                 reads=reads, writes=writes)

        def rstd_from_ss(ss_ap, ss_tl, n, width, out=None):
            ts("vector", rq_a.ap[:, 0:n], ss_ap, 1.0 / width, EPS, ALU.mult, ALU.add, [ss_tl], [rq_a])
            ha = new_stat()
            ts("vector", ha.ap[:, 0:n], rq_a.ap[:, 0:n], -0.5, None, ALU.mult, None, [rq_a], [ha])
            S.op("vector", lambda e: e.tensor_scalar(out=rq_i.ap[:, 0:n].bitcast(I32), in0=rq_a.ap[:, 0:n].bitcast(I32),
                                                     scalar1=1, scalar2=None, op0=ALU.logical_shift_right),
                 reads=[rq_a], writes=[rq_i])
            y = rq_y
            S.op("vector", lambda e: e.tensor_scalar(out=rq_y.ap[:, 0:n].bitcast(I32), in0=rq_i.ap[:, 0:n].bitcast(I32),
                                                     scalar1=-1.0, scalar2=1597463007.0, op0=ALU.mult, op1=ALU.add),
                 reads=[rq_i], writes=[rq_y])
            y_ap = y.ap[:, 0:n]
            y_tl = [y]
            for it in range(3):
                u = new_stat()
                if it == 2 and out is not None:
                    y2_ap, y2_tl = out
                else:
                    y2 = new_stat()
                    y2_ap, y2_tl = y2.ap[:, 0:n], [y2]
                if n == 1:
                    stt(u.ap[:, 0:1], y_ap, ha.ap[:, 0:1], y_ap, ALU.mult, ALU.mult, y_tl + [ha], [u])
                else:
                    tt("vector", u.ap[:, 0:n], y_ap, y_ap, ALU.mult, y_tl, [u])
                    tt("vector", u.ap[:, 0:n], u.ap[:, 0:n], ha.ap[:, 0:n], ALU.mult, [u, ha], [u])
                stt(y2_ap, u.ap[:, 0:n], 1.5, y_ap, ALU.add, ALU.mult, [u] + y_tl, y2_tl)
                y_ap, y_tl = y2_ap, y2_tl
            return y_tl, y_ap

        total_chunks = NCH * ntiles
        ws = {"g": 0, "loaded": 0, "released": 0}

        def ws_load():
            g = ws["loaded"]
            if g >= total_chunks or g >= ws["released"] + NSLOT:
                return
            t_, c = divmod(g, NCH)
            slot = ring[g % NSLOT]
            wd = WIDTHS[c]
            ct = c % NCONV
            if t_ <= ct:
                S.dma("gpsimd", slot.ap[:, 0:wd], w_d[c][:, 0:wd], writes=[slot], dsem=slot.dsem)
                if t_ == ct and ntiles > ct + 1:
                    S.dma("sync", scr_d[c][:, 0:wd], slot.ap[:, 0:wd], reads=[slot], writes=[scr[c]], dsem=slot.ssem)
            else:
                S.dma("sync", slot.ap[:, 0:wd], scr_d[c][:, 0:wd], reads=[scr[c]], writes=[slot], dsem=slot.ssem)
            ws["loaded"] += 1

        def ws_get():
            g = ws["g"]
            assert ws["loaded"] > g
            ws["g"] += 1
            return ring[g % NSLOT]

        def ws_done(n=1):
            for _ in range(n):
                ws["released"] += 1
                assert ws["released"] <= ws["g"]
                ws_load()

        def x_load(r, q="sync"):
            slot = xin[r % 2]
            S.dma(q, slot.ap, x_d[r * 128:(r + 1) * 128, :], writes=[slot], dsem=slot.dsem)

        def xblk(t, tb):
            if t == 0 and tb >= 2:
                return ostage[tb % 2]
            return xin[(4 * t + tb) % 2]

        S.dma("sync", cst.ap, c_d, writes=[cst], dsem=cs)
        x_load(0, "sync")
        x_load(1, "sync")
        for tb_ in (2, 3):
            S.dma("sync", ostage[tb_ % 2].ap, x_d[tb_ * 128:(tb_ + 1) * 128, :], writes=[ostage[tb_ % 2]],
                  dsem=ostage[tb_ % 2].dsem)
        S.dma("sync", ropeT.ap, r_d.rearrange("a p t -> p a t")[:, :, 0:TT], writes=[ropeT], dsem=ropeT.dsem)
        PROLOGUE_XSTATS_MARK = None
        S.op("vector", lambda e: e.memset(ones_bf.ap, 1.0), writes=[ones_bf])
        S.op("vector", lambda e: e.memset(ones_f.ap, 1.0), writes=[ones_f])
        S.op("vector", lambda e: e.tensor_copy(out=identb.ap, in_=identf_ap), reads=[cst], writes=[identb])
        for blk in range(4):
            src = (blk ^ 1) * 32
            S.op("vector", lambda e, blk=blk, src=src: e.tensor_copy(
                out=permb.ap[:, blk * 32:(blk + 1) * 32], in_=identb.ap[:, src:src + 32]), reads=[identb], writes=[permb])
        for c in range(4):
            S.op("vector", lambda e, c=c: e.memset(zbuf[c].ap[:, 0:2], 0.0), writes=[zbuf[c]])
        for t in range(ntiles):
            S.op("gpsimd", lambda e, t=t: e.memset(Vt[t].ap[:, :, :, 128:129], 1.0), writes=[Vt[t]])
        ts("vector", gsub.ap, cst.ap[:, C_SUB:C_SUB + 128], 1.0 - LAMBDA_INIT, None, ALU.mult, None, [cst], [gsub])
        ts("vector", hb.ap, cst.ap[:, C_BG:C_BG + 16], 0.5, None, ALU.mult, None, [cst], [hb])
        lv = cst.ap[:, C_LAM:C_LAM + 256]
        for i in range(2):
            tt("vector", tmpf[0].ap[:, 0:64], lv[:, (2 * i) * 64:(2 * i + 1) * 64], lv[:, (2 * i + 1) * 64:(2 * i + 2) * 64],
               ALU.mult, [cst], [tmpf[0]])
            act(tmpf[1].ap[:, 0:64], tmpf[0].ap[:, 0:64], AF.Copy, [tmpf[0]], [tmpf[1], lam_t],
                accum_out=lam_t.ap[:, i:i + 1])
        act(lam_t.ap[:, 2:4], lam_t.ap[:, 0:2], AF.Exp, [lam_t], [lam_t])
        tt("vector", lam_t.ap[:, 4:5], lam_t.ap[:, 3:4], lam_t.ap[:, 2:3], ALU.subtract, [lam_t], [lam_t])
        ts("vector", lam_t.ap[:, 5:6], lam_t.ap[:, 4:5], -LAMBDA_INIT, None, ALU.add, None, [lam_t], [lam_t])
        neglam = lam_t.ap[:, 5:6]

        rsin_cols = [Tl(rsin.ap[:, i:i + 1], f"rsin{i}") for i in range(8)]

        ssx = Tl(sb("ssx", [128, 8], F32), "ssx")

        def x_stats_dma(t, tbs):
            for tb in tbs:
                r = 4 * t + tb
                slot = ostage[r % 2]
                S.dma("sync", slot.ap, x_d[r * 128:(r + 1) * 128, :], writes=[slot], dsem=slot.dsem)

        def x_stats_sq(t, tbs):
            for tb in tbs:
                r = 4 * t + tb
                slot = ostage[r % 2]
                act(junk.ap, slot.ap, AF.Square, [slot], [junk, ssx], accum_out=ssx.ap[:, tb:tb + 1])

        def x_stats_chain(t):
            base = (t % 2) * 4
            rstd_from_ss(ssx.ap[:, 0:4], ssx, 4, D, out=(rsin.ap[:, base:base + 4], rsin_cols[base:base + 4]))

        def x_stats_tile(t):
            x_stats_dma(t, [0, 1])
            x_stats_sq(t, [0, 1])
            x_stats_dma(t, [2, 3])
            x_stats_sq(t, [2, 3])
            x_stats_chain(t)

        def input_prep(t, tb):
            r = 4 * t + tb
            xs = xblk(t, tb)
            col = rsin_cols[(t % 2) * 4 + tb]
            stt(xn.ap, xs.ap, col.ap, cst.ap[:, C_G1B:C_G1B + D], ALU.mult, ALU.mult, [xs, col, cst], [xn])

        def input_pe(t, tb):
            r = 4 * t + tb
            xs = xblk(t, tb)
            pA, pB = (0, 1) if tb % 2 == 0 else (4, 5)
            qb = 2 if tb % 2 == 0 else 6
            for dc in range(8):
                bk = bank[pA] if dc < 4 else bank[pB]
                S.op("tensor", lambda e, dc=dc, bk=bk, xs=xs: e.transpose(
                    bk.ap[:, (dc % 4) * 128:(dc % 4 + 1) * 128], xs.ap[:, dc * 128:(dc + 1) * 128], identf_ap),
                    reads=[xs, cst], writes=[bk], inc=(dc % 4 == 3))
            for dc in range(8):
                S.op("tensor", lambda e, dc=dc, qb=qb: e.transpose(
                    bank_bf(qb)[:, dc * 128:(dc + 1) * 128], xn.ap[:, dc * 128:(dc + 1) * 128], identb.ap),
                    reads=[xn, identb], writes=[bank[qb]], inc=(dc == 7))
            act(hT_t[:, :, tb * 128:(tb + 1) * 128],
                ps_all[:, pA:pA + 2, :].rearrange("p k (a b) -> p (k a) b", a=4), AF.Copy,
                [bank[pA], bank[pB]], hT)
            act(pool_t[:, 0:8, tb * 128:(tb + 1) * 128], bank_bf(qb).rearrange("p (a b) -> p a b", a=8), AF.Copy,
                [bank[qb]], uT)
            if r + 2 < 4 * ntiles and r + 2 >= 4:
                x_load(r + 2)

        sbk = bank[6]
        rbk = bank[7]
        dummy = Tl(stat_t[:, 63, :], "dummy")

        def norm_sq(dt, sq=None, defer_last=False):
            sq = sq or uT
            act(sq[dt].ap, hT[dt].ap, AF.Square, [hT[dt]], [sq[dt]])
            if dt >= 1:
                norm_mm(dt - 1, sq)
            if dt == 7 and not defer_last:
                norm_mm(7, sq)

        sq_alt = [pool[30 + (d_ % 2)] for d_ in range(8)]

        def mix_norm_and_v(t):
            s0 = ws_get()
            s1 = ws_get()
            v0 = s0.ap.rearrange("p (k c) -> p k c", k=4)
            v1 = s1.ap.rearrange("p (k c) -> p k c", k=4)

            def vgroup(tb):
                bk = bank[tb]
                mm_group(bk, bk.ap, [(uT[kc].ap[:, tb * 128:(tb + 1) * 128], (v0 if kc < 4 else v1)[:, kc % 4, :])
                                     for kc in range(8)], [s0, s1], uT)

            vgroup(0)
            norm_mm(7, sq_alt)
            for tb in range(1, 4):
                vgroup(tb)
            ws_done(2)
            rs_tl, rs = rstd_from_ss(sbk.ap[:, 0:4], sbk, 4, D)
            for tb in range(4):
                bk = bank[tb]
                act(V_t[:, 4 * t + tb, :, 0:128], bk.ap.rearrange("p (h e) -> p h e", h=4), AF.Copy,
                    [bk] + rs_tl, [Vt[t]], scale=rs[:, tb:tb + 1])
            tt("vector", dg4.ap, identf_ap.unsqueeze(1).broadcast_to([128, 4, 128]),
               rs.unsqueeze(2).broadcast_to([128, 4, 128]), ALU.mult, [cst] + rs_tl, [dg4])
            for tb in range(4):
                mm(rbk.ap[:, tb * 128:(tb + 1) * 128], ones_f.ap, dg4.ap[:, tb, :], True, True, [ones_f, dg4], [rbk],
                   tb == 3)
            for dc in range(8):
                stt(uT[dc].ap, hT[dc].ap, cst.ap[:, C_GMIX + dc:C_GMIX + dc + 1], rbk.ap, ALU.mult, ALU.mult,
                    [hT[dc], cst, rbk], [uT[dc]])

        def norm_mm(dt, sq):
            for tb in range(4):
                mm(sbk.ap[:, tb:tb + 1], sq[dt].ap[:, tb * 128:(tb + 1) * 128], ones_bf.ap[:, 0:1],
                   dt == 0 and tb == 0, dt == 7, [sq[dt], ones_bf], [sbk], tb == 3, skip_group_check=True)

        Rsb = tmpf[0]

        def norm_finish_deferred():
            rs_tl, rs = rstd_from_ss(sbk.ap[:, 0:4], sbk, 4, D)
            tt("vector", dg4.ap, identf_ap.unsqueeze(1).broadcast_to([128, 4, 128]),
               rs.unsqueeze(2).broadcast_to([128, 4, 128]), ALU.mult, [cst] + rs_tl, [dg4])

        def r_matmuls():
            for tb in range(4):
                mm(rbk.ap[:, tb * 128:(tb + 1) * 128], ones_f.ap, dg4.ap[:, tb, :], True, True, [ones_f, dg4], [rbk],
                   tb == 3)
            act(Rsb.ap, rbk.ap, AF.Copy, [rbk], [Rsb])

        def norm_finish(gcol0):
            rs_tl, rs = rstd_from_ss(sbk.ap[:, 0:4], sbk, 4, D)
            tt("vector", dg4.ap, identf_ap.unsqueeze(1).broadcast_to([128, 4, 128]),
               rs.unsqueeze(2).broadcast_to([128, 4, 128]), ALU.mult, [cst] + rs_tl, [dg4])
            for tb in range(4):
                mm(rbk.ap[:, tb * 128:(tb + 1) * 128], ones_f.ap, dg4.ap[:, tb, :], True, True, [ones_f, dg4], [rbk],
                   tb == 3)
            for dc in range(8):
                stt(uT[dc].ap, hT[dc].ap, cst.ap[:, gcol0 + dc:gcol0 + dc + 1], rbk.ap, ALU.mult, ALU.mult,
                    [hT[dc], cst, rbk], [uT[dc]])

        usb = [gP[2], gP[3]]

        def ffn(norm_after, hooks=None, post_scale=False, mix_defer=False):
            hooks = hooks or {}
            act_tk = {}
            dve_tk = {}
            pending = []
            for j in range(NFF):
                if ("up", j) in hooks:
                    hooks[("up", j)]()
                if j == 2:
                    for ev in pending:
                        ev()
                    pending = []
                slot = ws_get()
                sv = slot.ap.rearrange("p (m k c) -> p m k c", m=2, k=8)
                if post_scale:
                    bg_, bu_ = (bank[0], bank[1]) if j % 2 == 0 else (bank[2], bank[3])
                else:
                    bg_, bu_ = bank[2 * (j % 4)], bank[2 * (j % 4) + 1]
                    if j % 2 == 0 and j >= 4:
                        S._wait("tensor", "scalar", S.sem["scalar"], act_tk[j - 3])
                        S._wait("tensor", "vector", S.sem["vector"], dve_tk[j - 3])
                mm_group(bg_, bg_.ap, [(sv[:, 0, kc, :], uT[kc].ap) for kc in range(8)], [slot], uT)
                mm_group(bu_, bu_.ap, [(sv[:, 1, kc, :], uT[kc].ap) for kc in range(8)], [slot], uT)
                ws_done()
                sl = tmpb[j % 2]
                if post_scale:
                    def evac(j=j, bg_=bg_, bu_=bu_, sl=sl):
                        gs = tmpf[1 + j % 2]
                        tt("vector", gs.ap, bg_.ap, Rsb.ap, ALU.mult, [bg_, Rsb], [gs])
                        act(sl.ap, gs.ap, AF.Silu, [gs], [sl])
                        us = usb[j % 2]
                        tt("vector", us.ap, bu_.ap, Rsb.ap, ALU.mult, [bu_, Rsb], [us])
                        tt("vector", h1T[j].ap, sl.ap, us.ap, ALU.mult, [sl, us], [h1T[j]])
                    if j < 2:
                        pending.append(evac)
                    else:
                        evac()
                else:
                    act(sl.ap, bg_.ap, AF.Silu, [bg_], [sl])
                    act_tk[j] = S.cnt["scalar"]
                    tt("vector", h1T[j].ap, sl.ap, bu_.ap, ALU.mult, [sl, bu_], [h1T[j]])
                    dve_tk[j] = S.cnt["vector"]
            for dt in range(8):
                if ("down", dt) in hooks:
                    hooks[("down", dt)]()
                bk = bank[4 + dt % 2]
                for half in range(2):
                    slot = ws_get()
                    sv = slot.ap[:, 0:1408].rearrange("p (j c) -> p j c", j=11)
                    for jj in range(11):
                        j = half * 11 + jj
                        mm(bk.ap, sv[:, jj, :], h1T[j].ap, j == 0, j == NFF - 1, [slot, h1T[j]], [bk],
                           jj == 10)
                    ws_done()
                stt(hT[dt].ap, bk.ap, 0.5, hT[dt].ap, ALU.mult, ALU.add, [bk, hT[dt]], [hT[dt]])
                if norm_after and mix_defer:
                    act(uT[dt].ap, hT[dt].ap, AF.Copy, [hT[dt], cst], [uT[dt]],
                        scale=cst.ap[:, C_GMIX + dt:C_GMIX + dt + 1])
                    norm_sq(dt, sq_alt, defer_last=True)
                elif norm_after:
                    norm_sq(dt)

        fm_state = {"slot": None, "m": 0}

        def fm_next():
            if fm_state["slot"] is None:
                fm_state["slot"] = ws_get()
                fm_state["m"] = 0
            slot = fm_state["slot"]
            m = fm_state["m"]
            v = slot.ap.rearrange("p (m k c) -> p m k c", m=2, k=8)[:, m]
            return slot, v

        def fm_adv():
            fm_state["m"] += 1
            if fm_state["m"] == 2:
                fm_state["slot"] = None
                ws_done()

        def fm_proj(bk):
            slot, v = fm_next()
            mm_group(bk, bk.ap, [(v[:, kc, :], uT[kc].ap) for kc in range(8)], [slot], uT)
            fm_adv()

        def mix_proj(t):
            cw = cst.ap[:, C_CW:C_CW + 12]
            for c in range(4):
                bA, bB, bC = (bank[0], bank[1], bank[2]) if c % 2 == 0 else (bank[4], bank[5], bank[6])
                fm_proj(bA)
                fm_proj(bB)
                fm_proj(bC)
                xcs = tmpf[0]
                act(xcs.ap, bA.ap, AF.Copy, [bA], [xcs])
                z = zbuf[c]
                if t > 0:
                    S.op("vector", lambda e, z=z: e.tensor_copy(out=z.ap[:, 0:2], in_=z.ap[:, TT:TT + 2]),
                         reads=[z], writes=[z])
                tt("vector", z.ap[:, 2:TT + 2], bB.ap, xcs.ap, ALU.mult, [bB, xcs], [z])
                acc = tmpf[1]
                ts("vector", acc.ap, z.ap[:, 0:TT], cw[:, 3 * c:3 * c + 1], None, ALU.mult, None, [z, cst], [acc])
                stt(acc.ap, z.ap[:, 1:TT + 1], cw[:, 3 * c + 1:3 * c + 2], acc.ap, ALU.mult, ALU.add, [z, cst, acc], [acc])
                stt(acc.ap, z.ap[:, 2:TT + 2], cw[:, 3 * c + 2:3 * c + 3], acc.ap, ALU.mult, ALU.add, [z, cst, acc], [acc])
                tt("vector", cT[c].ap, acc.ap, bC.ap, ALU.mult, [acc, bC], [cT[c]])
            items = [(which, h) for which in range(2) for h in range(4)]

            T1 = [tmpf[2], tmpf[1]]
            T2 = [tmpf[3], tmpf[0]]

            def rope_tail(i):
                which, h = items[i]
                bB = bank[2 * (i % 4) + 1]
                qb = tmpb[i % 2]
                mm(bB.ap, permb.ap, qb.ap, True, True, [permb, qb], [bB], True)
                t1 = T1[i % 2]
                t2 = T2[i % 2]
                tt("vector", t1.ap, bB.ap, ropeT.ap[:, 1, :], ALU.mult, [bB, ropeT], [t1])
                dst = QT[h] if which == 0 else KT[h][t]
                tt("gpsimd", dst.ap, t1.ap, t2.ap, ALU.add, [t1, t2], [dst])

            for i in range(8):
                bA = bank[2 * (i % 4)]
                fm_proj(bA)
                act(tmpb[i % 2].ap, bA.ap, AF.Copy, [bA], [tmpb[i % 2]])
                tt("vector", T2[i % 2].ap, bA.ap, ropeT.ap[:, 0, :], ALU.mult, [bA, ropeT], [T2[i % 2]])
                if i >= 1:
                    rope_tail(i - 1)
            rope_tail(7)
            if t + 1 < ntiles:
                S.dma("sync", ropeT.ap, r_d.rearrange("a p t -> p a t")[:, :, (t + 1) * TT:(t + 2) * TT],
                      writes=[ropeT], dsem=ropeT.dsem)

        att_state = {"t": 0}

        def attention(t):
            att_state["t"] = t
            if t + 1 < ntiles:
                x_stats_dma(t + 1, [0, 1])
            steps = [(h, kb) for h in range(4) for kb in range(4 * t + 4)]
            nkb = 4 * t + 4

            def acc_ap(s, j):
                a = s * 4 + j
                b = bank[4 + a // 3]
                off = (a % 3) * 129
                return b, b.ap[:, off:off + 129]

            def emit_S(idx):
                h, kb = steps[idx]
                jmin = max(0, kb - 4 * t)
                ncol = (4 - jmin) * 128
                tk, kl = divmod(kb, 4)
                pi = idx % 2
                kt = KT[h][tk]
                for s in range(2):
                    bk = bank[2 * pi + s]
                    mm(bk.ap[:, 0:ncol], kt.ap[64 * s:64 * s + 64, kl * 128:(kl + 1) * 128],
                       QT[h].ap[64 * s:64 * s + 64, jmin * 128:TT], True, True, [kt, QT[h]], [bk], s == 1)

            st_ = {"started": set()}

            def emit_exp(idx):
                h, kb = steps[idx]
                jmin = max(0, kb - 4 * t)
                ncol = (4 - jmin) * 128
                pi = idx % 2
                pt = PT[idx % 3]
                act(pt.ap[:, :, 0:ncol], ps_all[:, 2 * pi:2 * pi + 2, 0:ncol], AF.Exp,
                    [bank[2 * pi], bank[2 * pi + 1]], [pt], scale=0.125)
                if kb >= 4 * t:
                    S.op("gpsimd", lambda e, pt=pt: e.memset(pt.ap[64:128, :, 0:64], 0.0), writes=[pt])

            def emit_AV(idx):
                h, kb = steps[idx]
                if kb == 0:
                    st_["started"] = set()
                started = st_["started"]
                jmin = max(0, kb - 4 * t)
                tk, kl = divmod(kb, 4)
                pt = PT[idx % 3]
                vt = Vt[tk]
                for s in range(2):
                    for j in range(jmin, 4):
                        b, aap = acc_ap(s, j)
                        first = b.name not in started
                        started.add(b.name)
                        last_for_j = (kb == 4 * t + j)
                        mm(aap, pt.ap[:, s, (j - jmin) * 128:(j - jmin + 1) * 128], V_t[:, kb, h, :],
                           first, last_for_j, [pt, vt], [b], (s == 1 and j == 3), skip_group_check=True)
                if kb == nkb - 1:
                    finalize_head(h)

            emit_S(0)
            for idx in range(len(steps)):
                if idx + 1 < len(steps):
                    emit_S(idx + 1)
                emit_exp(idx)
                if idx >= 1:
                    emit_AV(idx - 1)
            emit_AV(len(steps) - 1)

        def otok_transposes():
            for h in range(4):
                for j in range(4):
                    S.op("tensor", lambda e, h=h, j=j: e.transpose(
                        bank_bf(1)[:, j * 128:(j + 1) * 128], otok[j].ap[:, h * 128:(h + 1) * 128], identb.ap),
                        reads=[otok[j], identb], writes=[bank[1]], inc=(j == 3))
                S.op("vector", lambda e, h=h: e.tensor_copy(out=oT[h].ap, in_=bank_bf(1)[:, 0:TT]),
                     reads=[bank[1]], writes=[oT[h]])

        def finalize_head(h):
            if att_state["t"] <= 1:
                act(ostgA.ap, bank[4].ap[:, 0:387], AF.Copy, [bank[4]], [ostgA])
                act(ostgB.ap[:, 0, :], bank[5].ap[:, 0:387], AF.Copy, [bank[5]], [ostgB])
                act(ostgC.ap[:, 0:258], bank[6].ap[:, 0:258], AF.Copy, [bank[6]], [ostgC])
            else:
                S.op("vector", lambda e: e.tensor_copy(out=ostgA.ap, in_=bank[4].ap[:, 0:387]),
                     reads=[bank[4]], writes=[ostgA])
                S.op("vector", lambda e: e.tensor_copy(out=ostgB.ap[:, 0, :], in_=bank[5].ap[:, 0:387]),
                     reads=[bank[5]], writes=[ostgB])
                S.op("vector", lambda e: e.tensor_copy(out=ostgC.ap[:, 0:258], in_=bank[6].ap[:, 0:258]),
                     reads=[bank[6]], writes=[ostgC])
            stg = [ostgA, ostgB, ostgC]
            sap = [ostgA.ap, ostgB.ap[:, 0, :], ostgC.ap]

            def accs(a):
                return stg[a // 3], sap[a // 3][:, (a % 3) * 129:(a % 3) * 129 + 128]

            ls = new_stat()
            for bi in range(3):
                n = 3 if bi < 2 else 2
                S.op("vector", lambda e, bi=bi, n=n, ls=ls: e.tensor_copy(
                    out=ls.ap[:, 3 * bi:3 * bi + n],
                    in_=sap[bi][:, 0:129 * n].rearrange("p (a c) -> p a c", a=n)[:, :, 128]),
                    reads=[stg[bi]], writes=[ls])
            rl = new_stat()
            S.op("vector", lambda e: e.reciprocal(out=rl.ap, in_=ls.ap), reads=[ls], writes=[rl])
            rn = new_stat()
            ts("vector", rn.ap[:, 0:4], rl.ap[:, 4:8], neglam, None, ALU.mult, None, [rl, lam_t], [rn])
            sso = new_stat()
            for j in range(4):
                t0_, a0 = accs(j)
                t1_, a1 = accs(4 + j)
                tq = ot2[j % 2]
                ts("vector", tq.ap, a1, rn.ap[:, j:j + 1], None, ALU.mult, None, [t1_, rn], [tq])
                stt(o4.ap[:, j, :], a0, rl.ap[:, j:j + 1], tq.ap, ALU.mult, ALU.add, [t0_, rl, tq], [o4])
            for j in range(4):
                S.op("vector", lambda e, j=j, sso=sso: e.scalar_tensor_tensor(
                    out=junk2.ap, in0=o4.ap[:, j, :], scalar=1.0, in1=o4.ap[:, j, :], op0=ALU.mult, op1=ALU.mult,
                    accum_out=sso.ap[:, j:j + 1]), reads=[o4], writes=[junk2, sso])
            rs_tl, rs = rstd_from_ss(sso.ap[:, 0:4], sso, 4, 128)
            for j in range(4):
                stt(otok[j].ap[:, h * 128:(h + 1) * 128], o4.ap[:, j, :], rs[:, j:j + 1], gsub.ap, ALU.mult, ALU.mult,
                    [o4, gsub] + rs_tl, [otok[j]])

        def gates_yc(dt):
            bgc, bga = (bank[4], bank[5]) if dt % 2 == 0 else (bank[6], bank[7])
            byc = bank[0] if dt % 2 == 0 else bank[2]
            fm_proj(bgc)
            fm_proj(bga)
            tc_ = gP[2 * (dt % 2)]
            ta_ = gP[2 * (dt % 2) + 1]
            act(tc_.ap, bgc.ap, AF.Tanh, [bgc, hb], [tc_], scale=0.5, bias=hb.ap[:, dt:dt + 1])
            act(ta_.ap, bga.ap, AF.Tanh, [bga, hb], [ta_], scale=0.5, bias=hb.ap[:, 8 + dt:9 + dt])
            c_ = ws_get()
            cv = c_.ap[:, 0:512].rearrange("p (k c) -> p k c", k=4)
            mm_group(byc, byc.ap, [(cv[:, kc, :], cT[kc].ap) for kc in range(4)], [c_], cT)
            ws_done()

        def merge_wo():
            gates_yc(0)
            gates_yc(1)
            otok_transposes()
            for dt in range(8):
                byc, bya = (bank[0], bank[1]) if dt % 2 == 0 else (bank[2], bank[3])
                a_sl = ws_get()
                av = a_sl.ap[:, 0:512].rearrange("p (k c) -> p k c", k=4)
                mm_group(bya, bya.ap, [(av[:, kc, :], oT[kc].ap) for kc in range(4)], [a_sl], oT)
                ws_done()
                tc_ = gP[2 * (dt % 2)]
                ta_ = gP[2 * (dt % 2) + 1]
                a_ = tmpf[0]
                b_ = tmpf[1]
                stt(a_.ap, tc_.ap, 1.0, byc.ap, ALU.add, ALU.mult, [tc_, byc], [a_])
                stt(b_.ap, ta_.ap, 1.0, bya.ap, ALU.add, ALU.mult, [ta_, bya], [b_])
                tt("gpsimd", mT[dt].ap, a_.ap, b_.ap, ALU.add, [a_, b_], [mT[dt]])
                if dt + 2 < 8:
                    gates_yc(dt + 2)
            for i in range(4):
                slot = ws_get()
                sv = slot.ap.rearrange("p (m k c) -> p m k c", m=2, k=8)
                for m in range(2):
                    dt2 = 2 * i + m
                    bk = bank[dt2 % 4]
                    mm_group(bk, bk.ap, [(sv[:, m, dt, :], mT[dt].ap) for dt in range(8)], [slot], mT)
                    stt(hT[dt2].ap, bk.ap, 0.5, hT[dt2].ap, ALU.mult, ALU.add, [bk, hT[dt2]], [hT[dt2]])
                    norm_sq(dt2, pool[8:16])
                    act(uT[dt2].ap, hT[dt2].ap, AF.Copy, [hT[dt2], cst], [uT[dt2]],
                        scale=cst.ap[:, C_G2 + dt2:C_G2 + dt2 + 1])
                ws_done()

        def final_stats():
            rstd_from_ss(sbk.ap[:, 0:4], sbk, 4, D, out=(rsfin.ap[:, 0:4], [rsfin]))

        fin_ps = ps_all[:, 3:8:4, :]

        def final_tb(t, tb):
            gfb = cst.ap[:, C_GFB:C_GFB + D].rearrange("p (k n) -> p k n", k=2)
            r = 4 * t + tb
            if t == ntiles - 1:
                bA, bB = bank[2 * tb], bank[2 * tb + 1]
                fps = ps_all[:, 2 * tb:2 * tb + 2, :]
            else:
                bA, bB = bank[3], bank[7]
                fps = fin_ps
            for dc in range(8):
                bk = bA if dc < 4 else bB
                S.op("tensor", lambda e, dc=dc, bk=bk, tb=tb: e.transpose(
                    bk.ap[:, (dc % 4) * 128:(dc % 4 + 1) * 128], hT[dc].ap[:, tb * 128:(tb + 1) * 128], identf_ap),
                    reads=[hT[dc], cst], writes=[bk], inc=(dc % 4 == 3))
            os_ = ostage[r % 2]
            stt(os_.ap.rearrange("p (k n) -> p k n", k=2), fps, rsfin.ap[:, tb:tb + 1], gfb, ALU.mult, ALU.mult,
                [bA, bB, rsfin, cst], [os_])
            S.dma("sync", o_d[r * 128:(r + 1) * 128, :], os_.ap, reads=[os_], dsem=os_.dsem)

        def ffn2_with_prefetch(t):
            n_ = t + 1
            if n_ >= ntiles:
                ffn(True, {("up", 2): r_matmuls}, True)
                return
            ffn(True, {
                ("up", 2): r_matmuls,
                ("up", 0): lambda: (x_stats_sq(n_, [0, 1]), x_stats_dma(n_, [2, 3])),
                ("up", 11): lambda: x_stats_sq(n_, [2, 3]),
                ("down", 0): lambda: x_stats_chain(n_),
                ("down", 2): lambda: input_prep(n_, 0),
            }, True)

        stages = [lambda t: ffn(True, None, False, True), lambda t: mix_norm_and_v(t), lambda t: mix_proj(t),
                  lambda t: attention(t), lambda t: merge_wo(), lambda t: norm_finish_deferred(),
                  lambda t: ffn2_with_prefetch(t), lambda t: final_stats()]
        for xs_ in xin + ostage:
            S._wait("gpsimd", xs_.dsem.key, xs_.dsem.sem, 16)
        for _ in range(NSLOT):
            ws_load()

        def input_stats_direct(tb):
            xs = xblk(0, tb)
            ss = new_stat()
            act(junk.ap, xs.ap, AF.Square, [xs], [junk, ss], accum_out=ss.ap[:, 0:1])
            rstd_from_ss(ss.ap[:, 0:1], ss, 1, D, out=(rsin.ap[:, tb:tb + 1], [rsin_cols[tb]]))

        for tb in range(4):
            input_stats_direct(tb)
            input_prep(0, tb)
            input_pe(0, tb)
        for t in range(ntiles):
            for si, stg_ in enumerate(stages):
                if si < stop_after:
                    stg_(t)
            for tb in range(4):
                if t + 1 < ntiles and tb >= 1:
                    input_prep(t + 1, tb)
                final_tb(t, tb)
                if t + 1 < ntiles:
                    input_pe(t + 1, tb)

        for os_ in ostage:
            S._wait("sync", os_.dsem.key, os_.dsem.sem, os_.dsem.cnt)
        assert stop_after < 99 or ws["g"] == total_chunks, (ws["g"], total_chunks)
        S.emit()
    return nc


def _pad(a):
    out = np.zeros((128, 2048), np.float32)
    out[:, :a.shape[1]] = a
    return out


def _fm_chunk(A, B):
    k = A.shape[0] // 128
    s = np.stack([A.reshape(k, 128, 128), B.reshape(k, 128, 128)])
    return np.ascontiguousarray(s.transpose(2, 0, 1, 3)).reshape(128, 2 * k * 128)


def _ffn_chunks(wg, wu, wd):
    ch = []
    for j in range(NFF):
        ch.append(_fm_chunk(wg[:, j * 128:(j + 1) * 128], wu[:, j * 128:(j + 1) * 128]))
    for dt in range(8):
        for half in range(2):
            blk = wd[half * 11 * 128:(half + 1) * 11 * 128, dt * 128:(dt + 1) * 128]
            a = np.ascontiguousarray(blk.reshape(11, 128, 128).transpose(1, 0, 2)).reshape(128, 1408)
            ch.append(_pad(a))
    return ch


def _swap_perm():
    p = []
    for s in range(2):
        p += list(range(s * 64 + 32, s * 64 + 64)) + list(range(s * 64, s * 64 + 32))
    return np.array(p)


def build_wstream(inp):
    w_in = inp["w_in"][0]
    ch = _ffn_chunks(inp["ffn1_gate"][0], inp["ffn1_up"][0], inp["ffn1_down"][0])
    wv = w_in[:, 2560:3072]
    for kq in range(2):
        a = wv[kq * 512:(kq + 1) * 512, :].reshape(4, 128, 512).transpose(1, 0, 2)
        ch.append(np.ascontiguousarray(a).reshape(128, 2048))
    tiles = []
    for c in range(4):
        tiles.append(w_in[:, 0 + c * 128:0 + (c + 1) * 128])
        tiles.append(w_in[:, 1024 + c * 128:1024 + (c + 1) * 128])
        tiles.append(w_in[:, 512 + c * 128:512 + (c + 1) * 128])
    for i in range(0, 12, 2):
        ch.append(_fm_chunk(tiles[i], tiles[i + 1]))
    for base in (1536, 2048):
        for h in range(0, 4, 2):
            ch.append(_fm_chunk(w_in[:, base + h * 128:base + (h + 1) * 128],
                                w_in[:, base + (h + 1) * 128:base + (h + 2) * 128]))
    wco = inp["w_conv_out"][0].reshape(4, 128, 8, 128)
    wao = inp["w_attn_out"][0].reshape(4, 128, 8, 128)
    def _g(dt):
        return _fm_chunk(w_in[:, 3072 + dt * 128:3072 + (dt + 1) * 128], w_in[:, 4096 + dt * 128:4096 + (dt + 1) * 128])

    def _ca(wmat, dt):
        return _pad(np.ascontiguousarray(wmat[:, :, dt, :].transpose(1, 0, 2)).reshape(128, 512))

    ch += [_g(0), _ca(wco, 0), _g(1), _ca(wco, 1)]
    for dt in range(8):
        ch.append(_ca(wao, dt))
        if dt + 2 < 8:
            ch += [_g(dt + 2), _ca(wco, dt + 2)]
    wo = inp["w_o"][0]
    for i in range(4):
        ch.append(_fm_chunk(wo[:, (2 * i) * 128:(2 * i + 1) * 128], wo[:, (2 * i + 1) * 128:(2 * i + 2) * 128]))
    ch += _ffn_chunks(inp["ffn2_gate"][0], inp["ffn2_up"][0], inp["ffn2_down"][0])
    assert len(ch) == NCH, (len(ch), NCH)
    return np.stack(ch).astype(np.float32)


def build_consts(inp):
    c = np.zeros((128, C_TOT), np.float32)
    c[:, C_GMIX:C_GMIX + 8] = inp["norm_mix"][0].reshape(8, 128).T
    c[:, C_G2:C_G2 + 8] = inp["norm_ffn2"][0].reshape(8, 128).T
    c[:, C_BG:C_BG + 16] = inp["b_gate"][0].reshape(16, 128).T
    cw = inp["conv_w"][0]
    c[:, C_CW:C_CW + 12] = cw.reshape(3, 4, 128).transpose(2, 1, 0).reshape(128, 12)
    lam = np.concatenate([inp["lambda_q1"][0], inp["lambda_k1"][0], inp["lambda_q2"][0], inp["lambda_k2"][0]])
    c[:, C_LAM:C_LAM + 256] = np.broadcast_to(lam[None, :], (128, 256))
    c[:, C_SUB:C_SUB + 128] = np.broadcast_to(inp["subln_g"][0][None, :], (128, 128))
    c[:, C_ID:C_ID + 128] = np.eye(128, dtype=np.float32)
    c[:, C_G1B:C_G1B + D] = np.broadcast_to(inp["norm_ffn1"][0][None, :], (128, D))
    c[:, C_GFB:C_GFB + D] = np.broadcast_to(inp["norm_final"][None, :], (128, D))
    return c


def build_rope():
    inv_freq = (1.0 / (np.float32(10000.0) ** (np.arange(0, 64, 2, dtype=np.float32) / np.float32(64)))).astype(np.float32)
    ang = (np.arange(SEQ, dtype=np.float32)[:, None] * inv_freq[None, :]).astype(np.float32)
    cos = np.cos(ang).astype(np.float32).T
    sin = np.sin(ang).astype(np.float32).T
    tab = np.zeros((2, 128, SEQ), np.float32)
    for p in range(128):
        d = p % 64
        tab[0, p] = cos[d % 32]
        tab[1, p] = -sin[d % 32] if d < 32 else sin[d % 32]
    return tab


_CACHE = {}


def kernel(**inputs):
    inp = {k: np.asarray(v) for k, v in inputs.items()}
    x = inp["x"]
    n = x.shape[0]
    wsrc = build_wstream(inp)
    consts = build_consts(inp)
    rope = build_rope()
    if "nc" not in _CACHE:
        _CACHE["nc"] = build_program(NT_FULL)
    nc = _CACHE["nc"]
    in_maps = [{"x": np.ascontiguousarray(x[i]), "wsrc": wsrc, "consts": consts, "rope": rope} for i in range(n)]
    res = run_bass_kernel_spmd(nc, in_maps, core_ids=list(range(n)))
    return np.stack([np.asarray(r["out"]) for r in res.results]).astype(np.float32)
```
